# Optimizing a Trainium2 kernel written in Bass

```python
import jax, jax.numpy as jnp
from jax import lax
import numpy as np

D_MODEL = 2048
BATCH = 2
SEQ = 8192
DEPTH = 1

MLA_HEADS = 8
QK_NOPE_DIM = 128
QK_ROPE_DIM = 64
V_HEAD_DIM = 128
Q_LORA_RANK = 512
KV_LORA_RANK = 512
MLA_WIDTH = MLA_HEADS * V_HEAD_DIM
QK_HEAD_DIM = QK_NOPE_DIM + QK_ROPE_DIM
CONV_CHANNELS = D_MODEL - MLA_WIDTH
CONV_WIDTH = 31
CONV_PAD = CONV_WIDTH // 2
D_FF = 4 * D_MODEL
ROPE_BASE = 10000.0
Q_BLOCK = 128
LN_EPS = 1e-5
RMS_EPS = 1e-6
DEEPNORM_ALPHA = (2.0 * DEPTH) ** 0.25
DEEPNORM_BETA = (8.0 * DEPTH) ** -0.25
IN_COLS = Q_LORA_RANK + KV_LORA_RANK + QK_ROPE_DIM + 2 * CONV_CHANNELS

kernel_name = "hybrid_mla_conformer_deepnorm_encoder"


def layer_norm(x, g, b):
    xf = x.astype(jnp.float32)
    mu = jnp.mean(xf, axis=-1, keepdims=True)
    xc = xf - mu
    var = jnp.mean(jnp.square(xc), axis=-1, keepdims=True)
    y = xc * lax.rsqrt(var + LN_EPS)
    return (y * g.astype(jnp.float32) + b.astype(jnp.float32)).astype(x.dtype)


def rms_norm(x, g):
    xf = x.astype(jnp.float32)
    y = xf * lax.rsqrt(jnp.mean(jnp.square(xf), axis=-1, keepdims=True) + RMS_EPS)
    return (y * g.astype(jnp.float32)).astype(x.dtype)


def rope_tables(positions, dtype):
    half = QK_ROPE_DIM // 2
    inv_freq = ROPE_BASE ** (-jnp.arange(half, dtype=jnp.float32) * (2.0 / QK_ROPE_DIM))
    ang = positions.astype(jnp.float32)[..., None] * inv_freq
    return jnp.cos(ang).astype(dtype), jnp.sin(ang).astype(dtype)


def apply_rope(x, cos, sin):
    x1, x2 = jnp.split(x, 2, axis=-1)
    return jnp.concatenate([x1 * cos - x2 * sin, x2 * cos + x1 * sin], axis=-1)


def mla_attention(q_nope, q_rope, k_nope, k_rope, v):
    b, s, h, _ = q_nope.shape
    nb = s // Q_BLOCK
    scale = QK_HEAD_DIM ** -0.5
    qn = q_nope.reshape(b, nb, Q_BLOCK, h, QK_NOPE_DIM).transpose(1, 0, 2, 3, 4)
    qr = q_rope.reshape(b, nb, Q_BLOCK, h, QK_ROPE_DIM).transpose(1, 0, 2, 3, 4)

    def block(args):
        qn_b, qr_b = args
        scores = (jnp.einsum('bqhd,bkhd->bhqk', qn_b, k_nope)
                  + jnp.einsum('bqhr,bkr->bhqk', qr_b, k_rope))
        p = jax.nn.softmax(scores.astype(jnp.float32) * scale, axis=-1).astype(v.dtype)
        return jnp.einsum('bhqk,bkhd->bqhd', p, v)

    out = lax.map(block, (qn, qr))
    return out.transpose(1, 0, 2, 3, 4).reshape(b, s, h * V_HEAD_DIM)


def conformer_conv(u_in, conv_w, conv_b, g_ln, b_ln):
    a, gate = jnp.split(u_in, 2, axis=-1)
    u = a * jax.nn.sigmoid(gate)
    kern = conv_w.reshape(CONV_WIDTH, 1, CONV_CHANNELS).astype(u.dtype)
    u = lax.conv_general_dilated(
        u, kern, window_strides=(1,), padding=[(CONV_PAD, CONV_PAD)],
        dimension_numbers=('NWC', 'WIO', 'NWC'),
        feature_group_count=CONV_CHANNELS) + conv_b
    return jax.nn.silu(layer_norm(u, g_ln, b_ln))


def hybrid_layer(x, cos, sin, w_in, g_cq, w_uq, g_ckv, w_uk, w_uv, conv_w, conv_b,
                 g_conv_ln, b_conv_ln, w_out, g_ln1, b_ln1, w_ff1, w_ff2, g_ln2, b_ln2):
    b, s, _ = x.shape
    h = x @ w_in
    c_q, c_kv, k_rope, conv_in = jnp.split(
        h, [Q_LORA_RANK, Q_LORA_RANK + KV_LORA_RANK,
            Q_LORA_RANK + KV_LORA_RANK + QK_ROPE_DIM], axis=-1)
    q = (rms_norm(c_q, g_cq) @ w_uq).reshape(b, s, MLA_HEADS, QK_HEAD_DIM)
    q_nope, q_rope = jnp.split(q, [QK_NOPE_DIM], axis=-1)
    q_rope = apply_rope(q_rope, cos[:, :, None, :], sin[:, :, None, :])
    k_rope = apply_rope(k_rope, cos, sin)
    ckv = rms_norm(c_kv, g_ckv)
    k_nope = (ckv @ w_uk).reshape(b, s, MLA_HEADS, QK_NOPE_DIM)
    v = (ckv @ w_uv).reshape(b, s, MLA_HEADS, V_HEAD_DIM)
    attn_out = mla_attention(q_nope, q_rope, k_nope, k_rope, v)
    conv_out = conformer_conv(conv_in, conv_w, conv_b, g_conv_ln, b_conv_ln)
    mix = jnp.concatenate([attn_out, conv_out], axis=-1) @ w_out
    x = layer_norm(DEEPNORM_ALPHA * x + mix, g_ln1, b_ln1)
    ff = jnp.square(jax.nn.relu(x @ w_ff1)) @ w_ff2
    return layer_norm(DEEPNORM_ALPHA * x + ff, g_ln2, b_ln2)


def setup_inputs(seed: int = 0) -> dict:
    key = jax.random.key(seed)
    ks = jax.random.split(key, 24)
    f32 = jnp.float32

    def nrm(k, shape, scale):
        return jax.random.normal(k, shape, f32) * scale

    def gain(k, shape):
        return 1.0 + 0.02 * jax.random.normal(k, shape, f32)

    L = DEPTH
    beta = DEEPNORM_BETA
    return {
        "x": jax.random.normal(ks[0], (BATCH, SEQ, D_MODEL), f32),
        "positions": jnp.broadcast_to(jnp.arange(SEQ, dtype=jnp.int32), (BATCH, SEQ)),
        "ln_in_g": gain(ks[1], (D_MODEL,)),
        "ln_in_b": nrm(ks[2], (D_MODEL,), 0.02),
        "w_in": nrm(ks[3], (L, D_MODEL, IN_COLS), D_MODEL ** -0.5),
        "g_cq": gain(ks[4], (L, Q_LORA_RANK)),
        "w_uq": nrm(ks[5], (L, Q_LORA_RANK, MLA_HEADS * QK_HEAD_DIM), Q_LORA_RANK ** -0.5),
        "g_ckv": gain(ks[6], (L, KV_LORA_RANK)),
        "w_uk": nrm(ks[7], (L, KV_LORA_RANK, MLA_HEADS * QK_NOPE_DIM), KV_LORA_RANK ** -0.5),
        "w_uv": nrm(ks[8], (L, KV_LORA_RANK, MLA_HEADS * V_HEAD_DIM), beta * KV_LORA_RANK ** -0.5),
        "conv_w": nrm(ks[9], (L, CONV_WIDTH, CONV_CHANNELS), CONV_WIDTH ** -0.5),
        "conv_b": nrm(ks[10], (L, CONV_CHANNELS), 0.02),
        "g_conv_ln": gain(ks[11], (L, CONV_CHANNELS)),
        "b_conv_ln": nrm(ks[12], (L, CONV_CHANNELS), 0.02),
        "w_out": nrm(ks[13], (L, D_MODEL, D_MODEL), beta * D_MODEL ** -0.5),
        "g_ln1": gain(ks[14], (L, D_MODEL)),
        "b_ln1": nrm(ks[15], (L, D_MODEL), 0.02),
        "w_ff1": nrm(ks[16], (L, D_MODEL, D_FF), beta * D_MODEL ** -0.5),
        "w_ff2": nrm(ks[17], (L, D_FF, D_MODEL), beta * D_FF ** -0.5),
        "g_ln2": gain(ks[18], (L, D_MODEL)),
        "b_ln2": nrm(ks[19], (L, D_MODEL), 0.02),
    }


def reference(x, positions, ln_in_g, ln_in_b, w_in, g_cq, w_uq, g_ckv, w_uk, w_uv,
              conv_w, conv_b, g_conv_ln, b_conv_ln, w_out, g_ln1, b_ln1,
              w_ff1, w_ff2, g_ln2, b_ln2):
    cos, sin = rope_tables(positions, x.dtype)
    x = layer_norm(x, ln_in_g, ln_in_b)
    for l in range(DEPTH):
        x = hybrid_layer(x, cos, sin, w_in[l], g_cq[l], w_uq[l], g_ckv[l], w_uk[l], w_uv[l],
                         conv_w[l], conv_b[l], g_conv_ln[l], b_conv_ln[l], w_out[l],
                         g_ln1[l], b_ln1[l], w_ff1[l], w_ff2[l], g_ln2[l], b_ln2[l])
    return x
```

```python
import math
import contextlib
import numpy as np
import concourse.bass as bass
import concourse.mybir as mybir
from concourse.bass_utils import run_bass_kernel_spmd

F32 = mybir.dt.float32
BF16 = mybir.dt.bfloat16
I32 = mybir.dt.int32
AF = mybir.ActivationFunctionType
ALU = mybir.AluOpType

NCORES = 8
D = 2048
SEQ = 8192
NTOK = 2048
ALPHA = 2.0 ** 0.25
SCALE = 192 ** -0.5
LN_EPS = 1e-5
RMS_EPS = 1e-6
ARENA_BYTES = 204 * 1024

DEBUG = {}


class _Buf:
    __slots__ = ("w", "r")

    def __init__(self):
        self.w = None
        self.r = []


class _Op:
    __slots__ = ("eng", "fn", "deps", "dma", "tok", "signal", "idx", "cnt")


class Sched:
    ENGS = ("pe", "act", "dve", "pool", "sp")
    NDMA = 28
    NSP = 20

    def __init__(self):
        self.ops = {e: [] for e in self.ENGS}
        self.bufs = {}
        self.ndma = {"sp": 0, "pool": 0, "act": 0}
        self.dma_cnt = [0] * self.NDMA
        self.dma_last = [None] * self.NDMA
        self.pending_barrier = {e: None for e in self.ENGS}
        self.enabled = True

    def buf(self, key):
        b = self.bufs.get(key)
        if b is None:
            b = self.bufs[key] = _Buf()
        return b

    def barrier(self):
        if not self.enabled:
            return
        toks = []
        for e in self.ENGS:
            for op in reversed(self.ops[e]):
                if not op.dma:
                    toks.append(op.tok)
                    break
        for t in self.dma_last:
            if t is not None:
                toks.append(t)
        for e in self.ENGS:
            prev = self.pending_barrier[e]
            self.pending_barrier[e] = toks if prev is None else prev + toks

    def add(self, eng, fn, r=(), w=(), dma=False):
        if not self.enabled:
            return None
        op = _Op()
        op.eng = eng
        op.fn = fn
        op.dma = dma
        op.signal = False
        op.cnt = 0
        op.idx = len(self.ops[eng])
        deps = []
        if self.pending_barrier[eng] is not None:
            deps.extend(self.pending_barrier[eng])
            self.pending_barrier[eng] = None
        for k in r:
            b = self.buf(k)
            if b.w is not None:
                deps.append(b.w)
        for k in w:
            b = self.buf(k)
            if b.w is not None:
                deps.append(b.w)
            deps.extend(b.r)
        if dma:
            if eng == "sp":
                j = self.ndma["sp"] % self.NSP
            else:
                j = self.NSP + self.ndma[eng] % (self.NDMA - self.NSP)
            self.ndma[eng] += 1
            if self.dma_last[j] is not None:
                deps.append(self.dma_last[j])
            self.dma_cnt[j] += 1
            op.tok = ("d%d" % j, self.dma_cnt[j] * 16, op)
            self.dma_last[j] = op.tok
        else:
            op.tok = (eng, op.idx, op)
        seen = {}
        for t in deps:
            if t[2] is op:
                continue
            if (not dma) and eng == "pe" and t[0] == "pe":
                continue
            key = t[0]
            if key not in seen or seen[key][1] < t[1]:
                seen[key] = t
        op.deps = list(seen.values())
        for t in op.deps:
            t[2].signal = True
        for k in r:
            self.buf(k).r.append(op.tok)
        for k in w:
            b = self.buf(k)
            b.w = op.tok
            b.r = []
        self.ops[eng].append(op)
        return op

    def emit(self, nc, stack, final_keys=()):
        sems = {}
        for e in ("pe", "act", "dve", "pool"):
            sems[e] = stack.enter_context(nc.semaphore("s_" + e))
        for j in range(self.NDMA):
            sems["d%d" % j] = stack.enter_context(nc.semaphore("s_d%d" % j))
        final = [self.buf(k).w for k in final_keys]
        for t in final:
            t[2].signal = True
        for e in self.ENGS:
            c = 0
            for op in self.ops[e]:
                if op.dma:
                    continue
                if op.signal:
                    c += 1
                op.cnt = c

        def tokval(t):
            o = t[2]
            return t[1] if o.dma else o.cnt

        block = stack.enter_context(nc.Block())
        engobj = {"pe": "tensor", "act": "scalar", "dve": "vector", "pool": "gpsimd", "sp": "sync"}
        for e in self.ENGS:
            ops = self.ops[e]
            fw = final if e == "sp" else ()

            def body(eng, ops=ops, e=e, fw=fw):
                waited = {}
                for op in ops:
                    for t in op.deps:
                        v = tokval(t)
                        s = t[0]
                        if waited.get(s, 0) >= v:
                            continue
                        waited[s] = v
                        eng.wait_ge(sems[s], v)
                    inst = op.fn(eng)
                    if op.dma:
                        inst.then_inc(sems[op.tok[0]], 16)
                    elif op.signal:
                        inst.then_inc(sems[e], 1)
                for t in fw:
                    v = tokval(t)
                    s = t[0]
                    if waited.get(s, 0) >= v:
                        continue
                    waited[s] = v
                    eng.wait_ge(sems[s], v)

            getattr(block, engobj[e])(body)


def build_program(stop_after=None):
    nc = bass.Bass("TRN2", target_bir_lowering=False)
    S = Sched()
    A = S.add

    def din(name, shape, dt=F32):
        return nc.dram_tensor(name, shape, dt, kind="ExternalInput").ap()

    x = din("x", [SEQ, D])
    pos = din("pos", [1, SEQ], I32)
    hmask_d = din("hmask", [128, 2])
    ident_d = din("ident", [128, 128])
    rc_d = din("ropec", [128, 2])
    ln_in_g = din("ln_in_g", [1, D])
    ln_in_b = din("ln_in_b", [1, D])
    w_in = din("w_in", [D, 3136])
    g_cq = din("g_cq", [1, 512])
    w_uq = din("w_uq", [512, 1536])
    g_ckv = din("g_ckv", [1, 512])
    w_uk = din("w_uk", [512, 1024])
    w_uv = din("w_uv", [512, 1024])
    conv_w = din("conv_w", [31, 1024])
    conv_b = din("conv_b", [1, 1024])
    g_cln = din("g_conv_ln", [1, 1024])
    b_cln = din("b_conv_ln", [1, 1024])
    w_out = din("w_out", [D, D])
    g_ln1 = din("g_ln1", [1, D])
    b_ln1 = din("b_ln1", [1, D])
    w_ff1 = din("w_ff1", [D, 8192])
    w_ff2 = din("w_ff2", [8192, D])
    g_ln2 = din("g_ln2", [1, D])
    b_ln2 = din("b_ln2", [1, D])
    out = nc.dram_tensor("out", [NTOK, D], F32, kind="ExternalOutput").ap()
    catT_d = nc.dram_tensor("catT_s", [D, NTOK], BF16, kind="Internal").ap()
    x1_d = nc.dram_tensor("x1_s", [NTOK, D], F32, kind="Internal").ap()
    x1T_d = nc.dram_tensor("x1T_s", [D, NTOK], BF16, kind="Internal").ap()
    w1b_d = nc.dram_tensor("w1b_s", [D, 8192], BF16, kind="Internal").ap()
    w2b_d = nc.dram_tensor("w2b_s", [8192, D], BF16, kind="Internal").ap()
    dbg = {}
    if stop_after is not None:
        dbg["cqnT"] = nc.dram_tensor("dbg_cqnT", [128, 4 * NTOK], BF16, kind="ExternalOutput").ap()
        dbg["ckvnT"] = nc.dram_tensor("dbg_ckvnT", [128, 4 * SEQ], BF16, kind="ExternalOutput").ap()
        dbg["krT"] = nc.dram_tensor("dbg_krT", [128, SEQ], BF16, kind="ExternalOutput").ap()
        dbg["Tq"] = nc.dram_tensor("dbg_Tq", [128, NTOK], F32, kind="ExternalOutput").ap()
        dbg["catT"] = nc.dram_tensor("dbg_catT", [D, NTOK], BF16, kind="ExternalOutput").ap()
        dbg["x1"] = nc.dram_tensor("dbg_x1", [NTOK, D], F32, kind="ExternalOutput").ap()
        dbg["gen"] = nc.dram_tensor("dbg_gen", [128, 4096], BF16, kind="ExternalOutput").ap()
    dbg_keys = []

    with contextlib.ExitStack() as st:
        st.enter_context(nc.allow_non_contiguous_dma(reason="small param vectors"))

        def sb(name, shape, dt):
            return st.enter_context(nc.sbuf_tensor(name, shape, dt))

        arena = sb("arena", [128, ARENA_BYTES // 2], BF16)
        ident_f = sb("ident_f", [128, 128], F32)
        ident_b = sb("ident_b", [128, 128], BF16)
        ones_b = sb("ones_b", [128, 128], BF16)
        cst = sb("cst", [128, 8], F32)
        hmask = sb("hmask_t", [128, 2], F32)
        rc = sb("rc_t", [128, 2], F32)
        ginT = sb("ginT", [128, 16], F32)
        binT = sb("binT", [128, 16], F32)
        gcqT = sb("gcqT", [128, 4], F32)
        g1T = sb("g1T", [128, 16], F32)
        b1T = sb("b1T", [128, 16], F32)
        gckvT = sb("gckvT", [128, 4], F32)
        cbT = sb("cbT", [128, 8], F32)
        gclT = sb("gclT", [128, 8], F32)
        bclT = sb("bclT", [128, 8], F32)
        stats = sb("stats", [128, 4, 24], F32)
        mv = sb("mv", [128, 4, 2], F32)
        sd = sb("sd", [128, 4], F32)
        rstd = sb("rstd", [128, 4], F32)
        nmr = sb("nmr", [128, 4], F32)
        pb = [st.enter_context(nc.psum_tensor("pb%d" % i, [128, 512], F32)) for i in range(8)]

        def PB(i):
            return ("pb", i)

        def view(off, shape, dt):
            esz = 2 if dt == BF16 else 4
            n = 1
            for s_ in shape[1:]:
                n *= s_
            assert off % 4 == 0 and off + n * esz <= ARENA_BYTES, (off, n * esz)
            ap = arena[:, off // 2: off // 2 + n * esz // 2]
            if dt != BF16:
                ap = ap.bitcast(dt)
            if len(shape) == 3:
                ap = ap.rearrange("p (a b) -> p a b", a=shape[1])
            elif len(shape) == 4:
                ap = ap.rearrange("p (a b c) -> p a b c", a=shape[1], b=shape[2])
            return ap

        class Carver:
            def __init__(self, base=0):
                self.off = base

            def get(self, shape, dt):
                esz = 2 if dt == BF16 else 4
                n = 1
                for s_ in shape[1:]:
                    n *= s_
                v = view(self.off, shape, dt)
                self.off += (n * esz + 3) // 4 * 4
                return v

        A("sp", lambda e: e.dma_start(out=ident_f[:], in_=ident_d[:, :]), w=["ident_f"], dma=True)
        A("sp", lambda e: e.dma_start(out=hmask[:], in_=hmask_d[:, :]), w=["hmask"], dma=True)
        A("sp", lambda e: e.dma_start(out=rc[:], in_=rc_d[:, :]), w=["rc"], dma=True)

        def ldT(dst, src, nch, key):
            A("sp", lambda e: e.dma_start(out=dst[:], in_=src.rearrange("o (c p) -> p (o c)", p=128)),
              w=[key], dma=True)

        ldT(ginT, ln_in_g, 16, "ginT")
        ldT(binT, ln_in_b, 16, "binT")
        ldT(gcqT, g_cq, 4, "gcqT")
        ldT(g1T, g_ln1, 16, "g1T")
        ldT(b1T, b_ln1, 16, "b1T")
        ldT(gckvT, g_ckv, 4, "gckvT")
        ldT(cbT, conv_b, 8, "cbT")
        ldT(gclT, g_cln, 8, "gclT")
        ldT(bclT, b_cln, 8, "bclT")
        A("dve", lambda e: e.tensor_copy(ident_b[:], ident_f[:]), r=["ident_f"], w=["ident_b"])
        A("pool", lambda e: e.memset(ones_b[:], 1.0), w=["ones_b"])
        A("pool", lambda e: e.memset(cst[:, 0:1], LN_EPS), w=["cst0"])
        A("pool", lambda e: e.memset(cst[:, 1:2], RMS_EPS), w=["cst1"])
        A("pool", lambda e: e.memset(cst[:, 2:3], 0.0), w=["cst2"])
        CST = ["cst0", "cst1", "cst2"]

        if stop_after == "C0":
            A("sp", lambda e: e.dma_start(out=dbg["gen"][:, 0:128], in_=ident_b[:]), r=["ident_b", "ginT", "binT", "gcqT", "gckvT", "cbT", "gclT", "bclT", "hmask", "rc", "ones_b"] + CST, w=["dbg_gen"], dma=True)
            dbg_keys.append("dbg_gen")
            S.enabled = False
        gctr = [0]

        def ln_part(row0, xts, xn, xnk, nt=4):
            g = gctr[0]
            gctr[0] += 1
            nb = len(xts)
            used = [xts[(g * nt + t) % nb] for t in range(nt)]

            def load(t):
                xb, xk = used[t]
                A("sp", lambda e, xb=xb, t=t: e.dma_start(out=xb, in_=x[row0 + t * 128: row0 + (t + 1) * 128, :]),
                  w=[xk], dma=True)
            for t in range(min(nb, nt)):
                load(t)
            for t in range(nt):
                xb, xk = used[t]

                def bns(e, xb=xb, t=t):
                    for c in range(4):
                        i = e.bn_stats(out=stats[:, t, c * 6:(c + 1) * 6], in_=xb[:, c * 512:(c + 1) * 512])
                    return i
                A("dve", bns, r=[xk], w=[("stats", t)])
                A("dve", lambda e, t=t: e.bn_aggr(out=mv[:, t, :], in_=stats[:, t, :]), r=[("stats", t)], w=[("mv", t)])
                A("act", lambda e, t=t: e.activation(out=sd[:, t:t + 1], in_=mv[:, t, 1:2], func=AF.Sqrt, bias=cst[:, 0:1], scale=1.0),
                  r=[("mv", t), "cst0"], w=[("sd", t)])
                A("dve", lambda e, t=t: e.reciprocal(out=rstd[:, t:t + 1], in_=sd[:, t:t + 1]), r=[("sd", t)], w=[("rstd", t)])
                A("dve", lambda e, t=t: e.scalar_tensor_tensor(out=nmr[:, t:t + 1], in0=mv[:, t, 0:1], scalar=-1.0, in1=rstd[:, t:t + 1],
                                                               op0=ALU.mult, op1=ALU.mult), r=[("mv", t), ("rstd", t)], w=[("nmr", t)])
                A("act", lambda e, xb=xb, t=t: e.activation(out=xn[:, t, :], in_=xb, func=AF.Identity,
                                                           scale=rstd[:, t:t + 1], bias=nmr[:, t:t + 1]),
                  r=[xk, ("rstd", t), ("nmr", t)], w=[(xnk, t)])
                if t + nb < nt:
                    load(t + nb)

        def tr_part(xn, xnk, dstT, dkey, trb, nt=4):
            for c in range(16):
                bank = trb[c % len(trb)]
                pv = pb[bank][:].bitcast(BF16)[:, 0:nt * 128]

                def tr(e, c=c, pv=pv):
                    for t in range(nt):
                        i = e.transpose(pv[:, t * 128:(t + 1) * 128], xn[:, t, c * 128:(c + 1) * 128], ident_b[:])
                    return i
                A("pe", tr, r=[(xnk, t) for t in range(nt)] + ["ident_b"], w=[PB(bank)])
                if c % 2 == 0:
                    A("act", lambda e, c=c, pv=pv: e.activation(out=dstT[:, c, 0:nt * 128], in_=pv, func=AF.Identity,
                                                               scale=ginT[:, c:c + 1], bias=binT[:, c:c + 1]),
                      r=[PB(bank), "ginT", "binT"], w=[(dkey, c)])
                else:
                    A("dve", lambda e, c=c, pv=pv: e.tensor_scalar(out=dstT[:, c, 0:nt * 128], in0=pv, scalar1=ginT[:, c:c + 1],
                                                                  scalar2=binT[:, c:c + 1], op0=ALU.mult, op1=ALU.add),
                      r=[PB(bank), "ginT", "binT"], w=[(dkey, c)])

        def rms_block(src_banks, gT, gkey, dst_fn, dkeys, tmp, tkey, statbank, n=512):
            sq, sdv, rs = tmp["sq"], tmp["sdv"], tmp["rs"]
            for m in range(4):
                A("act", lambda e, m=m: e.activation(out=sq[:, m, 0:n], in_=pb[src_banks[m]][:, 0:n], func=AF.Square),
                  r=[PB(src_banks[m])], w=[(tkey, "sq", m)])

            def mm(e):
                for m in range(4):
                    i = e.matmul(pb[statbank][:, 0:n], ones_b[:], sq[:, m, 0:n], start=(m == 0), stop=(m == 3))
                return i
            A("pe", mm, r=[(tkey, "sq", m) for m in range(4)] + ["ones_b"], w=[PB(statbank)])
            A("act", lambda e: e.activation(out=sdv[:, 0:n], in_=pb[statbank][:, 0:n], func=AF.Sqrt, bias=cst[:, 1:2],
                                            scale=1.0 / 512.0), r=[PB(statbank), "cst1"], w=[(tkey, "sdv")])
            A("dve", lambda e: e.reciprocal(out=rs[:, 0:n], in_=sdv[:, 0:n]), r=[(tkey, "sdv")], w=[(tkey, "rs")])
            for m in range(4):
                A("dve", lambda e, m=m: e.scalar_tensor_tensor(out=dst_fn(m), in0=pb[src_banks[m]][:, 0:n], scalar=gT[:, m:m + 1],
                                                              in1=rs[:, 0:n], op0=ALU.mult, op1=ALU.mult),
                  r=[PB(src_banks[m]), (tkey, "rs"), gkey], w=[dkeys[m]])

        OFF_CQN = 0
        cqnT = view(OFF_CQN, [128, 4, NTOK], BF16)
        P0 = 16 * 1024
        cv = Carver(P0)
        x0T = cv.get([128, 16, NTOK], BF16)
        x0Th = cv.get([128, 16, 32], BF16)
        uT = cv.get([128, 8, 2080], BF16)
        R1 = cv.off
        xts = [(cv.get([128, D], F32), ("xt", i)) for i in range(4)]
        xn = cv.get([128, 4, D], BF16)
        xn2 = cv.get([128, 4, D], BF16)
        xnl = [xn, xn2]
        ln_part(0, xts, xnl[0], ("xn", 0))
        for g in range(4):
            if g + 1 < 4:
                ln_part((g + 1) * 512, xts, xnl[(g + 1) % 2], ("xn", (g + 1) % 2))
            tr_part(xnl[g % 2], ("xn", g % 2), x0T[:, :, g * 512:(g + 1) * 512], ("x0T", g), (4, 5, 6, 7))
        if stop_after == "A0g":
            A("sp", lambda e: e.dma_start(out=dbg["gen"][:, :], in_=x0T[:, 0:2, :].rearrange("p a b -> p (a b)")),
              r=[(("x0T", g), c) for g in range(4) for c in range(2)], w=["dbg_gen"], dma=True)
            dbg_keys.append("dbg_gen")
            S.enabled = False
        xh, xhk = xts[0]
        xnh = xn
        A("sp", lambda e: e.dma_start(out=xh[0:16, :], in_=x[SEQ - 16:SEQ, :]), w=[xhk], dma=True)
        A("sp", lambda e: e.dma_start(out=xh[16:32, :], in_=x[NTOK:NTOK + 16, :]), w=[xhk + ("b",)], r=[xhk], dma=True)
        def bnsh(e):
            for c in range(4):
                i = e.bn_stats(out=stats[0:32, 0, c * 6:(c + 1) * 6], in_=xh[0:32, c * 512:(c + 1) * 512])
            return i
        A("dve", bnsh, r=[xhk, xhk + ("b",)], w=[("stats", 0)])
        A("dve", lambda e: e.bn_aggr(out=mv[0:32, 0, :], in_=stats[0:32, 0, :]), r=[("stats", 0)], w=[("mv", 0)])
        A("act", lambda e: e.activation(out=sd[0:32, 0:1], in_=mv[0:32, 0:1, 1], func=AF.Sqrt, bias=cst[0:32, 0:1], scale=1.0),
          r=[("mv", 0), "cst0"], w=[("sd", 0)])
        A("dve", lambda e: e.reciprocal(out=rstd[0:32, 0:1], in_=sd[0:32, 0:1]), r=[("sd", 0)], w=[("rstd", 0)])
        A("dve", lambda e: e.scalar_tensor_tensor(out=nmr[0:32, 0:1], in0=mv[0:32, 0:1, 0], scalar=-1.0, in1=rstd[0:32, 0:1],
                                                  op0=ALU.mult, op1=ALU.mult), r=[("mv", 0), ("rstd", 0)], w=[("nmr", 0)])
        A("act", lambda e: e.activation(out=xnh[0:32, 0, :], in_=xh[0:32, :], func=AF.Identity, scale=rstd[0:32, 0:1],
                                        bias=nmr[0:32, 0:1]), r=[xhk, xhk + ("b",), ("rstd", 0), ("nmr", 0)], w=[(("xn", 0), 0)])
        for c in range(16):
            bank = (4, 5, 6, 7)[c % 4]
            pv = pb[bank][:].bitcast(BF16)[:, 0:32]
            A("pe", lambda e, c=c, pv=pv: e.transpose(pv, xnh[0:32, 0, c * 128:(c + 1) * 128], ident_b[0:32, 0:32]),
              r=[(("xn", 0), 0), "ident_b"], w=[PB(bank)])
            A("dve", lambda e, c=c, pv=pv: e.tensor_scalar(out=x0Th[:, c, :], in0=pv, scalar1=ginT[:, c:c + 1],
                                                          scalar2=binT[:, c:c + 1], op0=ALU.mult, op1=ALU.add),
              r=[PB(bank), "ginT", "binT"], w=[("x0Th", c)])
        if stop_after == "A0a":
            A("sp", lambda e: e.dma_start(out=dbg["gen"][:, :], in_=x0T[:, 0:2, :].rearrange("p a b -> p (a b)")),
              r=[(("x0T", g), c) for g in range(4) for c in range(2)] + [("x0Th", c) for c in range(16)], w=["dbg_gen"], dma=True)
            dbg_keys.append("dbg_gen")
            S.enabled = False
        S.barrier()
        cv = Carver(R1)
        wcq = cv.get([128, 16, 512], BF16)
        wch = [cv.get([128, 16, 256], BF16) for _ in range(2)]
        tmpA = {"sq": cv.get([128, 4, 512], BF16),
                "sdv": cv.get([128, 512], F32), "rs": cv.get([128, 512], F32)}
        sig = [cv.get([128, 512], F32) for _ in range(2)]
        uh = cv.get([128, 32], F32)
        X0K = lambda g: [(("x0T", g), c) for c in range(16)]
        w_in_r = w_in.rearrange("(k p) n -> p k n", p=128)
        A("pool", lambda e: e.dma_start(out=wcq, in_=w_in_r[:, :, 0:512]), w=["wcq"], dma=True)
        for tb in range(4):
            banks = [0, 1, 2, 3]
            for m in range(4):
                def mm(e, m=m, tb=tb):
                    for k in range(16):
                        i = e.matmul(pb[m][:], wcq[:, k, m * 128:(m + 1) * 128], x0T[:, k, tb * 512:(tb + 1) * 512],
                                     start=(k == 0), stop=(k == 15))
                    return i
                A("pe", mm, r=["wcq"] + X0K(tb), w=[PB(m)])
            rms_block(banks, gcqT, "gcqT", lambda m, tb=tb: cqnT[:, m, tb * 512:(tb + 1) * 512],
                      [("cqnT", m, tb) for m in range(4)], tmpA, "tA", 4)
        for ch in range(8):
            wb = wch[ch % 2]
            wk = ("wch", ch % 2)
            A("pool", lambda e, wb=wb, ch=ch: e.dma_start(out=wb[:, :, 0:128], in_=w_in_r[:, :, 1088 + ch * 128:1088 + (ch + 1) * 128]),
              w=[wk], dma=True)
            A("pool", lambda e, wb=wb, ch=ch: e.dma_start(out=wb[:, :, 128:256], in_=w_in_r[:, :, 2112 + ch * 128:2112 + (ch + 1) * 128]),
              w=[wk + ("g",)], dma=True)
            for tb in range(5):
                n = 512 if tb < 4 else 32
                ba, bg = (0, 1) if tb % 2 == 0 else (2, 3)
                rhs_fn = (lambda k, tb=tb: x0T[:, k, tb * 512:(tb + 1) * 512]) if tb < 4 else (lambda k: x0Th[:, k, :])
                rk = X0K(tb) if tb < 4 else [("x0Th", c) for c in range(16)]

                def mm(e, wb=wb, ba=ba, bg=bg, n=n, rhs_fn=rhs_fn):
                    for k in range(16):
                        e.matmul(pb[ba][:, 0:n], wb[:, k, 0:128], rhs_fn(k), start=(k == 0), stop=(k == 15))
                    for k in range(16):
                        i = e.matmul(pb[bg][:, 0:n], wb[:, k, 128:256], rhs_fn(k), start=(k == 0), stop=(k == 15))
                    return i
                A("pe", mm, r=[wk, wk + ("g",)] + rk, w=[PB(ba), PB(bg)])
                sg = sig[tb % 2]
                sk = ("sig", tb % 2)
                A("act", lambda e, sg=sg, bg=bg, n=n: e.activation(out=sg[:, 0:n], in_=pb[bg][:, 0:n], func=AF.Sigmoid),
                  r=[PB(bg)], w=[sk])
                if tb < 4:
                    A("dve", lambda e, sg=sg, ba=ba, ch=ch, tb=tb: e.tensor_tensor(
                        out=uT[:, ch, 16 + tb * 512:16 + (tb + 1) * 512], in0=pb[ba][:], in1=sg[:], op=ALU.mult),
                      r=[PB(ba), sk], w=[("uT", ch, tb)])
                else:
                    A("dve", lambda e, sg=sg, ba=ba: e.tensor_tensor(out=uh[:], in0=pb[ba][:, 0:32], in1=sg[:, 0:32], op=ALU.mult),
                      r=[PB(ba), sk], w=["uh"])
                    A("dve", lambda e, ch=ch: e.tensor_scalar(out=uT[:, ch, 0:16], in0=uh[:, 0:16], scalar1=hmask[:, 0:1], scalar2=0.0,
                                                             op0=ALU.mult), r=["uh", "hmask"], w=[("uT", ch, "lo")])
                    A("dve", lambda e, ch=ch: e.tensor_scalar(out=uT[:, ch, 2064:2080], in0=uh[:, 16:32], scalar1=hmask[:, 1:2], scalar2=0.0,
                                                             op0=ALU.mult), r=["uh", "hmask"], w=[("uT", ch, "hi")])
        if stop_after == "A0b":
            A("sp", lambda e: e.dma_start(out=dbg["gen"][:, 0:2080], in_=uT[:, 0, :]),
              r=[("uT", 0, q) for q in (0, 1, 2, 3, "lo", "hi")] + [("cqnT", m, tb) for m in range(4) for tb in range(4)], w=["dbg_gen"], dma=True)
            dbg_keys.append("dbg_gen")
            S.enabled = False
        S.barrier()
        cv = Carver(P0)
        diag = cv.get([128, 8, 31, 128], BF16)
        assert cv.off <= P0 + 65 * 1024
        cv = Carver(R1)
        cw_sb = cv.get([128, 1024], F32)
        cwT = cv.get([128, 8, 31], F32)
        cw_bf = cv.get([128, 1024], BF16)
        ycv = cv.get([128, 8, 512], F32)
        ybf = cv.get([128, 8, 512], BF16)
        ysq = cv.get([128, 8, 512], BF16)
        mean = cv.get([128, 512], F32)
        ex2 = cv.get([128, 512], F32)
        rsc = cv.get([128, 512], F32)
        costg = [cv.get([128, 8, 512], BF16) for _ in range(2)]
        A("sp", lambda e: e.dma_start(out=cw_sb[0:31, :], in_=conv_w[:, :]), w=["cw_sb"], dma=True)
        A("dve", lambda e: e.tensor_copy(cw_bf[0:31, :], cw_sb[0:31, :]), r=["cw_sb"], w=["cw_bf"])
        pv0 = pb[0][:].bitcast(BF16)
        for ch in range(8):
            A("pe", lambda e, ch=ch: e.transpose(pv0[:, ch * 32:ch * 32 + 31], cw_bf[0:31, ch * 128:(ch + 1) * 128], ident_b[0:31, 0:31]),
              r=["cw_bf", "ident_b"], w=[PB(0)])
        A("dve", lambda e: e.tensor_copy(cwT[:], pv0[:, 0:256].rearrange("p (c k) -> p c k", k=32)[:, :, 0:31]),
          r=[PB(0)], w=["cwT"])
        for ch in range(8):
            A("dve", lambda e, ch=ch: e.tensor_tensor(out=diag[:, ch, :, :], in0=ident_b[:].unsqueeze(1).to_broadcast([128, 31, 128]),
                                                     in1=cwT[:, ch, :].unsqueeze(2).to_broadcast([128, 31, 128]), op=ALU.mult),
              r=["ident_b", "cwT"], w=[("diag", ch)])
        for tb in range(4):
            for ch in range(8):
                bk = 1 + (ch % 4)

                def mm(e, ch=ch, tb=tb, bk=bk):
                    for k in range(31):
                        i = e.matmul(pb[bk][:], diag[:, ch, k, :], uT[:, ch, tb * 512 + k + 1: tb * 512 + k + 1 + 512],
                                     start=(k == 0), stop=(k == 30))
                    return i
                A("pe", mm, r=[("diag", ch)] + [("uT", ch, q) for q in (0, 1, 2, 3, "lo", "hi")], w=[PB(bk)])
                A("act", lambda e, ch=ch, bk=bk: e.activation(out=ycv[:, ch, :], in_=pb[bk][:], func=AF.Identity, bias=cbT[:, ch:ch + 1], scale=1.0),
                  r=[PB(bk), "cbT"], w=[("ycv", ch)])
                A("act", lambda e, ch=ch: e.activation(out=ysq[:, ch, :], in_=ycv[:, ch, :], func=AF.Square), r=[("ycv", ch)], w=[("ysq", ch)])
                A("dve", lambda e, ch=ch: e.tensor_copy(ybf[:, ch, :], ycv[:, ch, :]), r=[("ycv", ch)], w=[("ybf", ch)])

            def mm1(e):
                for ch in range(8):
                    i = e.matmul(pb[5][:], ones_b[:], ybf[:, ch, :], start=(ch == 0), stop=(ch == 7))
                return i

            def mm2(e):
                for ch in range(8):
                    i = e.matmul(pb[6][:], ones_b[:], ysq[:, ch, :], start=(ch == 0), stop=(ch == 7))
                return i
            A("pe", mm1, r=[("ybf", ch) for ch in range(8)] + ["ones_b"], w=[PB(5)])
            A("pe", mm2, r=[("ysq", ch) for ch in range(8)] + ["ones_b"], w=[PB(6)])
            A("act", lambda e: e.activation(out=mean[:], in_=pb[5][:], func=AF.Identity, scale=1.0 / 1024.0, bias=cst[:, 2:3]),
              r=[PB(5), "cst2"], w=["mean"])
            A("act", lambda e: e.activation(out=ex2[:], in_=pb[6][:], func=AF.Identity, scale=1.0 / 1024.0, bias=cst[:, 2:3]),
              r=[PB(6), "cst2"], w=["ex2"])
            A("dve", lambda e: e.tensor_tensor(out=rsc[:], in0=mean[:], in1=mean[:], op=ALU.mult), r=["mean"], w=["rsc"])
            A("dve", lambda e: e.tensor_tensor(out=ex2[:], in0=ex2[:], in1=rsc[:], op=ALU.subtract), r=["ex2", "rsc"], w=["ex2"])
            A("act", lambda e: e.activation(out=rsc[:], in_=ex2[:], func=AF.Sqrt, bias=cst[:, 0:1], scale=1.0), r=["ex2", "cst0"], w=["rsc"])
            A("dve", lambda e: e.reciprocal(out=rsc[:], in_=rsc[:]), r=["rsc"], w=["rsc"])
            A("dve", lambda e: e.tensor_tensor(out=ycv[:], in0=ycv[:], in1=mean[:].unsqueeze(1).to_broadcast([128, 8, 512]), op=ALU.subtract),
              r=[("ycv", ch) for ch in range(8)] + ["mean"], w=[("ycv", ch) for ch in range(8)])
            A("dve", lambda e: e.tensor_tensor(out=ycv[:], in0=ycv[:], in1=rsc[:].unsqueeze(1).to_broadcast([128, 8, 512]), op=ALU.mult),
              r=[("ycv", ch) for ch in range(8)] + ["rsc"], w=[("ycv", ch) for ch in range(8)])
            cs = costg[tb % 2]
            ck = ("costg", tb % 2)
            for ch in range(8):
                A("act", lambda e, ch=ch, cs=cs: e.activation(out=cs[:, ch, :], in_=ycv[:, ch, :], func=AF.Silu, scale=gclT[:, ch:ch + 1],
                                                             bias=bclT[:, ch:ch + 1]), r=[("ycv", ch), "gclT", "bclT"], w=[ck + (ch,)])
            A("sp", lambda e, cs=cs, tb=tb: e.dma_start(out=catT_d[1024:2048, tb * 512:(tb + 1) * 512].rearrange("(c p) n -> p c n", p=128), in_=cs[:]),
              r=[ck + (ch,) for ch in range(8)], w=[("catT_d", "conv", tb)], dma=True)
        S.barrier()
        if stop_after is not None and S.enabled:
            A("sp", lambda e: e.dma_start(out=dbg["cqnT"][:, :], in_=view(OFF_CQN, [128, 4 * NTOK], BF16)),
              r=[("cqnT", m, tb) for m in range(4) for tb in range(4)], w=["dbg_cqnT"], dma=True)
            dbg_keys.append("dbg_cqnT")
        PH = {None: 9, "C0": -1, "A0g": -1, "A0a": -1, "A0b": -1, "A0": 0, "A1": 1, "B": 2, "C": 3, "D": 4}[stop_after]
        def stop_here(k):
            if PH != k or not S.enabled:
                return
            CATD_ = [kk for kk in S.bufs if isinstance(kk, tuple) and kk[0] == "catT_d"]
            if k == 1:
                A("sp", lambda e: e.dma_start(out=dbg["ckvnT"][:, :], in_=ckvnT.rearrange("p a b -> p (a b)")),
                  r=[("ckvnT", m, j) for m in range(4) for j in range(16)], w=["dbg_ckvnT"], dma=True)
                A("sp", lambda e: e.dma_start(out=dbg["krT"][:, :], in_=krT), r=[("krT", j) for j in range(16)], w=["dbg_krT"], dma=True)
                A("sp", lambda e: e.dma_start(out=dbg["Tq"][:, :], in_=Tq), r=[("Tq", j, s_) for j in range(4) for s_ in range(2)], w=["dbg_Tq"], dma=True)
                dbg_keys.extend(["dbg_ckvnT", "dbg_krT", "dbg_Tq"])
            if k in (0, 2):
                lo = 1024 if k == 0 else 0
                A("sp", lambda e: e.dma_start(out=dbg["catT"][lo:2048, :], in_=catT_d[lo:2048, :]), r=CATD_, w=["dbg_catT"], dma=True)
                dbg_keys.append("dbg_catT")
            if k == 3:
                A("sp", lambda e: e.dma_start(out=dbg["x1"][:, :], in_=x1_d[:, :]), r=[("x1_d", i) for i in range(16)], w=["dbg_x1"], dma=True)
                dbg_keys.append("dbg_x1")
            S.enabled = False

        stop_here(0)

        cv = Carver(16 * 1024)
        ckvnT = cv.get([128, 4, SEQ], BF16)
        krT = cv.get([128, SEQ], BF16)
        Tq = cv.get([128, NTOK], F32)
        PB_BASE = cv.off
        xts = [(cv.get([128, D], F32), ("xt", i)) for i in range(2)]
        xn = cv.get([128, 4, D], BF16)
        x0Tb = cv.get([128, 16, 512], BF16)
        wkv = cv.get([128, 16, 768], BF16)
        tmpB = {"sq": cv.get([128, 4, 512], BF16),
                "sdv": cv.get([128, 512], F32), "rs": cv.get([128, 512], F32)}
        posi = cv.get([128, 512], I32)
        u2 = cv.get([128, 2, 512], F32)
        k2i = cv.get([128, 2, 512], I32)
        k2f = cv.get([128, 2, 512], F32)
        tab = u2
        t1 = cv.get([128, 512], F32)
        A("pool", lambda e: e.dma_start(out=wkv[:, :, 0:512], in_=w_in_r[:, :, 512:1024]), w=["wkv0"], dma=True)
        for hcol, src in ((512, 1024), (576, 1024), (640, 1056), (672, 1024), (704, 1056), (736, 1024)):
            wdt = 64 if hcol < 640 else 32
            A("pool", lambda e, hcol=hcol, src=src, wdt=wdt: e.dma_start(out=wkv[:, :, hcol:hcol + wdt], in_=w_in_r[:, :, src:src + wdt]),
              w=[("wkv", hcol)], dma=True)
        WKV = ["wkv0"] + [("wkv", h_) for h_ in (512, 576, 640, 672, 704, 736)]
        ln_part(0, xts, xn, "xnB")
        for j in range(16):
            tr_part(xn, "xnB", x0Tb, "x0Tb", (6, 7))
            xk = [("x0Tb", c) for c in range(16)]
            for m in range(6):
                def mm(e, m=m):
                    for k in range(16):
                        i = e.matmul(pb[m][:], wkv[:, k, m * 128:(m + 1) * 128], x0Tb[:, k, :], start=(k == 0), stop=(k == 15))
                    return i
                A("pe", mm, r=WKV + xk, w=[PB(m)])
            if j + 1 < 16:
                ln_part((j + 1) * 512, xts, xn, "xnB")
            A("sp", lambda e, j=j: e.dma_start(out=posi[:], in_=pos[0:1, j * 512:(j + 1) * 512].partition_broadcast(128)), w=["posi"], dma=True)
            A("dve", lambda e: e.tensor_copy(t1[:], posi[:]), r=["posi"], w=["t1"])
            A("dve", lambda e: e.tensor_scalar(out=u2[:, 0, :], in0=t1[:], scalar1=rc[:, 0:1], scalar2=0.25, op0=ALU.mult, op1=ALU.add),
              r=["t1", "rc"], w=["u2a"])
            A("dve", lambda e: e.tensor_scalar(out=u2[:, 1, :], in0=t1[:], scalar1=rc[:, 0:1], scalar2=0.0, op0=ALU.mult, op1=ALU.add),
              r=["t1", "rc"], w=["u2b"])
            A("dve", lambda e: e.tensor_copy(k2i[:], u2[:]), r=["u2a", "u2b"], w=["k2i"])
            A("dve", lambda e: e.tensor_copy(k2f[:], k2i[:]), r=["k2i"], w=["k2f"])
            A("dve", lambda e: e.tensor_tensor(out=u2[:], in0=u2[:], in1=k2f[:], op=ALU.subtract), r=["u2a", "u2b", "k2f"], w=["u2a", "u2b"])
            A("dve", lambda e: e.scalar_tensor_tensor(out=k2f[:], in0=u2[:], scalar=0.5, in1=u2[:], op0=ALU.is_gt, op1=ALU.subtract),
              r=["u2a", "u2b"], w=["k2f"])
            A("act", lambda e: e.activation(out=tab[:, 0, :], in_=k2f[:, 0, :], func=AF.Sin, scale=-2.0 * math.pi), r=["k2f", "u2a"], w=["u2a"])
            A("act", lambda e: e.activation(out=tab[:, 1, :], in_=k2f[:, 1, :], func=AF.Sin, scale=rc[:, 1:2]), r=["k2f", "rc", "u2b"], w=["u2b"])
            if j < 4:
                A("pool", lambda e, j=j: e.tensor_copy(Tq[0:64, j * 512:(j + 1) * 512], tab[0:64, 0, :]), r=["u2a"], w=[("Tq", j, 0)])
                A("pool", lambda e, j=j: e.tensor_copy(Tq[64:128, j * 512:(j + 1) * 512], tab[64:128, 1, :]), r=["u2b"], w=[("Tq", j, 1)])
            A("dve", lambda e: e.tensor_tensor(out=t1[:], in0=pb[4][:], in1=tab[:, 0, :], op=ALU.mult), r=[PB(4), "u2a"], w=["t1"])
            A("dve", lambda e: e.tensor_tensor(out=k2f[:, 0, :], in0=pb[5][:], in1=tab[:, 1, :], op=ALU.mult), r=[PB(5), "u2b", "k2f"], w=["k2f"])
            A("dve", lambda e, j=j: e.tensor_tensor(out=krT[:, j * 512:(j + 1) * 512], in0=t1[:], in1=k2f[:, 0, :], op=ALU.add),
              r=["t1", "k2f"], w=[("krT", j)])
            rms_block([0, 1, 2, 3], gckvT, "gckvT", lambda m, j=j: ckvnT[:, m, j * 512:(j + 1) * 512],
                      [("ckvnT", m, j) for m in range(4)], tmpB, "tB", 4)
        S.barrier()
        stop_here(1)

        W1B = [("w1b", i) for i in range(8)]
        W2B = [("w2b", i) for i in range(8)]
        cv = Carver(PB_BASE)
        KT = cv.get([128, SEQ], BF16)
        Vh = cv.get([128, 64, 128], BF16)
        qnT = cv.get([128, NTOK], BF16)
        qrT = cv.get([128, NTOK], BF16)
        wq = [cv.get([128, 4, 256], BF16) for _ in range(2)]
        wk_ = [cv.get([128, 4, 128], BF16) for _ in range(2)]
        wv_ = [cv.get([128, 4, 128], BF16) for _ in range(2)]
        NPT = 6
        PT = [cv.get([128, 512], BF16) for _ in range(NPT)]
        rec = cv.get([128, 512], F32)
        ostg = [cv.get([128, 512], BF16) for _ in range(2)]
        w_uq_r = w_uq.rearrange("(k p) n -> p k n", p=128)
        w_uk_r = w_uk.rearrange("(k p) n -> p k n", p=128)
        w_uv_r = w_uv.rearrange("(k p) n -> p k n", p=128)
        CQ = [("cqnT", m, tb) for m in range(4) for tb in range(4)]
        CKV = lambda j: [("ckvnT", m, j) for m in range(4)]
        TQK = [("Tq", j, s_) for j in range(4) for s_ in range(2)]
        sctr = [0]
        ectr = [0]
        for h in range(8):
            p2 = h % 2
            c0 = h * 192
            for (dst0, src0, wdt) in ((0, c0, 128), (128, c0 + 128, 64), (192, c0 + 160, 32), (224, c0 + 128, 32)):
                A("pool", lambda e, p2=p2, dst0=dst0, src0=src0, wdt=wdt: e.dma_start(out=wq[p2][:, :, dst0:dst0 + wdt], in_=w_uq_r[:, :, src0:src0 + wdt]),
                  w=[("wq", p2, dst0)], dma=True)
            A("pool", lambda e, p2=p2, h=h: e.dma_start(out=wk_[p2], in_=w_uk_r[:, :, h * 128:(h + 1) * 128]), w=[("wk", p2)], dma=True)
            A("pool", lambda e, p2=p2, h=h: e.dma_start(out=wv_[p2], in_=w_uv_r[:, :, h * 128:(h + 1) * 128]), w=[("wv", p2)], dma=True)
            WQ = [("wq", p2, d_) for d_ in (0, 128, 192, 224)]
            A("pool", lambda e, i=h: e.dma_start(out=w1b_d[i * 256:(i + 1) * 256, :], in_=w_ff1[i * 256:(i + 1) * 256, :]),
              w=[("w1b", h)], dma=True)
            A("pool", lambda e, i=h: e.dma_start(out=w2b_d[i * 1024:(i + 1) * 1024, :], in_=w_ff2[i * 1024:(i + 1) * 1024, :]),
              w=[("w2b", h)], dma=True)

            def evac(out_ap, bank, rk, wkeys):
                i = ectr[0]
                ectr[0] += 1
                if i % 2 == 0:
                    A("act", lambda e: e.activation(out=out_ap, in_=pb[bank][:], func=AF.Identity, bias=cst[:, 2:3], scale=1.0),
                      r=[PB(bank), "cst2"] + rk, w=wkeys)
                else:
                    A("dve", lambda e: e.tensor_copy(out_ap, pb[bank][:]), r=[PB(bank)] + rk, w=wkeys)

            for tb in range(4):
                bk = sctr[0] % 4
                sctr[0] += 1

                def mm(e, tb=tb, bk=bk, p2=p2):
                    for k in range(4):
                        i = e.matmul(pb[bk][:], wq[p2][:, k, 0:128], cqnT[:, k, tb * 512:(tb + 1) * 512], start=(k == 0), stop=(k == 3))
                    return i
                A("pe", mm, r=WQ + CQ, w=[PB(bk)])
                evac(qnT[:, tb * 512:(tb + 1) * 512], bk, [], [("qnT", tb)])
                bk = sctr[0] % 4
                sctr[0] += 1

                def mm(e, tb=tb, bk=bk, p2=p2):
                    for k in range(4):
                        i = e.matmul(pb[bk][:], wq[p2][:, k, 128:256], cqnT[:, k, tb * 512:(tb + 1) * 512], start=(k == 0), stop=(k == 3))
                    return i
                A("pe", mm, r=WQ + CQ, w=[PB(bk)])
                A("dve", lambda e, tb=tb, bk=bk: e.tensor_tensor(out=qrT[:, tb * 512:(tb + 1) * 512], in0=pb[bk][:], in1=Tq[:, tb * 512:(tb + 1) * 512], op=ALU.mult),
                  r=[PB(bk)] + TQK, w=[("qrT", tb)])
            for j in range(16):
                bk = sctr[0] % 4
                sctr[0] += 1

                def mm(e, j=j, bk=bk, p2=p2):
                    for k in range(4):
                        i = e.matmul(pb[bk][:], wk_[p2][:, k, :], ckvnT[:, k, j * 512:(j + 1) * 512], start=(k == 0), stop=(k == 3))
                    return i
                A("pe", mm, r=[("wk", p2)] + CKV(j), w=[PB(bk)])
                evac(KT[:, j * 512:(j + 1) * 512], bk, [], [("KT", j)])
                bk = sctr[0] % 4
                sctr[0] += 1

                def mm(e, j=j, bk=bk, p2=p2):
                    for i4 in range(4):
                        kt = j * 4 + i4
                        for k in range(4):
                            i = e.matmul(pb[bk][:, i4 * 128:(i4 + 1) * 128], ckvnT[:, k, kt * 128:(kt + 1) * 128], wv_[p2][:, k, :],
                                         start=(k == 0), stop=(k == 3))
                    return i
                A("pe", mm, r=[("wv", p2)] + CKV(j), w=[PB(bk)])
                evac(Vh[:, j * 4:(j + 1) * 4, :].rearrange("p a b -> p (a b)"), bk, [], [("Vh", j)])
            QK = [("qnT", tb) for tb in range(4)] + [("qrT", tb) for tb in range(4)]
            for qb in range(4):
                ob = 4 + (qb % 2) * 2
                db = ob + 1
                qs = slice(qb * 512, (qb + 1) * 512)
                sb_of = {}

                def score(kt, qs=qs, qb=qb, sb_of=sb_of):
                    bk = sctr[0] % 4
                    sctr[0] += 1
                    sb_of[kt] = bk

                    def mm(e, kt=kt, bk=bk, qs=qs):
                        e.matmul(pb[bk][:], KT[:, kt * 128:(kt + 1) * 128], qnT[:, qs], start=True, stop=False)
                        return e.matmul(pb[bk][:], krT[:, kt * 128:(kt + 1) * 128], qrT[:, qs], start=False, stop=True)
                    A("pe", mm, r=[("KT", kt // 4), ("krT", kt // 4), ("qnT", qb), ("qrT", qb)], w=[PB(bk)])

                score(0)
                score(1)
                for kt in range(64):
                    if kt + 2 < 64:
                        score(kt + 2)
                    bk = sb_of[kt]
                    pt = PT[kt % NPT]
                    pk = ("PT", kt % NPT)
                    A("act", lambda e, bk=bk, pt=pt: e.activation(out=pt[:], in_=pb[bk][:], func=AF.Exp, scale=SCALE, bias=cst[:, 2:3]),
                      r=[PB(bk), "cst2"], w=[pk])

                    def mm(e, kt=kt, pt=pt, ob=ob, db=db):
                        e.matmul(pb[ob][:], Vh[:, kt, :], pt[:], start=(kt == 0), stop=(kt == 63))
                        return e.matmul(pb[db][:], ones_b[:], pt[:], start=(kt == 0), stop=(kt == 63))
                    A("pe", mm, r=[("Vh", kt // 4), pk, "ones_b"], w=[PB(ob), PB(db)])
                og = ostg[qb % 2]
                ok = ("ostg", qb % 2)
                A("dve", lambda e, db=db: e.reciprocal(out=rec[:], in_=pb[db][:]), r=[PB(db)], w=["rec"])
                A("dve", lambda e, ob=ob, og=og: e.tensor_tensor(out=og[:], in0=pb[ob][:], in1=rec[:], op=ALU.mult), r=[PB(ob), "rec"], w=[ok])
                A("sp", lambda e, og=og, h=h, qb=qb: e.dma_start(out=catT_d[h * 128:(h + 1) * 128, qb * 512:(qb + 1) * 512], in_=og[:]),
                  r=[ok], w=[("catT_d", "att", h, qb)], dma=True)
        S.barrier()
        stop_here(2)

        cv = Carver(0)
        catTb = [cv.get([128, 16, 512], BF16) for _ in range(2)]
        wo = cv.get([128, 16, D], BF16)
        gA = cv.get([128, D], F32)
        bAf = cv.get([128, D], F32)
        bA = cv.get([128, D], BF16)
        xts = [(cv.get([128, D], F32), ("xt", i)) for i in range(2)]
        x0c = [cv.get([128, D], F32) for _ in range(3)]
        x1b = [cv.get([128, D], BF16) for _ in range(2)]
        x1Tst = cv.get([128, 16, 512], BF16)
        CATD = [k for k in S.bufs if isinstance(k, tuple) and k[0] == "catT_d"]
        w_out_r = w_out.rearrange("(k p) n -> p k n", p=128)
        for q4 in range(4):
            A("pool", lambda e, q4=q4: e.dma_start(out=wo[:, q4 * 4:(q4 + 1) * 4, :], in_=w_out_r[:, q4 * 4:(q4 + 1) * 4, :]), w=[("wo", q4)], dma=True)
        WO = [("wo", q4) for q4 in range(4)]
        A("sp", lambda e: e.dma_start(out=gA, in_=ln_in_g[0:1, :].partition_broadcast(128)), w=["gA"], dma=True)
        A("sp", lambda e: e.dma_start(out=bAf[0:1, :], in_=ln_in_b[0:1, :]), w=["bAf"], dma=True)
        A("dve", lambda e: e.tensor_scalar(out=gA, in0=gA, scalar1=ALPHA, scalar2=0.0, op0=ALU.mult, op1=ALU.add), r=["gA"], w=["gA"])
        A("dve", lambda e: e.tensor_scalar(out=bA[0:1, :], in0=bAf[0:1, :], scalar1=ALPHA, scalar2=0.0, op0=ALU.mult, op1=ALU.add), r=["bAf"], w=["bA"])

        def ln_stats(src, skeys, sl):
            def bns(e):
                for c in range(4):
                    i = e.bn_stats(out=stats[:, sl, c * 6:(c + 1) * 6], in_=src[:, c * 512:(c + 1) * 512])
                return i
            A("dve", bns, r=skeys, w=[("stats", sl)])
            A("dve", lambda e: e.bn_aggr(out=mv[:, sl, :], in_=stats[:, sl, :]), r=[("stats", sl)], w=[("mv", sl)])
            A("act", lambda e: e.activation(out=sd[:, sl:sl + 1], in_=mv[:, sl, 1:2], func=AF.Sqrt, bias=cst[:, 0:1], scale=1.0),
              r=[("mv", sl), "cst0"], w=[("sd", sl)])
            A("dve", lambda e: e.reciprocal(out=rstd[:, sl:sl + 1], in_=sd[:, sl:sl + 1]), r=[("sd", sl)], w=[("rstd", sl)])
            A("dve", lambda e: e.scalar_tensor_tensor(out=nmr[:, sl:sl + 1], in0=mv[:, sl, 0:1], scalar=-1.0, in1=rstd[:, sl:sl + 1],
                                                      op0=ALU.mult, op1=ALU.mult), r=[("mv", sl), ("rstd", sl)], w=[("nmr", sl)])

        def c_s1(i):
            xb, xk = xts[i % 2]
            sl = i % 2
            xc = x0c[i % 3]
            xck = ("x0c", i % 3)
            A("sp", lambda e, xb=xb, i=i: e.dma_start(out=xb, in_=x[i * 128:(i + 1) * 128, :]), w=[xk], dma=True)
            ln_stats(xb, [xk], sl)
            A("act", lambda e: e.activation(out=xc, in_=xb, func=AF.Identity, scale=rstd[:, sl:sl + 1], bias=nmr[:, sl:sl + 1]),
              r=[xk, ("rstd", sl), ("nmr", sl)], w=[xck])
            A("dve", lambda e: e.tensor_tensor(out=xc, in0=xc, in1=gA, op=ALU.mult), r=[xck, "gA"], w=[xck])

        def c_mix(i):
            catT = catTb[(i // 4) % 2]
            CATK = [("catTb", (i // 4) % 2)]
            if i % 4 == 0:
                A("sp", lambda e, catT=catT, i=i: e.dma_start(out=catT, in_=catT_d[:, (i // 4) * 512:(i // 4 + 1) * 512].rearrange("(c p) n -> p c n", p=128)),
                  r=CATD, w=CATK, dma=True)
            for cb in range(4):
                def mm(e, i=i, cb=cb, catT=catT):
                    e.matmul(pb[cb][:], ones_b[0:1, :], bA[0:1, cb * 512:(cb + 1) * 512], start=True, stop=False)
                    for k in range(16):
                        r_ = e.matmul(pb[cb][:], catT[:, k, (i % 4) * 128:(i % 4 + 1) * 128], wo[:, k, cb * 512:(cb + 1) * 512], start=False, stop=(k == 15))
                    return r_
                A("pe", mm, r=CATK + WO + ["bA", "ones_b"], w=[PB(cb)])

        c_s1(0)
        c_s1(1)
        c_mix(0)
        for i in range(16):
            xc = x0c[i % 3]
            xck = ("x0c", i % 3)
            xbf = x1b[i % 2]
            xbk = ("x1b", i % 2)
            sl = 2 + i % 2
            for cb in range(4):
                A("dve", lambda e, xc=xc, cb=cb: e.tensor_tensor(out=xc[:, cb * 512:(cb + 1) * 512], in0=xc[:, cb * 512:(cb + 1) * 512],
                                                                  in1=pb[cb][:], op=ALU.add), r=[xck, PB(cb)], w=[xck])
            if i + 1 < 16:
                c_mix(i + 1)
            ln_stats(xc, [xck], sl)
            A("act", lambda e, xc=xc, xbf=xbf, sl=sl: e.activation(out=xbf[:], in_=xc, func=AF.Identity, scale=rstd[:, sl:sl + 1], bias=nmr[:, sl:sl + 1]),
              r=[xck, ("rstd", sl), ("nmr", sl)], w=[xbk])
            A("act", lambda e, xc=xc, sl=sl: e.activation(out=xc, in_=xc, func=AF.Identity, scale=rstd[:, sl:sl + 1], bias=nmr[:, sl:sl + 1]),
              r=[xck, ("rstd", sl), ("nmr", sl)], w=[xck])
            A("sp", lambda e, xc=xc, i=i: e.dma_start(out=x1_d[i * 128:(i + 1) * 128, :], in_=xc), r=[xck], w=[("x1_d", i)], dma=True)
            for c4 in range(4):
                bank = 4 + c4
                pv = pb[bank][:].bitcast(BF16)[:, 0:512]

                def tr(e, c4=c4, pv=pv, xbf=xbf):
                    for cc in range(4):
                        c = c4 * 4 + cc
                        r_ = e.transpose(pv[:, cc * 128:(cc + 1) * 128], xbf[:, c * 128:(c + 1) * 128], ident_b[:])
                    return r_
                A("pe", tr, r=[xbk, "ident_b"], w=[PB(bank)])
                for cc in range(4):
                    c = c4 * 4 + cc
                    dstv = x1Tst[:, c, (i % 4) * 128:(i % 4 + 1) * 128]
                    srcv = pv[:, cc * 128:(cc + 1) * 128]
                    if c4 % 2 == 0:
                        A("act", lambda e, srcv=srcv, dstv=dstv, c=c: e.activation(out=dstv, in_=srcv, func=AF.Identity, scale=g1T[:, c:c + 1], bias=b1T[:, c:c + 1]),
                          r=[PB(bank), "g1T", "b1T"], w=[("x1Tst", c, i % 4)])
                    else:
                        A("dve", lambda e, srcv=srcv, dstv=dstv, c=c: e.tensor_scalar(out=dstv, in0=srcv, scalar1=g1T[:, c:c + 1], scalar2=b1T[:, c:c + 1],
                                                                                     op0=ALU.mult, op1=ALU.add),
                          r=[PB(bank), "g1T", "b1T"], w=[("x1Tst", c, i % 4)])
            if i % 4 == 3:
                tb = i // 4
                A("sp", lambda e, tb=tb: e.dma_start(out=x1T_d[:, tb * 512:(tb + 1) * 512].rearrange("(c p) n -> p c n", p=128), in_=x1Tst[:]),
                  r=[("x1Tst", c, t_) for c in range(16) for t_ in range(4)], w=[("x1T_d", tb)], dma=True)
            if i + 2 < 16:
                c_s1(i + 2)
        S.barrier()
        stop_here(3)

        cv = Carver(0)
        h1T = cv.get([128, 64, 512], BF16)
        x1Tb = [cv.get([128, 16, 512], BF16) for _ in range(1)]
        xres = [cv.get([128, D], F32) for _ in range(4)]
        gb2 = [cv.get([128, D], F32) for _ in range(4)]
        NW1 = 4
        w1 = [cv.get([128, 16, 256], BF16) for _ in range(NW1)]
        NW2 = 4
        w2 = [cv.get([128, 2, 1024], BF16) for _ in range(NW2)]
        rl = [cv.get([128, 512], F32) for _ in range(3)]
        for i, src in enumerate((g_ln1, b_ln1, g_ln2, b_ln2)):
            A("sp", lambda e, i=i, src=src: e.dma_start(out=gb2[i], in_=src[0:1, :].partition_broadcast(128)), w=[("gb2", i)], dma=True)
        for i in range(2):
            A("dve", lambda e, i=i: e.tensor_scalar(out=gb2[i], in0=gb2[i], scalar1=ALPHA, scalar2=0.0, op0=ALU.mult, op1=ALU.add),
              r=[("gb2", i)], w=[("gb2", i)])
        w1b_r = w1b_d.rearrange("(k p) f -> p k f", p=128)
        w2b_r = w2b_d.rearrange("(c p) d -> p c d", p=128)
        w1c = [0]
        w2c = [0]
        rlc = [0]
        for tb in range(4):
            xT = x1Tb[0]
            xTk = ("x1Tb", 0)
            A("sp", lambda e, xT=xT, tb=tb: e.dma_start(out=xT, in_=x1T_d[:, tb * 512:(tb + 1) * 512].rearrange("(c p) n -> p c n", p=128)),
              r=[("x1T_d", tb)], w=[xTk], dma=True)
            for t in range(4):
                A("sp", lambda e, t=t, tb=tb: e.dma_start(out=xres[t], in_=x1_d[(tb * 4 + t) * 128:(tb * 4 + t + 1) * 128, :]),
                  r=[("x1_d", tb * 4 + t)], w=[("xres", t)], dma=True)
                A("dve", lambda e, t=t: e.tensor_tensor(out=xres[t], in0=xres[t], in1=gb2[0], op=ALU.mult), r=[("xres", t), ("gb2", 0)], w=[("xres", t)])
                A("pool", lambda e, t=t: e.tensor_tensor(out=xres[t], in0=xres[t], in1=gb2[1], op=ALU.add), r=[("xres", t), ("gb2", 1)], w=[("xres", t)])
            for fg in range(32):
                wi = w1c[0] % NW1
                w1c[0] += 1
                A("sp", lambda e, wi=wi, fg=fg: e.dma_start(out=w1[wi], in_=w1b_r[:, :, fg * 256:(fg + 1) * 256]), r=W1B, w=[("w1", wi)], dma=True)
                for fc in range(2):
                    f = fg * 2 + fc
                    bk = f % 4

                    def mm(e, wi=wi, fc=fc, bk=bk, xT=xT):
                        for k in range(16):
                            r_ = e.matmul(pb[bk][:], w1[wi][:, k, fc * 128:(fc + 1) * 128], xT[:, k, :], start=(k == 0), stop=(k == 15))
                        return r_
                    A("pe", mm, r=[("w1", wi), xTk], w=[PB(bk)])
                    ri = rlc[0] % 3
                    rlc[0] += 1
                    A("act", lambda e, bk=bk, ri=ri: e.activation(out=rl[ri][:], in_=pb[bk][:], func=AF.Relu, bias=cst[:, 2:3], scale=1.0),
                      r=[PB(bk), "cst2"], w=[("rl", ri)])
                    A("dve", lambda e, ri=ri, f=f: e.tensor_tensor(out=h1T[:, f, :], in0=rl[ri][:], in1=rl[ri][:], op=ALU.mult),
                      r=[("rl", ri)], w=[("h1T", f)])
            for ps_ in range(2):
                for fg in range(32):
                    wi = w2c[0] % NW2
                    w2c[0] += 1
                    A("sp", lambda e, wi=wi, fg=fg, ps_=ps_: e.dma_start(out=w2[wi], in_=w2b_r[:, fg * 2:(fg + 1) * 2, ps_ * 1024:(ps_ + 1) * 1024]),
                      r=W2B, w=[("w2", wi)], dma=True)

                    def mm(e, wi=wi, fg=fg):
                        for fc in range(2):
                            f = fg * 2 + fc
                            for t in range(4):
                                for cb in range(2):
                                    r_ = e.matmul(pb[t * 2 + cb][:], h1T[:, f, t * 128:(t + 1) * 128], w2[wi][:, fc, cb * 512:(cb + 1) * 512],
                                                  start=(f == 0), stop=(f == 63))
                        return r_
                    A("pe", mm, r=[("w2", wi), ("h1T", fg * 2), ("h1T", fg * 2 + 1)], w=[PB(b_) for b_ in range(8)])
                for t in range(4):
                    for cb in range(2):
                        col = ps_ * 1024 + cb * 512
                        A("dve", lambda e, t=t, cb=cb, col=col: e.tensor_tensor(out=xres[t][:, col:col + 512], in0=xres[t][:, col:col + 512],
                                                                                in1=pb[t * 2 + cb][:], op=ALU.add),
                          r=[("xres", t), PB(t * 2 + cb)], w=[("xres", t)])
            for t in range(4):
                ln_stats(xres[t], [("xres", t)], t)
                A("act", lambda e, t=t: e.activation(out=xres[t], in_=xres[t], func=AF.Identity, scale=rstd[:, t:t + 1], bias=nmr[:, t:t + 1]),
                  r=[("xres", t), ("rstd", t), ("nmr", t)], w=[("xres", t)])
                A("dve", lambda e, t=t: e.tensor_tensor(out=xres[t], in0=xres[t], in1=gb2[2], op=ALU.mult), r=[("xres", t), ("gb2", 2)], w=[("xres", t)])
                A("pool", lambda e, t=t: e.tensor_tensor(out=xres[t], in0=xres[t], in1=gb2[3], op=ALU.add), r=[("xres", t), ("gb2", 3)], w=[("xres", t)])
                row = (tb * 4 + t) * 128
                A("sp", lambda e, t=t, row=row: e.dma_start(out=out[row:row + 128, :], in_=xres[t]), r=[("xres", t)], w=[("out", tb * 4 + t)], dma=True)
        final_keys = [("out", i) for i in range(16)] if S.enabled else list(dbg_keys)
        S.enabled = True
        S.emit(nc, st, final_keys=final_keys)
    return nc


_ROPE_C = None


def _consts():
    p = np.arange(128)
    inv = (10000.0 ** (-(p % 32).astype(np.float64) * (2.0 / 64))) / (2.0 * math.pi)
    sgn = np.where((p % 64) < 32, -1.0, 1.0)
    rc = np.stack([inv, -2.0 * math.pi * sgn], axis=1).astype(np.float32)
    return rc, np.eye(128, dtype=np.float32)


def _make_in_maps(x, positions, ln_in_g, ln_in_b, w_in, g_cq, w_uq, g_ckv, w_uk, w_uv, conv_w, conv_b,
           g_conv_ln, b_conv_ln, w_out, g_ln1, b_ln1, w_ff1, w_ff2, g_ln2, b_ln2):
    x = np.asarray(x)
    positions = np.asarray(positions)
    rc, ident = _consts()
    f = lambda a: np.ascontiguousarray(np.asarray(a, dtype=np.float32))
    common = {
        "ident": ident, "ropec": rc,
        "ln_in_g": f(ln_in_g).reshape(1, -1), "ln_in_b": f(ln_in_b).reshape(1, -1),
        "w_in": f(w_in[0]), "g_cq": f(g_cq[0]).reshape(1, -1), "w_uq": f(w_uq[0]),
        "g_ckv": f(g_ckv[0]).reshape(1, -1), "w_uk": f(w_uk[0]), "w_uv": f(w_uv[0]),
        "conv_w": f(conv_w[0]), "conv_b": f(conv_b[0]).reshape(1, -1),
        "g_conv_ln": f(g_conv_ln[0]).reshape(1, -1), "b_conv_ln": f(b_conv_ln[0]).reshape(1, -1),
        "w_out": f(w_out[0]), "g_ln1": f(g_ln1[0]).reshape(1, -1), "b_ln1": f(b_ln1[0]).reshape(1, -1),
        "w_ff1": f(w_ff1[0]), "w_ff2": f(w_ff2[0]),
        "g_ln2": f(g_ln2[0]).reshape(1, -1), "b_ln2": f(b_ln2[0]).reshape(1, -1),
    }
    in_maps = []
    for c in range(NCORES):
        b, r = c // 4, c % 4
        t0 = r * NTOK
        m = dict(common)
        m["x"] = np.ascontiguousarray(np.roll(x[b], -t0, axis=0), dtype=np.float32)
        m["pos"] = np.ascontiguousarray(np.roll(positions[b], -t0).reshape(1, -1).astype(np.int32))
        hm = np.zeros((128, 2), np.float32)
        hm[:, 0] = 1.0 if r > 0 else 0.0
        hm[:, 1] = 1.0 if r < 3 else 0.0
        m["hmask"] = hm
        in_maps.append(m)
    return in_maps


def kernel(**inputs):
    in_maps = _make_in_maps(**inputs)
    nc = build_program()
    res = run_bass_kernel_spmd(nc, in_maps, core_ids=list(range(NCORES)))
    outp = np.empty((2, SEQ, D), np.float32)
    for c in range(NCORES):
        b, r = c // 4, c % 4
        outp[b, r * NTOK:(r + 1) * NTOK] = res.results[c]["out"]
    return outp
```

```python
import math
import contextlib
import numpy as np
import concourse.bass as bass
import concourse.mybir as mybir
from concourse.bass_utils import run_bass_kernel_spmd

F32 = mybir.dt.float32
BF16 = mybir.dt.bfloat16
I32 = mybir.dt.int32
AF = mybir.ActivationFunctionType
ALU = mybir.AluOpType

NCORES = 8
D = 2048
SEQ = 8192
NTOK = 2048
ALPHA = 2.0 ** 0.25
SCALE = 192 ** -0.5
LN_EPS = 1e-5
RMS_EPS = 1e-6
ARENA_BYTES = 204 * 1024

DEBUG = {}


class _Buf:
    __slots__ = ("w", "r")

    def __init__(self):
        self.w = None
        self.r = []


class _Op:
    __slots__ = ("eng", "fn", "deps", "dma", "tok", "signal", "idx", "cnt")


class Sched:
    ENGS = ("pe", "act", "dve", "pool", "sp")
    NDMA = 28
    NSP = 20

    def __init__(self):
        self.ops = {e: [] for e in self.ENGS}
        self.bufs = {}
        self.ndma = {"sp": 0, "pool": 0, "act": 0}
        self.dma_cnt = [0] * self.NDMA
        self.dma_last = [None] * self.NDMA
        self.pending_barrier = {e: None for e in self.ENGS}
        self.enabled = True

    def buf(self, key):
        b = self.bufs.get(key)
        if b is None:
            b = self.bufs[key] = _Buf()
        return b

    def barrier(self):
        if not self.enabled:
            return
        toks = []
        for e in self.ENGS:
            for op in reversed(self.ops[e]):
                if not op.dma:
                    toks.append(op.tok)
                    break
        for t in self.dma_last:
            if t is not None:
                toks.append(t)
        for e in self.ENGS:
            prev = self.pending_barrier[e]
            self.pending_barrier[e] = toks if prev is None else prev + toks

    def add(self, eng, fn, r=(), w=(), dma=False):
        if not self.enabled:
            return None
        op = _Op()
        op.eng = eng
        op.fn = fn
        op.dma = dma
        op.signal = False
        op.cnt = 0
        op.idx = len(self.ops[eng])
        deps = []
        if self.pending_barrier[eng] is not None:
            deps.extend(self.pending_barrier[eng])
            self.pending_barrier[eng] = None
        for k in r:
            b = self.buf(k)
            if b.w is not None:
                deps.append(b.w)
        for k in w:
            b = self.buf(k)
            if b.w is not None:
                deps.append(b.w)
            deps.extend(b.r)
        if dma:
            if eng == "sp":
                j = self.ndma["sp"] % self.NSP
            else:
                j = self.NSP + self.ndma[eng] % (self.NDMA - self.NSP)
            self.ndma[eng] += 1
            if self.dma_last[j] is not None:
                deps.append(self.dma_last[j])
            self.dma_cnt[j] += 1
            op.tok = ("d%d" % j, self.dma_cnt[j] * 16, op)
            self.dma_last[j] = op.tok
        else:
            op.tok = (eng, op.idx, op)
        seen = {}
        for t in deps:
            if t[2] is op:
                continue
            if (not dma) and eng == "pe" and t[0] == "pe":
                continue
            key = t[0]
            if key not in seen or seen[key][1] < t[1]:
                seen[key] = t
        op.deps = list(seen.values())
        for t in op.deps:
            t[2].signal = True
        for k in r:
            self.buf(k).r.append(op.tok)
        for k in w:
            b = self.buf(k)
            b.w = op.tok
            b.r = []
        self.ops[eng].append(op)
        return op

    def emit(self, nc, stack, final_keys=()):
        sems = {}
        for e in ("pe", "act", "dve", "pool"):
            sems[e] = stack.enter_context(nc.semaphore("s_" + e))
        for j in range(self.NDMA):
            sems["d%d" % j] = stack.enter_context(nc.semaphore("s_d%d" % j))
        final = [self.buf(k).w for k in final_keys]
        for t in final:
            t[2].signal = True
        for e in self.ENGS:
            c = 0
            for op in self.ops[e]:
                if op.dma:
                    continue
                if op.signal:
                    c += 1
                op.cnt = c

        def tokval(t):
            o = t[2]
            return t[1] if o.dma else o.cnt

        block = stack.enter_context(nc.Block())
        engobj = {"pe": "tensor", "act": "scalar", "dve": "vector", "pool": "gpsimd", "sp": "sync"}
        for e in self.ENGS:
            ops = self.ops[e]
            fw = final if e == "sp" else ()

            def body(eng, ops=ops, e=e, fw=fw):
                waited = {}
                for op in ops:
                    for t in op.deps:
                        v = tokval(t)
                        s = t[0]
                        if waited.get(s, 0) >= v:
                            continue
                        waited[s] = v
                        eng.wait_ge(sems[s], v)
                    inst = op.fn(eng)
                    if op.dma:
                        inst.then_inc(sems[op.tok[0]], 16)
                    elif op.signal:
                        inst.then_inc(sems[e], 1)
                for t in fw:
                    v = tokval(t)
                    s = t[0]
                    if waited.get(s, 0) >= v:
                        continue
                    waited[s] = v
                    eng.wait_ge(sems[s], v)

            getattr(block, engobj[e])(body)


def build_program(stop_after=None):
    nc = bass.Bass("TRN2", target_bir_lowering=False)
    S = Sched()
    A = S.add

    def din(name, shape, dt=F32):
        return nc.dram_tensor(name, shape, dt, kind="ExternalInput").ap()

    x = din("x", [SEQ, D])
    pos = din("pos", [1, SEQ], I32)
    hmask_d = din("hmask", [128, 2])
    ident_d = din("ident", [128, 128])
    rc_d = din("ropec", [128, 2])
    ln_in_g = din("ln_in_g", [1, D])
    ln_in_b = din("ln_in_b", [1, D])
    w_in = din("w_in", [D, 3136])
    g_cq = din("g_cq", [1, 512])
    w_uq = din("w_uq", [512, 1536])
    g_ckv = din("g_ckv", [1, 512])
    w_uk = din("w_uk", [512, 1024])
    w_uv = din("w_uv", [512, 1024])
    conv_w = din("conv_w", [31, 1024])
    conv_b = din("conv_b", [1, 1024])
    g_cln = din("g_conv_ln", [1, 1024])
    b_cln = din("b_conv_ln", [1, 1024])
    w_out = din("w_out", [D, D])
    g_ln1 = din("g_ln1", [1, D])
    b_ln1 = din("b_ln1", [1, D])
    w_ff1 = din("w_ff1", [D, 8192])
    w_ff2 = din("w_ff2", [8192, D])
    g_ln2 = din("g_ln2", [1, D])
    b_ln2 = din("b_ln2", [1, D])
    out = nc.dram_tensor("out", [NTOK, D], F32, kind="ExternalOutput").ap()
    catT_d = nc.dram_tensor("catT_s", [D, NTOK], BF16, kind="Internal").ap()
    x1_d = nc.dram_tensor("x1_s", [NTOK, D], F32, kind="Internal").ap()
    x1T_d = nc.dram_tensor("x1T_s", [D, NTOK], BF16, kind="Internal").ap()
    w1b_d = nc.dram_tensor("w1b_s", [D, 8192], BF16, kind="Internal").ap()
    w2b_d = nc.dram_tensor("w2b_s", [8192, D], BF16, kind="Internal").ap()
    dbg = {}
    if stop_after is not None:
        dbg["cqnT"] = nc.dram_tensor("dbg_cqnT", [128, 4 * NTOK], BF16, kind="ExternalOutput").ap()
        dbg["ckvnT"] = nc.dram_tensor("dbg_ckvnT", [128, 4 * SEQ], BF16, kind="ExternalOutput").ap()
        dbg["krT"] = nc.dram_tensor("dbg_krT", [128, SEQ], BF16, kind="ExternalOutput").ap()
        dbg["Tq"] = nc.dram_tensor("dbg_Tq", [128, NTOK], F32, kind="ExternalOutput").ap()
        dbg["catT"] = nc.dram_tensor("dbg_catT", [D, NTOK], BF16, kind="ExternalOutput").ap()
        dbg["x1"] = nc.dram_tensor("dbg_x1", [NTOK, D], F32, kind="ExternalOutput").ap()
        dbg["gen"] = nc.dram_tensor("dbg_gen", [128, 4096], BF16, kind="ExternalOutput").ap()
    dbg_keys = []

    with contextlib.ExitStack() as st:
        st.enter_context(nc.allow_non_contiguous_dma(reason="small param vectors"))

        def sb(name, shape, dt):
            return st.enter_context(nc.sbuf_tensor(name, shape, dt))

        arena = sb("arena", [128, ARENA_BYTES // 2], BF16)
        ident_f = sb("ident_f", [128, 128], F32)
        ident_b = sb("ident_b", [128, 128], BF16)
        ones_b = sb("ones_b", [128, 128], BF16)
        cst = sb("cst", [128, 8], F32)
        hmask = sb("hmask_t", [128, 2], F32)
        rc = sb("rc_t", [128, 2], F32)
        ginT = sb("ginT", [128, 16], F32)
        binT = sb("binT", [128, 16], F32)
        gcqT = sb("gcqT", [128, 4], F32)
        g1T = sb("g1T", [128, 16], F32)
        b1T = sb("b1T", [128, 16], F32)
        gckvT = sb("gckvT", [128, 4], F32)
        cbT = sb("cbT", [128, 8], F32)
        gclT = sb("gclT", [128, 8], F32)
        bclT = sb("bclT", [128, 8], F32)
        stats = sb("stats", [128, 4, 24], F32)
        mv = sb("mv", [128, 4, 2], F32)
        sd = sb("sd", [128, 4], F32)
        rstd = sb("rstd", [128, 4], F32)
        nmr = sb("nmr", [128, 4], F32)
        pb = [st.enter_context(nc.psum_tensor("pb%d" % i, [128, 512], F32)) for i in range(8)]

        def PB(i):
            return ("pb", i)

        def view(off, shape, dt):
            esz = 2 if dt == BF16 else 4
            n = 1
            for s_ in shape[1:]:
                n *= s_
            assert off % 4 == 0 and off + n * esz <= ARENA_BYTES, (off, n * esz)
            ap = arena[:, off // 2: off // 2 + n * esz // 2]
            if dt != BF16:
                ap = ap.bitcast(dt)
            if len(shape) == 3:
                ap = ap.rearrange("p (a b) -> p a b", a=shape[1])
            elif len(shape) == 4:
                ap = ap.rearrange("p (a b c) -> p a b c", a=shape[1], b=shape[2])
            return ap

        class Carver:
            def __init__(self, base=0):
                self.off = base

            def get(self, shape, dt):
                esz = 2 if dt == BF16 else 4
                n = 1
                for s_ in shape[1:]:
                    n *= s_
                v = view(self.off, shape, dt)
                self.off += (n * esz + 3) // 4 * 4
                return v

        A("sp", lambda e: e.dma_start(out=ident_f[:], in_=ident_d[:, :]), w=["ident_f"], dma=True)
        A("sp", lambda e: e.dma_start(out=hmask[:], in_=hmask_d[:, :]), w=["hmask"], dma=True)
        A("sp", lambda e: e.dma_start(out=rc[:], in_=rc_d[:, :]), w=["rc"], dma=True)

        def ldT(dst, src, nch, key):
            A("sp", lambda e: e.dma_start(out=dst[:], in_=src.rearrange("o (c p) -> p (o c)", p=128)),
              w=[key], dma=True)

        ldT(ginT, ln_in_g, 16, "ginT")
        ldT(binT, ln_in_b, 16, "binT")
        ldT(gcqT, g_cq, 4, "gcqT")
        ldT(g1T, g_ln1, 16, "g1T")
        ldT(b1T, b_ln1, 16, "b1T")
        ldT(gckvT, g_ckv, 4, "gckvT")
        ldT(cbT, conv_b, 8, "cbT")
        ldT(gclT, g_cln, 8, "gclT")
        ldT(bclT, b_cln, 8, "bclT")
        A("dve", lambda e: e.tensor_copy(ident_b[:], ident_f[:]), r=["ident_f"], w=["ident_b"])
        A("pool", lambda e: e.memset(ones_b[:], 1.0), w=["ones_b"])
        A("pool", lambda e: e.memset(cst[:, 0:1], LN_EPS), w=["cst0"])
        A("pool", lambda e: e.memset(cst[:, 1:2], RMS_EPS), w=["cst1"])
        A("pool", lambda e: e.memset(cst[:, 2:3], 0.0), w=["cst2"])
        CST = ["cst0", "cst1", "cst2"]

        if stop_after == "C0":
            A("sp", lambda e: e.dma_start(out=dbg["gen"][:, 0:128], in_=ident_b[:]), r=["ident_b", "ginT", "binT", "gcqT", "gckvT", "cbT", "gclT", "bclT", "hmask", "rc", "ones_b"] + CST, w=["dbg_gen"], dma=True)
            dbg_keys.append("dbg_gen")
            S.enabled = False
        gctr = [0]

        def ln_part(row0, xts, xn, xnk, nt=4):
            g = gctr[0]
            gctr[0] += 1
            nb = len(xts)
            used = [xts[(g * nt + t) % nb] for t in range(nt)]

            def load(t):
                xb, xk = used[t]
                A("sp", lambda e, xb=xb, t=t: e.dma_start(out=xb, in_=x[row0 + t * 128: row0 + (t + 1) * 128, :]),
                  w=[xk], dma=True)
            for t in range(min(nb, nt)):
                load(t)
            for t in range(nt):
                xb, xk = used[t]

                def bns(e, xb=xb, t=t):
                    for c in range(4):
                        i = e.bn_stats(out=stats[:, t, c * 6:(c + 1) * 6], in_=xb[:, c * 512:(c + 1) * 512])
                    return i
                A("dve", bns, r=[xk], w=[("stats", t)])
                A("dve", lambda e, t=t: e.bn_aggr(out=mv[:, t, :], in_=stats[:, t, :]), r=[("stats", t)], w=[("mv", t)])
                A("act", lambda e, t=t: e.activation(out=sd[:, t:t + 1], in_=mv[:, t, 1:2], func=AF.Sqrt, bias=cst[:, 0:1], scale=1.0),
                  r=[("mv", t), "cst0"], w=[("sd", t)])
                A("dve", lambda e, t=t: e.reciprocal(out=rstd[:, t:t + 1], in_=sd[:, t:t + 1]), r=[("sd", t)], w=[("rstd", t)])
                A("dve", lambda e, t=t: e.scalar_tensor_tensor(out=nmr[:, t:t + 1], in0=mv[:, t, 0:1], scalar=-1.0, in1=rstd[:, t:t + 1],
                                                               op0=ALU.mult, op1=ALU.mult), r=[("mv", t), ("rstd", t)], w=[("nmr", t)])
                A("act", lambda e, xb=xb, t=t: e.activation(out=xn[:, t, :], in_=xb, func=AF.Identity,
                                                           scale=rstd[:, t:t + 1], bias=nmr[:, t:t + 1]),
                  r=[xk, ("rstd", t), ("nmr", t)], w=[(xnk, t)])
                if t + nb < nt:
                    load(t + nb)

        def tr_part(xn, xnk, dstT, dkey, trb, nt=4):
            for c in range(16):
                bank = trb[c % len(trb)]
                pv = pb[bank][:].bitcast(BF16)[:, 0:nt * 128]

                def tr(e, c=c, pv=pv):
                    for t in range(nt):
                        i = e.transpose(pv[:, t * 128:(t + 1) * 128], xn[:, t, c * 128:(c + 1) * 128], ident_b[:])
                    return i
                A("pe", tr, r=[(xnk, t) for t in range(nt)] + ["ident_b"], w=[PB(bank)])
                if c % 2 == 0:
                    A("act", lambda e, c=c, pv=pv: e.activation(out=dstT[:, c, 0:nt * 128], in_=pv, func=AF.Identity,
                                                               scale=ginT[:, c:c + 1], bias=binT[:, c:c + 1]),
                      r=[PB(bank), "ginT", "binT"], w=[(dkey, c)])
                else:
                    A("dve", lambda e, c=c, pv=pv: e.tensor_scalar(out=dstT[:, c, 0:nt * 128], in0=pv, scalar1=ginT[:, c:c + 1],
                                                                  scalar2=binT[:, c:c + 1], op0=ALU.mult, op1=ALU.add),
                      r=[PB(bank), "ginT", "binT"], w=[(dkey, c)])

        def rms_block(src_banks, gT, gkey, dst_fn, dkeys, tmp, tkey, statbank, n=512):
            sq, sdv, rs = tmp["sq"], tmp["sdv"], tmp["rs"]
            for m in range(4):
                A("act", lambda e, m=m: e.activation(out=sq[:, m, 0:n], in_=pb[src_banks[m]][:, 0:n], func=AF.Square),
                  r=[PB(src_banks[m])], w=[(tkey, "sq", m)])

            def mm(e):
                for m in range(4):
                    i = e.matmul(pb[statbank][:, 0:n], ones_b[:], sq[:, m, 0:n], start=(m == 0), stop=(m == 3))
                return i
            A("pe", mm, r=[(tkey, "sq", m) for m in range(4)] + ["ones_b"], w=[PB(statbank)])
            A("act", lambda e: e.activation(out=sdv[:, 0:n], in_=pb[statbank][:, 0:n], func=AF.Sqrt, bias=cst[:, 1:2],
                                            scale=1.0 / 512.0), r=[PB(statbank), "cst1"], w=[(tkey, "sdv")])
            A("dve", lambda e: e.reciprocal(out=rs[:, 0:n], in_=sdv[:, 0:n]), r=[(tkey, "sdv")], w=[(tkey, "rs")])
            for m in range(4):
                A("dve", lambda e, m=m: e.scalar_tensor_tensor(out=dst_fn(m), in0=pb[src_banks[m]][:, 0:n], scalar=gT[:, m:m + 1],
                                                              in1=rs[:, 0:n], op0=ALU.mult, op1=ALU.mult),
                  r=[PB(src_banks[m]), (tkey, "rs"), gkey], w=[dkeys[m]])

        OFF_CQN = 0
        cqnT = view(OFF_CQN, [128, 4, NTOK], BF16)
        P0 = 16 * 1024
        cv = Carver(P0)
        x0T = cv.get([128, 16, NTOK], BF16)
        x0Th = cv.get([128, 16, 32], BF16)
        uT = cv.get([128, 8, 2080], BF16)
        R1 = cv.off
        xts = [(cv.get([128, D], F32), ("xt", i)) for i in range(4)]
        xn = cv.get([128, 4, D], BF16)
        xn2 = cv.get([128, 4, D], BF16)
        xnl = [xn, xn2]
        ln_part(0, xts, xnl[0], ("xn", 0))
        for g in range(4):
            if g + 1 < 4:
                ln_part((g + 1) * 512, xts, xnl[(g + 1) % 2], ("xn", (g + 1) % 2))
            tr_part(xnl[g % 2], ("xn", g % 2), x0T[:, :, g * 512:(g + 1) * 512], ("x0T", g), (4, 5, 6, 7))
        if stop_after == "A0g":
            A("sp", lambda e: e.dma_start(out=dbg["gen"][:, :], in_=x0T[:, 0:2, :].rearrange("p a b -> p (a b)")),
              r=[(("x0T", g), c) for g in range(4) for c in range(2)], w=["dbg_gen"], dma=True)
            dbg_keys.append("dbg_gen")
            S.enabled = False
        xh, xhk = xts[0]
        xnh = xn
        A("sp", lambda e: e.dma_start(out=xh[0:16, :], in_=x[SEQ - 16:SEQ, :]), w=[xhk], dma=True)
        A("sp", lambda e: e.dma_start(out=xh[16:32, :], in_=x[NTOK:NTOK + 16, :]), w=[xhk + ("b",)], r=[xhk], dma=True)
        def bnsh(e):
            for c in range(4):
                i = e.bn_stats(out=stats[0:32, 0, c * 6:(c + 1) * 6], in_=xh[0:32, c * 512:(c + 1) * 512])
            return i
        A("dve", bnsh, r=[xhk, xhk + ("b",)], w=[("stats", 0)])
        A("dve", lambda e: e.bn_aggr(out=mv[0:32, 0, :], in_=stats[0:32, 0, :]), r=[("stats", 0)], w=[("mv", 0)])
        A("act", lambda e: e.activation(out=sd[0:32, 0:1], in_=mv[0:32, 0:1, 1], func=AF.Sqrt, bias=cst[0:32, 0:1], scale=1.0),
          r=[("mv", 0), "cst0"], w=[("sd", 0)])
        A("dve", lambda e: e.reciprocal(out=rstd[0:32, 0:1], in_=sd[0:32, 0:1]), r=[("sd", 0)], w=[("rstd", 0)])
        A("dve", lambda e: e.scalar_tensor_tensor(out=nmr[0:32, 0:1], in0=mv[0:32, 0:1, 0], scalar=-1.0, in1=rstd[0:32, 0:1],
                                                  op0=ALU.mult, op1=ALU.mult), r=[("mv", 0), ("rstd", 0)], w=[("nmr", 0)])
        A("act", lambda e: e.activation(out=xnh[0:32, 0, :], in_=xh[0:32, :], func=AF.Identity, scale=rstd[0:32, 0:1],
                                        bias=nmr[0:32, 0:1]), r=[xhk, xhk + ("b",), ("rstd", 0), ("nmr", 0)], w=[(("xn", 0), 0)])
        for c in range(16):
            bank = (4, 5, 6, 7)[c % 4]
            pv = pb[bank][:].bitcast(BF16)[:, 0:32]
            A("pe", lambda e, c=c, pv=pv: e.transpose(pv, xnh[0:32, 0, c * 128:(c + 1) * 128], ident_b[0:32, 0:32]),
              r=[(("xn", 0), 0), "ident_b"], w=[PB(bank)])
            A("dve", lambda e, c=c, pv=pv: e.tensor_scalar(out=x0Th[:, c, :], in0=pv, scalar1=ginT[:, c:c + 1],
                                                          scalar2=binT[:, c:c + 1], op0=ALU.mult, op1=ALU.add),
              r=[PB(bank), "ginT", "binT"], w=[("x0Th", c)])
        if stop_after == "A0a":
            A("sp", lambda e: e.dma_start(out=dbg["gen"][:, :], in_=x0T[:, 0:2, :].rearrange("p a b -> p (a b)")),
              r=[(("x0T", g), c) for g in range(4) for c in range(2)] + [("x0Th", c) for c in range(16)], w=["dbg_gen"], dma=True)
            dbg_keys.append("dbg_gen")
            S.enabled = False
        S.barrier()
        cv = Carver(R1)
        wcq = cv.get([128, 16, 512], BF16)
        wch = [cv.get([128, 16, 256], BF16) for _ in range(2)]
        tmpA = {"sq": cv.get([128, 4, 512], BF16),
                "sdv": cv.get([128, 512], F32), "rs": cv.get([128, 512], F32)}
        sig = [cv.get([128, 512], F32) for _ in range(2)]
        uh = cv.get([128, 32], F32)
        X0K = lambda g: [(("x0T", g), c) for c in range(16)]
        w_in_r = w_in.rearrange("(k p) n -> p k n", p=128)
        A("pool", lambda e: e.dma_start(out=wcq, in_=w_in_r[:, :, 0:512]), w=["wcq"], dma=True)
        for tb in range(4):
            banks = [0, 1, 2, 3]
            for m in range(4):
                def mm(e, m=m, tb=tb):
                    for k in range(16):
                        i = e.matmul(pb[m][:], wcq[:, k, m * 128:(m + 1) * 128], x0T[:, k, tb * 512:(tb + 1) * 512],
                                     start=(k == 0), stop=(k == 15))
                    return i
                A("pe", mm, r=["wcq"] + X0K(tb), w=[PB(m)])
            rms_block(banks, gcqT, "gcqT", lambda m, tb=tb: cqnT[:, m, tb * 512:(tb + 1) * 512],
                      [("cqnT", m, tb) for m in range(4)], tmpA, "tA", 4)
        for ch in range(8):
            wb = wch[ch % 2]
            wk = ("wch", ch % 2)
            A("pool", lambda e, wb=wb, ch=ch: e.dma_start(out=wb[:, :, 0:128], in_=w_in_r[:, :, 1088 + ch * 128:1088 + (ch + 1) * 128]),
              w=[wk], dma=True)
            A("pool", lambda e, wb=wb, ch=ch: e.dma_start(out=wb[:, :, 128:256], in_=w_in_r[:, :, 2112 + ch * 128:2112 + (ch + 1) * 128]),
              w=[wk + ("g",)], dma=True)
            for tb in range(5):
                n = 512 if tb < 4 else 32
                ba, bg = (0, 1) if tb % 2 == 0 else (2, 3)
                rhs_fn = (lambda k, tb=tb: x0T[:, k, tb * 512:(tb + 1) * 512]) if tb < 4 else (lambda k: x0Th[:, k, :])
                rk = X0K(tb) if tb < 4 else [("x0Th", c) for c in range(16)]

                def mm(e, wb=wb, ba=ba, bg=bg, n=n, rhs_fn=rhs_fn):
                    for k in range(16):
                        e.matmul(pb[ba][:, 0:n], wb[:, k, 0:128], rhs_fn(k), start=(k == 0), stop=(k == 15))
                    for k in range(16):
                        i = e.matmul(pb[bg][:, 0:n], wb[:, k, 128:256], rhs_fn(k), start=(k == 0), stop=(k == 15))
                    return i
                A("pe", mm, r=[wk, wk + ("g",)] + rk, w=[PB(ba), PB(bg)])
                sg = sig[tb % 2]
                sk = ("sig", tb % 2)
                A("act", lambda e, sg=sg, bg=bg, n=n: e.activation(out=sg[:, 0:n], in_=pb[bg][:, 0:n], func=AF.Sigmoid),
                  r=[PB(bg)], w=[sk])
                if tb < 4:
                    A("dve", lambda e, sg=sg, ba=ba, ch=ch, tb=tb: e.tensor_tensor(
                        out=uT[:, ch, 16 + tb * 512:16 + (tb + 1) * 512], in0=pb[ba][:], in1=sg[:], op=ALU.mult),
                      r=[PB(ba), sk], w=[("uT", ch, tb)])
                else:
                    A("dve", lambda e, sg=sg, ba=ba: e.tensor_tensor(out=uh[:], in0=pb[ba][:, 0:32], in1=sg[:, 0:32], op=ALU.mult),
                      r=[PB(ba), sk], w=["uh"])
                    A("dve", lambda e, ch=ch: e.tensor_scalar(out=uT[:, ch, 0:16], in0=uh[:, 0:16], scalar1=hmask[:, 0:1], scalar2=0.0,
                                                             op0=ALU.mult), r=["uh", "hmask"], w=[("uT", ch, "lo")])
                    A("dve", lambda e, ch=ch: e.tensor_scalar(out=uT[:, ch, 2064:2080], in0=uh[:, 16:32], scalar1=hmask[:, 1:2], scalar2=0.0,
                                                             op0=ALU.mult), r=["uh", "hmask"], w=[("uT", ch, "hi")])
        if stop_after == "A0b":
            A("sp", lambda e: e.dma_start(out=dbg["gen"][:, 0:2080], in_=uT[:, 0, :]),
              r=[("uT", 0, q) for q in (0, 1, 2, 3, "lo", "hi")] + [("cqnT", m, tb) for m in range(4) for tb in range(4)], w=["dbg_gen"], dma=True)
            dbg_keys.append("dbg_gen")
            S.enabled = False
        S.barrier()
        cv = Carver(P0)
        diag = cv.get([128, 8, 31, 128], BF16)
        assert cv.off <= P0 + 65 * 1024
        cv = Carver(R1)
        cw_sb = cv.get([128, 1024], F32)
        cwT = cv.get([128, 8, 31], F32)
        cw_bf = cv.get([128, 1024], BF16)
        ycv = cv.get([128, 8, 512], F32)
        ybf = cv.get([128, 8, 512], BF16)
        ysq = cv.get([128, 8, 512], BF16)
        mean = cv.get([128, 512], F32)
        ex2 = cv.get([128, 512], F32)
        rsc = cv.get([128, 512], F32)
        costg = [cv.get([128, 8, 512], BF16) for _ in range(2)]
        A("sp", lambda e: e.dma_start(out=cw_sb[0:31, :], in_=conv_w[:, :]), w=["cw_sb"], dma=True)
        A("dve", lambda e: e.tensor_copy(cw_bf[0:31, :], cw_sb[0:31, :]), r=["cw_sb"], w=["cw_bf"])
        pv0 = pb[0][:].bitcast(BF16)
        for ch in range(8):
            A("pe", lambda e, ch=ch: e.transpose(pv0[:, ch * 32:ch * 32 + 31], cw_bf[0:31, ch * 128:(ch + 1) * 128], ident_b[0:31, 0:31]),
              r=["cw_bf", "ident_b"], w=[PB(0)])
        A("dve", lambda e: e.tensor_copy(cwT[:], pv0[:, 0:256].rearrange("p (c k) -> p c k", k=32)[:, :, 0:31]),
          r=[PB(0)], w=["cwT"])
        for ch in range(8):
            A("dve", lambda e, ch=ch: e.tensor_tensor(out=diag[:, ch, :, :], in0=ident_b[:].unsqueeze(1).to_broadcast([128, 31, 128]),
                                                     in1=cwT[:, ch, :].unsqueeze(2).to_broadcast([128, 31, 128]), op=ALU.mult),
              r=["ident_b", "cwT"], w=[("diag", ch)])
        for tb in range(4):
            for ch in range(8):
                bk = 1 + (ch % 4)

                def mm(e, ch=ch, tb=tb, bk=bk):
                    for k in range(31):
                        i = e.matmul(pb[bk][:], diag[:, ch, k, :], uT[:, ch, tb * 512 + k + 1: tb * 512 + k + 1 + 512],
                                     start=(k == 0), stop=(k == 30))
                    return i
                A("pe", mm, r=[("diag", ch)] + [("uT", ch, q) for q in (0, 1, 2, 3, "lo", "hi")], w=[PB(bk)])
                A("act", lambda e, ch=ch, bk=bk: e.activation(out=ycv[:, ch, :], in_=pb[bk][:], func=AF.Identity, bias=cbT[:, ch:ch + 1], scale=1.0),
                  r=[PB(bk), "cbT"], w=[("ycv", ch)])
                A("act", lambda e, ch=ch: e.activation(out=ysq[:, ch, :], in_=ycv[:, ch, :], func=AF.Square), r=[("ycv", ch)], w=[("ysq", ch)])
                A("dve", lambda e, ch=ch: e.tensor_copy(ybf[:, ch, :], ycv[:, ch, :]), r=[("ycv", ch)], w=[("ybf", ch)])

            def mm1(e):
                for ch in range(8):
                    i = e.matmul(pb[5][:], ones_b[:], ybf[:, ch, :], start=(ch == 0), stop=(ch == 7))
                return i

            def mm2(e):
                for ch in range(8):
                    i = e.matmul(pb[6][:], ones_b[:], ysq[:, ch, :], start=(ch == 0), stop=(ch == 7))
                return i
            A("pe", mm1, r=[("ybf", ch) for ch in range(8)] + ["ones_b"], w=[PB(5)])
            A("pe", mm2, r=[("ysq", ch) for ch in range(8)] + ["ones_b"], w=[PB(6)])
            A("act", lambda e: e.activation(out=mean[:], in_=pb[5][:], func=AF.Identity, scale=1.0 / 1024.0, bias=cst[:, 2:3]),
              r=[PB(5), "cst2"], w=["mean"])
            A("act", lambda e: e.activation(out=ex2[:], in_=pb[6][:], func=AF.Identity, scale=1.0 / 1024.0, bias=cst[:, 2:3]),
              r=[PB(6), "cst2"], w=["ex2"])
            A("dve", lambda e: e.tensor_tensor(out=rsc[:], in0=mean[:], in1=mean[:], op=ALU.mult), r=["mean"], w=["rsc"])
            A("dve", lambda e: e.tensor_tensor(out=ex2[:], in0=ex2[:], in1=rsc[:], op=ALU.subtract), r=["ex2", "rsc"], w=["ex2"])
            A("act", lambda e: e.activation(out=rsc[:], in_=ex2[:], func=AF.Sqrt, bias=cst[:, 0:1], scale=1.0), r=["ex2", "cst0"], w=["rsc"])
            A("dve", lambda e: e.reciprocal(out=rsc[:], in_=rsc[:]), r=["rsc"], w=["rsc"])
            A("dve", lambda e: e.tensor_tensor(out=ycv[:], in0=ycv[:], in1=mean[:].unsqueeze(1).to_broadcast([128, 8, 512]), op=ALU.subtract),
              r=[("ycv", ch) for ch in range(8)] + ["mean"], w=[("ycv", ch) for ch in range(8)])
            A("dve", lambda e: e.tensor_tensor(out=ycv[:], in0=ycv[:], in1=rsc[:].unsqueeze(1).to_broadcast([128, 8, 512]), op=ALU.mult),
              r=[("ycv", ch) for ch in range(8)] + ["rsc"], w=[("ycv", ch) for ch in range(8)])
            cs = costg[tb % 2]
            ck = ("costg", tb % 2)
            for ch in range(8):
                A("act", lambda e, ch=ch, cs=cs: e.activation(out=cs[:, ch, :], in_=ycv[:, ch, :], func=AF.Silu, scale=gclT[:, ch:ch + 1],
                                                             bias=bclT[:, ch:ch + 1]), r=[("ycv", ch), "gclT", "bclT"], w=[ck + (ch,)])
            A("sp", lambda e, cs=cs, tb=tb: e.dma_start(out=catT_d[1024:2048, tb * 512:(tb + 1) * 512].rearrange("(c p) n -> p c n", p=128), in_=cs[:]),
              r=[ck + (ch,) for ch in range(8)], w=[("catT_d", "conv", tb)], dma=True)
        S.barrier()
        if stop_after is not None and S.enabled:
            A("sp", lambda e: e.dma_start(out=dbg["cqnT"][:, :], in_=view(OFF_CQN, [128, 4 * NTOK], BF16)),
              r=[("cqnT", m, tb) for m in range(4) for tb in range(4)], w=["dbg_cqnT"], dma=True)
            dbg_keys.append("dbg_cqnT")
        PH = {None: 9, "C0": -1, "A0g": -1, "A0a": -1, "A0b": -1, "A0": 0, "A1": 1, "B": 2, "C": 3, "D": 4}[stop_after]
        def stop_here(k):
            if PH != k or not S.enabled:
                return
            CATD_ = [kk for kk in S.bufs if isinstance(kk, tuple) and kk[0] == "catT_d"]
            if k == 1:
                A("sp", lambda e: e.dma_start(out=dbg["ckvnT"][:, :], in_=ckvnT.rearrange("p a b -> p (a b)")),
                  r=[("ckvnT", m, j) for m in range(4) for j in range(16)], w=["dbg_ckvnT"], dma=True)
                A("sp", lambda e: e.dma_start(out=dbg["krT"][:, :], in_=krT), r=[("krT", j) for j in range(16)], w=["dbg_krT"], dma=True)
                A("sp", lambda e: e.dma_start(out=dbg["Tq"][:, :], in_=Tq), r=[("Tq", j, s_) for j in range(4) for s_ in range(2)], w=["dbg_Tq"], dma=True)
                dbg_keys.extend(["dbg_ckvnT", "dbg_krT", "dbg_Tq"])
            if k in (0, 2):
                lo = 1024 if k == 0 else 0
                A("sp", lambda e: e.dma_start(out=dbg["catT"][lo:2048, :], in_=catT_d[lo:2048, :]), r=CATD_, w=["dbg_catT"], dma=True)
                dbg_keys.append("dbg_catT")
            if k == 3:
                A("sp", lambda e: e.dma_start(out=dbg["x1"][:, :], in_=x1_d[:, :]), r=[("x1_d", i) for i in range(16)], w=["dbg_x1"], dma=True)
                dbg_keys.append("dbg_x1")
            S.enabled = False

        stop_here(0)

        cv = Carver(16 * 1024)
        ckvnT = cv.get([128, 4, SEQ], BF16)
        krT = cv.get([128, SEQ], BF16)
        Tq = cv.get([128, NTOK], F32)
        PB_BASE = cv.off
        xts = [(cv.get([128, D], F32), ("xt", i)) for i in range(2)]
        xn = cv.get([128, 4, D], BF16)
        x0Tb = cv.get([128, 16, 512], BF16)
        wkv = cv.get([128, 16, 768], BF16)
        tmpB = {"sq": cv.get([128, 4, 512], BF16),
                "sdv": cv.get([128, 512], F32), "rs": cv.get([128, 512], F32)}
        posi = cv.get([128, 512], I32)
        u2 = cv.get([128, 2, 512], F32)
        k2i = cv.get([128, 2, 512], I32)
        k2f = cv.get([128, 2, 512], F32)
        tab = u2
        t1 = cv.get([128, 512], F32)
        A("pool", lambda e: e.dma_start(out=wkv[:, :, 0:512], in_=w_in_r[:, :, 512:1024]), w=["wkv0"], dma=True)
        for hcol, src in ((512, 1024), (576, 1024), (640, 1056), (672, 1024), (704, 1056), (736, 1024)):
            wdt = 64 if hcol < 640 else 32
            A("pool", lambda e, hcol=hcol, src=src, wdt=wdt: e.dma_start(out=wkv[:, :, hcol:hcol + wdt], in_=w_in_r[:, :, src:src + wdt]),
              w=[("wkv", hcol)], dma=True)
        WKV = ["wkv0"] + [("wkv", h_) for h_ in (512, 576, 640, 672, 704, 736)]
        ln_part(0, xts, xn, "xnB")
        for j in range(16):
            tr_part(xn, "xnB", x0Tb, "x0Tb", (6, 7))
            xk = [("x0Tb", c) for c in range(16)]
            for m in range(6):
                def mm(e, m=m):
                    for k in range(16):
                        i = e.matmul(pb[m][:], wkv[:, k, m * 128:(m + 1) * 128], x0Tb[:, k, :], start=(k == 0), stop=(k == 15))
                    return i
                A("pe", mm, r=WKV + xk, w=[PB(m)])
            if j + 1 < 16:
                ln_part((j + 1) * 512, xts, xn, "xnB")
            A("sp", lambda e, j=j: e.dma_start(out=posi[:], in_=pos[0:1, j * 512:(j + 1) * 512].partition_broadcast(128)), w=["posi"], dma=True)
            A("dve", lambda e: e.tensor_copy(t1[:], posi[:]), r=["posi"], w=["t1"])
            A("dve", lambda e: e.tensor_scalar(out=u2[:, 0, :], in0=t1[:], scalar1=rc[:, 0:1], scalar2=0.25, op0=ALU.mult, op1=ALU.add),
              r=["t1", "rc"], w=["u2a"])
            A("dve", lambda e: e.tensor_scalar(out=u2[:, 1, :], in0=t1[:], scalar1=rc[:, 0:1], scalar2=0.0, op0=ALU.mult, op1=ALU.add),
              r=["t1", "rc"], w=["u2b"])
            A("dve", lambda e: e.tensor_copy(k2i[:], u2[:]), r=["u2a", "u2b"], w=["k2i"])
            A("dve", lambda e: e.tensor_copy(k2f[:], k2i[:]), r=["k2i"], w=["k2f"])
            A("dve", lambda e: e.tensor_tensor(out=u2[:], in0=u2[:], in1=k2f[:], op=ALU.subtract), r=["u2a", "u2b", "k2f"], w=["u2a", "u2b"])
            A("dve", lambda e: e.scalar_tensor_tensor(out=k2f[:], in0=u2[:], scalar=0.5, in1=u2[:], op0=ALU.is_gt, op1=ALU.subtract),
              r=["u2a", "u2b"], w=["k2f"])
            A("act", lambda e: e.activation(out=tab[:, 0, :], in_=k2f[:, 0, :], func=AF.Sin, scale=-2.0 * math.pi), r=["k2f", "u2a"], w=["u2a"])
            A("act", lambda e: e.activation(out=tab[:, 1, :], in_=k2f[:, 1, :], func=AF.Sin, scale=rc[:, 1:2]), r=["k2f", "rc", "u2b"], w=["u2b"])
            if j < 4:
                A("pool", lambda e, j=j: e.tensor_copy(Tq[0:64, j * 512:(j + 1) * 512], tab[0:64, 0, :]), r=["u2a"], w=[("Tq", j, 0)])
                A("pool", lambda e, j=j: e.tensor_copy(Tq[64:128, j * 512:(j + 1) * 512], tab[64:128, 1, :]), r=["u2b"], w=[("Tq", j, 1)])
            A("dve", lambda e: e.tensor_tensor(out=t1[:], in0=pb[4][:], in1=tab[:, 0, :], op=ALU.mult), r=[PB(4), "u2a"], w=["t1"])
            A("dve", lambda e: e.tensor_tensor(out=k2f[:, 0, :], in0=pb[5][:], in1=tab[:, 1, :], op=ALU.mult), r=[PB(5), "u2b", "k2f"], w=["k2f"])
            A("dve", lambda e, j=j: e.tensor_tensor(out=krT[:, j * 512:(j + 1) * 512], in0=t1[:], in1=k2f[:, 0, :], op=ALU.add),
              r=["t1", "k2f"], w=[("krT", j)])
            rms_block([0, 1, 2, 3], gckvT, "gckvT", lambda m, j=j: ckvnT[:, m, j * 512:(j + 1) * 512],
                      [("ckvnT", m, j) for m in range(4)], tmpB, "tB", 4)
        S.barrier()
        stop_here(1)

        W1B = [("w1b", i) for i in range(8)]
        W2B = [("w2b", i) for i in range(8)]
        cv = Carver(PB_BASE)
        KT = cv.get([128, SEQ], BF16)
        Vh = cv.get([128, 64, 128], BF16)
        qnT = cv.get([128, NTOK], BF16)
        qrT = cv.get([128, NTOK], BF16)
        wq = [cv.get([128, 4, 256], BF16) for _ in range(2)]
        wk_ = [cv.get([128, 4, 128], BF16) for _ in range(2)]
        wv_ = [cv.get([128, 4, 128], BF16) for _ in range(2)]
        NPT = 6
        PT = [cv.get([128, 512], BF16) for _ in range(NPT)]
        rec = cv.get([128, 512], F32)
        ostg = [cv.get([128, 512], BF16) for _ in range(2)]
        w_uq_r = w_uq.rearrange("(k p) n -> p k n", p=128)
        w_uk_r = w_uk.rearrange("(k p) n -> p k n", p=128)
        w_uv_r = w_uv.rearrange("(k p) n -> p k n", p=128)
        CQ = [("cqnT", m, tb) for m in range(4) for tb in range(4)]
        CKV = lambda j: [("ckvnT", m, j) for m in range(4)]
        TQK = [("Tq", j, s_) for j in range(4) for s_ in range(2)]
        sctr = [0]
        ectr = [0]
        for h in range(8):
            p2 = h % 2
            c0 = h * 192
            for (dst0, src0, wdt) in ((0, c0, 128), (128, c0 + 128, 64), (192, c0 + 160, 32), (224, c0 + 128, 32)):
                A("pool", lambda e, p2=p2, dst0=dst0, src0=src0, wdt=wdt: e.dma_start(out=wq[p2][:, :, dst0:dst0 + wdt], in_=w_uq_r[:, :, src0:src0 + wdt]),
                  w=[("wq", p2, dst0)], dma=True)
            A("pool", lambda e, p2=p2, h=h: e.dma_start(out=wk_[p2], in_=w_uk_r[:, :, h * 128:(h + 1) * 128]), w=[("wk", p2)], dma=True)
            A("pool", lambda e, p2=p2, h=h: e.dma_start(out=wv_[p2], in_=w_uv_r[:, :, h * 128:(h + 1) * 128]), w=[("wv", p2)], dma=True)
            WQ = [("wq", p2, d_) for d_ in (0, 128, 192, 224)]
            A("pool", lambda e, i=h: e.dma_start(out=w1b_d[i * 256:(i + 1) * 256, :], in_=w_ff1[i * 256:(i + 1) * 256, :]),
              w=[("w1b", h)], dma=True)
            A("pool", lambda e, i=h: e.dma_start(out=w2b_d[i * 1024:(i + 1) * 1024, :], in_=w_ff2[i * 1024:(i + 1) * 1024, :]),
              w=[("w2b", h)], dma=True)

            def evac(out_ap, bank, rk, wkeys):
                i = ectr[0]
                ectr[0] += 1
                if i % 2 == 0:
                    A("act", lambda e: e.activation(out=out_ap, in_=pb[bank][:], func=AF.Identity, bias=cst[:, 2:3], scale=1.0),
                      r=[PB(bank), "cst2"] + rk, w=wkeys)
                else:
                    A("dve", lambda e: e.tensor_copy(out_ap, pb[bank][:]), r=[PB(bank)] + rk, w=wkeys)

            for tb in range(4):
                bk = sctr[0] % 4
                sctr[0] += 1

                def mm(e, tb=tb, bk=bk, p2=p2):
                    for k in range(4):
                        i = e.matmul(pb[bk][:], wq[p2][:, k, 0:128], cqnT[:, k, tb * 512:(tb + 1) * 512], start=(k == 0), stop=(k == 3))
                    return i
                A("pe", mm, r=WQ + CQ, w=[PB(bk)])
                evac(qnT[:, tb * 512:(tb + 1) * 512], bk, [], [("qnT", tb)])
                bk = sctr[0] % 4
                sctr[0] += 1

                def mm(e, tb=tb, bk=bk, p2=p2):
                    for k in range(4):
                        i = e.matmul(pb[bk][:], wq[p2][:, k, 128:256], cqnT[:, k, tb * 512:(tb + 1) * 512], start=(k == 0), stop=(k == 3))
                    return i
                A("pe", mm, r=WQ + CQ, w=[PB(bk)])
                A("dve", lambda e, tb=tb, bk=bk: e.tensor_tensor(out=qrT[:, tb * 512:(tb + 1) * 512], in0=pb[bk][:], in1=Tq[:, tb * 512:(tb + 1) * 512], op=ALU.mult),
                  r=[PB(bk)] + TQK, w=[("qrT", tb)])
            for j in range(16):
                bk = sctr[0] % 4
                sctr[0] += 1

                def mm(e, j=j, bk=bk, p2=p2):
                    for k in range(4):
                        i = e.matmul(pb[bk][:], wk_[p2][:, k, :], ckvnT[:, k, j * 512:(j + 1) * 512], start=(k == 0), stop=(k == 3))
                    return i
                A("pe", mm, r=[("wk", p2)] + CKV(j), w=[PB(bk)])
                evac(KT[:, j * 512:(j + 1) * 512], bk, [], [("KT", j)])
                bk = sctr[0] % 4
                sctr[0] += 1

                def mm(e, j=j, bk=bk, p2=p2):
                    for i4 in range(4):
                        kt = j * 4 + i4
                        for k in range(4):
                            i = e.matmul(pb[bk][:, i4 * 128:(i4 + 1) * 128], ckvnT[:, k, kt * 128:(kt + 1) * 128], wv_[p2][:, k, :],
                                         start=(k == 0), stop=(k == 3))
                    return i
                A("pe", mm, r=[("wv", p2)] + CKV(j), w=[PB(bk)])
                evac(Vh[:, j * 4:(j + 1) * 4, :].rearrange("p a b -> p (a b)"), bk, [], [("Vh", j)])
            QK = [("qnT", tb) for tb in range(4)] + [("qrT", tb) for tb in range(4)]
            for qb in range(4):
                ob = 4 + (qb % 2) * 2
                db = ob + 1
                qs = slice(qb * 512, (qb + 1) * 512)
                sb_of = {}

                def score(kt, qs=qs, qb=qb, sb_of=sb_of):
                    bk = sctr[0] % 4
                    sctr[0] += 1
                    sb_of[kt] = bk

                    def mm(e, kt=kt, bk=bk, qs=qs):
                        e.matmul(pb[bk][:], KT[:, kt * 128:(kt + 1) * 128], qnT[:, qs], start=True, stop=False)
                        return e.matmul(pb[bk][:], krT[:, kt * 128:(kt + 1) * 128], qrT[:, qs], start=False, stop=True)
                    A("pe", mm, r=[("KT", kt // 4), ("krT", kt // 4), ("qnT", qb), ("qrT", qb)], w=[PB(bk)])

                score(0)
                score(1)
                for kt in range(64):
                    if kt + 2 < 64:
                        score(kt + 2)
                    bk = sb_of[kt]
                    pt = PT[kt % NPT]
                    pk = ("PT", kt % NPT)
                    A("act", lambda e, bk=bk, pt=pt: e.activation(out=pt[:], in_=pb[bk][:], func=AF.Exp, scale=SCALE, bias=cst[:, 2:3]),
                      r=[PB(bk), "cst2"], w=[pk])

                    def mm(e, kt=kt, pt=pt, ob=ob, db=db):
                        e.matmul(pb[ob][:], Vh[:, kt, :], pt[:], start=(kt == 0), stop=(kt == 63))
                        return e.matmul(pb[db][:], ones_b[:], pt[:], start=(kt == 0), stop=(kt == 63))
                    A("pe", mm, r=[("Vh", kt // 4), pk, "ones_b"], w=[PB(ob), PB(db)])
                og = ostg[qb % 2]
                ok = ("ostg", qb % 2)
                A("dve", lambda e, db=db: e.reciprocal(out=rec[:], in_=pb[db][:]), r=[PB(db)], w=["rec"])
                A("dve", lambda e, ob=ob, og=og: e.tensor_tensor(out=og[:], in0=pb[ob][:], in1=rec[:], op=ALU.mult), r=[PB(ob), "rec"], w=[ok])
                A("sp", lambda e, og=og, h=h, qb=qb: e.dma_start(out=catT_d[h * 128:(h + 1) * 128, qb * 512:(qb + 1) * 512], in_=og[:]),
                  r=[ok], w=[("catT_d", "att", h, qb)], dma=True)
        S.barrier()
        stop_here(2)

        cv = Carver(0)
        catTb = [cv.get([128, 16, 512], BF16) for _ in range(2)]
        wo = cv.get([128, 16, D], BF16)
        gA = cv.get([128, D], F32)
        bAf = cv.get([128, D], F32)
        bA = cv.get([128, D], BF16)
        xts = [(cv.get([128, D], F32), ("xt", i)) for i in range(2)]
        x0c = [cv.get([128, D], F32) for _ in range(3)]
        x1b = [cv.get([128, D], BF16) for _ in range(2)]
        x1Tst = cv.get([128, 16, 512], BF16)
        CATD = [k for k in S.bufs if isinstance(k, tuple) and k[0] == "catT_d"]
        w_out_r = w_out.rearrange("(k p) n -> p k n", p=128)
        for q4 in range(4):
            A("pool", lambda e, q4=q4: e.dma_start(out=wo[:, q4 * 4:(q4 + 1) * 4, :], in_=w_out_r[:, q4 * 4:(q4 + 1) * 4, :]), w=[("wo", q4)], dma=True)
        WO = [("wo", q4) for q4 in range(4)]
        A("sp", lambda e: e.dma_start(out=gA, in_=ln_in_g[0:1, :].partition_broadcast(128)), w=["gA"], dma=True)
        A("sp", lambda e: e.dma_start(out=bAf[0:1, :], in_=ln_in_b[0:1, :]), w=["bAf"], dma=True)
        A("dve", lambda e: e.tensor_scalar(out=gA, in0=gA, scalar1=ALPHA, scalar2=0.0, op0=ALU.mult, op1=ALU.add), r=["gA"], w=["gA"])
        A("dve", lambda e: e.tensor_scalar(out=bA[0:1, :], in0=bAf[0:1, :], scalar1=ALPHA, scalar2=0.0, op0=ALU.mult, op1=ALU.add), r=["bAf"], w=["bA"])

        def ln_stats(src, skeys, sl):
            def bns(e):
                for c in range(4):
                    i = e.bn_stats(out=stats[:, sl, c * 6:(c + 1) * 6], in_=src[:, c * 512:(c + 1) * 512])
                return i
            A("dve", bns, r=skeys, w=[("stats", sl)])
            A("dve", lambda e: e.bn_aggr(out=mv[:, sl, :], in_=stats[:, sl, :]), r=[("stats", sl)], w=[("mv", sl)])
            A("act", lambda e: e.activation(out=sd[:, sl:sl + 1], in_=mv[:, sl, 1:2], func=AF.Sqrt, bias=cst[:, 0:1], scale=1.0),
              r=[("mv", sl), "cst0"], w=[("sd", sl)])
            A("dve", lambda e: e.reciprocal(out=rstd[:, sl:sl + 1], in_=sd[:, sl:sl + 1]), r=[("sd", sl)], w=[("rstd", sl)])
            A("dve", lambda e: e.scalar_tensor_tensor(out=nmr[:, sl:sl + 1], in0=mv[:, sl, 0:1], scalar=-1.0, in1=rstd[:, sl:sl + 1],
                                                      op0=ALU.mult, op1=ALU.mult), r=[("mv", sl), ("rstd", sl)], w=[("nmr", sl)])

        def c_s1(i):
            xb, xk = xts[i % 2]
            sl = i % 2
            xc = x0c[i % 3]
            xck = ("x0c", i % 3)
            A("sp", lambda e, xb=xb, i=i: e.dma_start(out=xb, in_=x[i * 128:(i + 1) * 128, :]), w=[xk], dma=True)
            ln_stats(xb, [xk], sl)
            A("act", lambda e: e.activation(out=xc, in_=xb, func=AF.Identity, scale=rstd[:, sl:sl + 1], bias=nmr[:, sl:sl + 1]),
              r=[xk, ("rstd", sl), ("nmr", sl)], w=[xck])
            A("dve", lambda e: e.tensor_tensor(out=xc, in0=xc, in1=gA, op=ALU.mult), r=[xck, "gA"], w=[xck])

        def c_mix(i):
            catT = catTb[(i // 4) % 2]
            CATK = [("catTb", (i // 4) % 2)]
            if i % 4 == 0:
                A("sp", lambda e, catT=catT, i=i: e.dma_start(out=catT, in_=catT_d[:, (i // 4) * 512:(i // 4 + 1) * 512].rearrange("(c p) n -> p c n", p=128)),
                  r=CATD, w=CATK, dma=True)
            for cb in range(4):
                def mm(e, i=i, cb=cb, catT=catT):
                    e.matmul(pb[cb][:], ones_b[0:1, :], bA[0:1, cb * 512:(cb + 1) * 512], start=True, stop=False)
                    for k in range(16):
                        r_ = e.matmul(pb[cb][:], catT[:, k, (i % 4) * 128:(i % 4 + 1) * 128], wo[:, k, cb * 512:(cb + 1) * 512], start=False, stop=(k == 15))
                    return r_
                A("pe", mm, r=CATK + WO + ["bA", "ones_b"], w=[PB(cb)])

        def c_r(i):
            xc = x0c[i % 3]
            xck = ("x0c", i % 3)
            for cb in range(4):
                A("dve", lambda e, xc=xc, cb=cb: e.tensor_tensor(out=xc[:, cb * 512:(cb + 1) * 512], in0=xc[:, cb * 512:(cb + 1) * 512],
                                                                  in1=pb[cb][:], op=ALU.add), r=[xck, PB(cb)], w=[xck])

        c_s1(0)
        c_s1(1)
        c_mix(0)
        c_r(0)
        for i in range(16):
            xc = x0c[i % 3]
            xck = ("x0c", i % 3)
            xbf = x1b[i % 2]
            xbk = ("x1b", i % 2)
            sl = 2 + i % 2
            if i + 1 < 16:
                c_mix(i + 1)
            ln_stats(xc, [xck], sl)
            A("act", lambda e, xc=xc, xbf=xbf, sl=sl: e.activation(out=xbf[:], in_=xc, func=AF.Identity, scale=rstd[:, sl:sl + 1], bias=nmr[:, sl:sl + 1]),
              r=[xck, ("rstd", sl), ("nmr", sl)], w=[xbk])
            A("act", lambda e, xc=xc, sl=sl: e.activation(out=xc, in_=xc, func=AF.Identity, scale=rstd[:, sl:sl + 1], bias=nmr[:, sl:sl + 1]),
              r=[xck, ("rstd", sl), ("nmr", sl)], w=[xck])
            A("sp", lambda e, xc=xc, i=i: e.dma_start(out=x1_d[i * 128:(i + 1) * 128, :], in_=xc), r=[xck], w=[("x1_d", i)], dma=True)
            if i + 2 < 16:
                c_s1(i + 2)
            trs = []
            for c4 in range(4):
                bank = 4 + c4
                pv = pb[bank][:].bitcast(BF16)[:, 0:512]

                def tr(e, c4=c4, pv=pv, xbf=xbf):
                    for cc in range(4):
                        c = c4 * 4 + cc
                        r_ = e.transpose(pv[:, cc * 128:(cc + 1) * 128], xbf[:, c * 128:(c + 1) * 128], ident_b[:])
                    return r_
                A("pe", tr, r=[xbk, "ident_b"], w=[PB(bank)])
                trs.append((c4, bank, pv))
            if i + 1 < 16:
                c_r(i + 1)
            for (c4, bank, pv) in trs:
                for cc in range(4):
                    c = c4 * 4 + cc
                    dstv = x1Tst[:, c, (i % 4) * 128:(i % 4 + 1) * 128]
                    srcv = pv[:, cc * 128:(cc + 1) * 128]
                    if c4 % 2 == 0:
                        A("act", lambda e, srcv=srcv, dstv=dstv, c=c: e.activation(out=dstv, in_=srcv, func=AF.Identity, scale=g1T[:, c:c + 1], bias=b1T[:, c:c + 1]),
                          r=[PB(bank), "g1T", "b1T"], w=[("x1Tst", c, i % 4)])
                    else:
                        A("dve", lambda e, srcv=srcv, dstv=dstv, c=c: e.tensor_scalar(out=dstv, in0=srcv, scalar1=g1T[:, c:c + 1], scalar2=b1T[:, c:c + 1],
                                                                                     op0=ALU.mult, op1=ALU.add),
                          r=[PB(bank), "g1T", "b1T"], w=[("x1Tst", c, i % 4)])
            if i % 4 == 3:
                tb = i // 4
                A("sp", lambda e, tb=tb: e.dma_start(out=x1T_d[:, tb * 512:(tb + 1) * 512].rearrange("(c p) n -> p c n", p=128), in_=x1Tst[:]),
                  r=[("x1Tst", c, t_) for c in range(16) for t_ in range(4)], w=[("x1T_d", tb)], dma=True)
        S.barrier()
        stop_here(3)

        cv = Carver(0)
        h1T = cv.get([128, 64, 512], BF16)
        x1Tb = [cv.get([128, 16, 512], BF16) for _ in range(1)]
        xres = [cv.get([128, D], F32) for _ in range(4)]
        gb2 = [cv.get([128, D], F32) for _ in range(4)]
        NW1 = 4
        w1 = [cv.get([128, 16, 256], BF16) for _ in range(NW1)]
        NW2 = 4
        w2 = [cv.get([128, 2, 1024], BF16) for _ in range(NW2)]
        rl = [cv.get([128, 512], F32) for _ in range(3)]
        for i, src in enumerate((g_ln1, b_ln1, g_ln2, b_ln2)):
            A("sp", lambda e, i=i, src=src: e.dma_start(out=gb2[i], in_=src[0:1, :].partition_broadcast(128)), w=[("gb2", i)], dma=True)
        for i in range(2):
            A("dve", lambda e, i=i: e.tensor_scalar(out=gb2[i], in0=gb2[i], scalar1=ALPHA, scalar2=0.0, op0=ALU.mult, op1=ALU.add),
              r=[("gb2", i)], w=[("gb2", i)])
        w1b_r = w1b_d.rearrange("(k p) f -> p k f", p=128)
        w2b_r = w2b_d.rearrange("(c p) d -> p c d", p=128)
        w1c = [0]
        w2c = [0]
        rlc = [0]
        for tb in range(4):
            xT = x1Tb[0]
            xTk = ("x1Tb", 0)
            A("sp", lambda e, xT=xT, tb=tb: e.dma_start(out=xT, in_=x1T_d[:, tb * 512:(tb + 1) * 512].rearrange("(c p) n -> p c n", p=128)),
              r=[("x1T_d", tb)], w=[xTk], dma=True)
            for t in range(4):
                A("sp", lambda e, t=t, tb=tb: e.dma_start(out=xres[t], in_=x1_d[(tb * 4 + t) * 128:(tb * 4 + t + 1) * 128, :]),
                  r=[("x1_d", tb * 4 + t)], w=[("xres", t)], dma=True)
                A("dve", lambda e, t=t: e.tensor_tensor(out=xres[t], in0=xres[t], in1=gb2[0], op=ALU.mult), r=[("xres", t), ("gb2", 0)], w=[("xres", t)])
                A("pool", lambda e, t=t: e.tensor_tensor(out=xres[t], in0=xres[t], in1=gb2[1], op=ALU.add), r=[("xres", t), ("gb2", 1)], w=[("xres", t)])
            for fg in range(32):
                wi = w1c[0] % NW1
                w1c[0] += 1
                A("sp", lambda e, wi=wi, fg=fg: e.dma_start(out=w1[wi], in_=w1b_r[:, :, fg * 256:(fg + 1) * 256]), r=W1B, w=[("w1", wi)], dma=True)
                for fc in range(2):
                    f = fg * 2 + fc
                    bk = f % 4

                    def mm(e, wi=wi, fc=fc, bk=bk, xT=xT):
                        for k in range(16):
                            r_ = e.matmul(pb[bk][:], w1[wi][:, k, fc * 128:(fc + 1) * 128], xT[:, k, :], start=(k == 0), stop=(k == 15))
                        return r_
                    A("pe", mm, r=[("w1", wi), xTk], w=[PB(bk)])
                    ri = rlc[0] % 3
                    rlc[0] += 1
                    A("act", lambda e, bk=bk, ri=ri: e.activation(out=rl[ri][:], in_=pb[bk][:], func=AF.Relu, bias=cst[:, 2:3], scale=1.0),
                      r=[PB(bk), "cst2"], w=[("rl", ri)])
                    A("dve", lambda e, ri=ri, f=f: e.tensor_tensor(out=h1T[:, f, :], in0=rl[ri][:], in1=rl[ri][:], op=ALU.mult),
                      r=[("rl", ri)], w=[("h1T", f)])
            for ps_ in range(2):
                for fg in range(32):
                    wi = w2c[0] % NW2
                    w2c[0] += 1
                    A("sp", lambda e, wi=wi, fg=fg, ps_=ps_: e.dma_start(out=w2[wi], in_=w2b_r[:, fg * 2:(fg + 1) * 2, ps_ * 1024:(ps_ + 1) * 1024]),
                      r=W2B, w=[("w2", wi)], dma=True)

                    def mm(e, wi=wi, fg=fg):
                        for fc in range(2):
                            f = fg * 2 + fc
                            for t in range(4):
                                for cb in range(2):
                                    r_ = e.matmul(pb[t * 2 + cb][:], h1T[:, f, t * 128:(t + 1) * 128], w2[wi][:, fc, cb * 512:(cb + 1) * 512],
                                                  start=(f == 0), stop=(f == 63))
                        return r_
                    A("pe", mm, r=[("w2", wi), ("h1T", fg * 2), ("h1T", fg * 2 + 1)], w=[PB(b_) for b_ in range(8)])
                for t in range(4):
                    for cb in range(2):
                        col = ps_ * 1024 + cb * 512
                        A("dve", lambda e, t=t, cb=cb, col=col: e.tensor_tensor(out=xres[t][:, col:col + 512], in0=xres[t][:, col:col + 512],
                                                                                in1=pb[t * 2 + cb][:], op=ALU.add),
                          r=[("xres", t), PB(t * 2 + cb)], w=[("xres", t)])
            for t in range(4):
                ln_stats(xres[t], [("xres", t)], t)
                A("act", lambda e, t=t: e.activation(out=xres[t], in_=xres[t], func=AF.Identity, scale=rstd[:, t:t + 1], bias=nmr[:, t:t + 1]),
                  r=[("xres", t), ("rstd", t), ("nmr", t)], w=[("xres", t)])
                A("dve", lambda e, t=t: e.tensor_tensor(out=xres[t], in0=xres[t], in1=gb2[2], op=ALU.mult), r=[("xres", t), ("gb2", 2)], w=[("xres", t)])
                A("pool", lambda e, t=t: e.tensor_tensor(out=xres[t], in0=xres[t], in1=gb2[3], op=ALU.add), r=[("xres", t), ("gb2", 3)], w=[("xres", t)])
                row = (tb * 4 + t) * 128
                A("sp", lambda e, t=t, row=row: e.dma_start(out=out[row:row + 128, :], in_=xres[t]), r=[("xres", t)], w=[("out", tb * 4 + t)], dma=True)
        final_keys = [("out", i) for i in range(16)] if S.enabled else list(dbg_keys)
        S.enabled = True
        S.emit(nc, st, final_keys=final_keys)
    return nc


_ROPE_C = None


def _consts():
    p = np.arange(128)
    inv = (10000.0 ** (-(p % 32).astype(np.float64) * (2.0 / 64))) / (2.0 * math.pi)
    sgn = np.where((p % 64) < 32, -1.0, 1.0)
    rc = np.stack([inv, -2.0 * math.pi * sgn], axis=1).astype(np.float32)
    return rc, np.eye(128, dtype=np.float32)


def _make_in_maps(x, positions, ln_in_g, ln_in_b, w_in, g_cq, w_uq, g_ckv, w_uk, w_uv, conv_w, conv_b,
           g_conv_ln, b_conv_ln, w_out, g_ln1, b_ln1, w_ff1, w_ff2, g_ln2, b_ln2):
    x = np.asarray(x)
    positions = np.asarray(positions)
    rc, ident = _consts()
    f = lambda a: np.ascontiguousarray(np.asarray(a, dtype=np.float32))
    common = {
        "ident": ident, "ropec": rc,
        "ln_in_g": f(ln_in_g).reshape(1, -1), "ln_in_b": f(ln_in_b).reshape(1, -1),
        "w_in": f(w_in[0]), "g_cq": f(g_cq[0]).reshape(1, -1), "w_uq": f(w_uq[0]),
        "g_ckv": f(g_ckv[0]).reshape(1, -1), "w_uk": f(w_uk[0]), "w_uv": f(w_uv[0]),
        "conv_w": f(conv_w[0]), "conv_b": f(conv_b[0]).reshape(1, -1),
        "g_conv_ln": f(g_conv_ln[0]).reshape(1, -1), "b_conv_ln": f(b_conv_ln[0]).reshape(1, -1),
        "w_out": f(w_out[0]), "g_ln1": f(g_ln1[0]).reshape(1, -1), "b_ln1": f(b_ln1[0]).reshape(1, -1),
        "w_ff1": f(w_ff1[0]), "w_ff2": f(w_ff2[0]),
        "g_ln2": f(g_ln2[0]).reshape(1, -1), "b_ln2": f(b_ln2[0]).reshape(1, -1),
    }
    in_maps = []
    for c in range(NCORES):
        b, r = c // 4, c % 4
        t0 = r * NTOK
        m = dict(common)
        m["x"] = np.ascontiguousarray(np.roll(x[b], -t0, axis=0), dtype=np.float32)
        m["pos"] = np.ascontiguousarray(np.roll(positions[b], -t0).reshape(1, -1).astype(np.int32))
        hm = np.zeros((128, 2), np.float32)
        hm[:, 0] = 1.0 if r > 0 else 0.0
        hm[:, 1] = 1.0 if r < 3 else 0.0
        m["hmask"] = hm
        in_maps.append(m)
    return in_maps


def kernel(**inputs):
    in_maps = _make_in_maps(**inputs)
    nc = build_program()
    res = run_bass_kernel_spmd(nc, in_maps, core_ids=list(range(NCORES)))
    outp = np.empty((2, SEQ, D), np.float32)
    for c in range(NCORES):
        b, r = c // 4, c % 4
        outp[b, r * NTOK:(r + 1) * NTOK] = res.results[c]["out"]
    return outp
```

```python
import math
import contextlib
import numpy as np
import concourse.bass as bass
import concourse.mybir as mybir
from concourse.bass_utils import run_bass_kernel_spmd

F32 = mybir.dt.float32
BF16 = mybir.dt.bfloat16
I32 = mybir.dt.int32
AF = mybir.ActivationFunctionType
ALU = mybir.AluOpType

NCORES = 8
D = 2048
SEQ = 8192
NTOK = 2048
ALPHA = 2.0 ** 0.25
SCALE = 192 ** -0.5
LN_EPS = 1e-5
RMS_EPS = 1e-6
ARENA_BYTES = 204 * 1024

DEBUG = {}


class _Buf:
    __slots__ = ("w", "r")

    def __init__(self):
        self.w = None
        self.r = []


class _Op:
    __slots__ = ("eng", "fn", "deps", "dma", "tok", "signal", "idx", "cnt")


class Sched:
    ENGS = ("pe", "act", "dve", "pool", "sp")
    NDMA = 28
    NSP = 20

    def __init__(self):
        self.ops = {e: [] for e in self.ENGS}
        self.bufs = {}
        self.ndma = {"sp": 0, "pool": 0, "act": 0}
        self.dma_cnt = [0] * self.NDMA
        self.dma_last = [None] * self.NDMA
        self.pending_barrier = {e: None for e in self.ENGS}
        self.enabled = True

    def buf(self, key):
        b = self.bufs.get(key)
        if b is None:
            b = self.bufs[key] = _Buf()
        return b

    def barrier(self):
        if not self.enabled:
            return
        toks = []
        for e in self.ENGS:
            for op in reversed(self.ops[e]):
                if not op.dma:
                    toks.append(op.tok)
                    break
        for t in self.dma_last:
            if t is not None:
                toks.append(t)
        for e in self.ENGS:
            prev = self.pending_barrier[e]
            self.pending_barrier[e] = toks if prev is None else prev + toks

    def add(self, eng, fn, r=(), w=(), dma=False):
        if not self.enabled:
            return None
        op = _Op()
        op.eng = eng
        op.fn = fn
        op.dma = dma
        op.signal = False
        op.cnt = 0
        op.idx = len(self.ops[eng])
        deps = []
        if self.pending_barrier[eng] is not None:
            deps.extend(self.pending_barrier[eng])
            self.pending_barrier[eng] = None
        for k in r:
            b = self.buf(k)
            if b.w is not None:
                deps.append(b.w)
        for k in w:
            b = self.buf(k)
            if b.w is not None:
                deps.append(b.w)
            deps.extend(b.r)
        if dma:
            if eng == "sp":
                j = self.ndma["sp"] % self.NSP
            else:
                j = self.NSP + self.ndma[eng] % (self.NDMA - self.NSP)
            self.ndma[eng] += 1
            if self.dma_last[j] is not None:
                deps.append(self.dma_last[j])
            self.dma_cnt[j] += 1
            op.tok = ("d%d" % j, self.dma_cnt[j] * 16, op)
            self.dma_last[j] = op.tok
        else:
            op.tok = (eng, op.idx, op)
        seen = {}
        for t in deps:
            if t[2] is op:
                continue
            if (not dma) and eng == "pe" and t[0] == "pe":
                continue
            key = t[0]
            if key not in seen or seen[key][1] < t[1]:
                seen[key] = t
        op.deps = list(seen.values())
        for t in op.deps:
            t[2].signal = True
        for k in r:
            self.buf(k).r.append(op.tok)
        for k in w:
            b = self.buf(k)
            b.w = op.tok
            b.r = []
        self.ops[eng].append(op)
        return op

    def emit(self, nc, stack, final_keys=()):
        sems = {}
        for e in ("pe", "act", "dve", "pool"):
            sems[e] = stack.enter_context(nc.semaphore("s_" + e))
        for j in range(self.NDMA):
            sems["d%d" % j] = stack.enter_context(nc.semaphore("s_d%d" % j))
        final = [self.buf(k).w for k in final_keys]
        for t in final:
            t[2].signal = True
        for e in self.ENGS:
            c = 0
            for op in self.ops[e]:
                if op.dma:
                    continue
                if op.signal:
                    c += 1
                op.cnt = c

        def tokval(t):
            o = t[2]
            return t[1] if o.dma else o.cnt

        block = stack.enter_context(nc.Block())
        engobj = {"pe": "tensor", "act": "scalar", "dve": "vector", "pool": "gpsimd", "sp": "sync"}
        for e in self.ENGS:
            ops = self.ops[e]
            fw = final if e == "sp" else ()

            def body(eng, ops=ops, e=e, fw=fw):
                waited = {}
                for op in ops:
                    for t in op.deps:
                        v = tokval(t)
                        s = t[0]
                        if waited.get(s, 0) >= v:
                            continue
                        waited[s] = v
                        eng.wait_ge(sems[s], v)
                    inst = op.fn(eng)
                    if op.dma:
                        inst.then_inc(sems[op.tok[0]], 16)
                    elif op.signal:
                        inst.then_inc(sems[e], 1)
                for t in fw:
                    v = tokval(t)
                    s = t[0]
                    if waited.get(s, 0) >= v:
                        continue
                    waited[s] = v
                    eng.wait_ge(sems[s], v)

            getattr(block, engobj[e])(body)


def build_program(stop_after=None):
    nc = bass.Bass("TRN2", target_bir_lowering=False)
    S = Sched()
    A = S.add

    def din(name, shape, dt=F32):
        return nc.dram_tensor(name, shape, dt, kind="ExternalInput").ap()

    x = din("x", [SEQ, D])
    pos = din("pos", [1, SEQ], I32)
    hmask_d = din("hmask", [128, 2])
    ident_d = din("ident", [128, 128])
    rc_d = din("ropec", [128, 2])
    ln_in_g = din("ln_in_g", [1, D])
    ln_in_b = din("ln_in_b", [1, D])
    w_in = din("w_in", [D, 3136])
    g_cq = din("g_cq", [1, 512])
    w_uq = din("w_uq", [512, 1536])
    g_ckv = din("g_ckv", [1, 512])
    w_uk = din("w_uk", [512, 1024])
    w_uv = din("w_uv", [512, 1024])
    conv_w = din("conv_w", [31, 1024])
    conv_b = din("conv_b", [1, 1024])
    g_cln = din("g_conv_ln", [1, 1024])
    b_cln = din("b_conv_ln", [1, 1024])
    w_out = din("w_out", [D, D])
    g_ln1 = din("g_ln1", [1, D])
    b_ln1 = din("b_ln1", [1, D])
    w_ff1 = din("w_ff1", [D, 8192])
    w_ff2 = din("w_ff2", [8192, D])
    g_ln2 = din("g_ln2", [1, D])
    b_ln2 = din("b_ln2", [1, D])
    out = nc.dram_tensor("out", [NTOK, D], F32, kind="ExternalOutput").ap()
    catT_d = nc.dram_tensor("catT_s", [D, NTOK], BF16, kind="Internal").ap()
    x1_d = nc.dram_tensor("x1_s", [NTOK, D], F32, kind="Internal").ap()
    x1T_d = nc.dram_tensor("x1T_s", [D, NTOK], BF16, kind="Internal").ap()
    w1b_d = nc.dram_tensor("w1b_s", [D, 8192], BF16, kind="Internal").ap()
    w2b_d = nc.dram_tensor("w2b_s", [8192, D], BF16, kind="Internal").ap()
    dbg = {}
    if stop_after is not None:
        dbg["cqnT"] = nc.dram_tensor("dbg_cqnT", [128, 4 * NTOK], BF16, kind="ExternalOutput").ap()
        dbg["ckvnT"] = nc.dram_tensor("dbg_ckvnT", [128, 4 * SEQ], BF16, kind="ExternalOutput").ap()
        dbg["krT"] = nc.dram_tensor("dbg_krT", [128, SEQ], BF16, kind="ExternalOutput").ap()
        dbg["Tq"] = nc.dram_tensor("dbg_Tq", [128, NTOK], F32, kind="ExternalOutput").ap()
        dbg["catT"] = nc.dram_tensor("dbg_catT", [D, NTOK], BF16, kind="ExternalOutput").ap()
        dbg["x1"] = nc.dram_tensor("dbg_x1", [NTOK, D], F32, kind="ExternalOutput").ap()
        dbg["gen"] = nc.dram_tensor("dbg_gen", [128, 4096], BF16, kind="ExternalOutput").ap()
    dbg_keys = []

    with contextlib.ExitStack() as st:
        st.enter_context(nc.allow_non_contiguous_dma(reason="small param vectors"))

        def sb(name, shape, dt):
            return st.enter_context(nc.sbuf_tensor(name, shape, dt))

        arena = sb("arena", [128, ARENA_BYTES // 2], BF16)
        ident_f = sb("ident_f", [128, 128], F32)
        ident_b = sb("ident_b", [128, 128], BF16)
        ones_b = sb("ones_b", [128, 128], BF16)
        cst = sb("cst", [128, 8], F32)
        hmask = sb("hmask_t", [128, 2], F32)
        rc = sb("rc_t", [128, 2], F32)
        ginT = sb("ginT", [128, 16], F32)
        binT = sb("binT", [128, 16], F32)
        gcqT = sb("gcqT", [128, 4], F32)
        g1T = sb("g1T", [128, 16], F32)
        b1T = sb("b1T", [128, 16], F32)
        gckvT = sb("gckvT", [128, 4], F32)
        cbT = sb("cbT", [128, 8], F32)
        gclT = sb("gclT", [128, 8], F32)
        bclT = sb("bclT", [128, 8], F32)
        stats = sb("stats", [128, 4, 24], F32)
        mv = sb("mv", [128, 4, 2], F32)
        sd = sb("sd", [128, 4], F32)
        rstd = sb("rstd", [128, 4], F32)
        nmr = sb("nmr", [128, 4], F32)
        pb = [st.enter_context(nc.psum_tensor("pb%d" % i, [128, 512], F32)) for i in range(8)]

        def PB(i):
            return ("pb", i)

        def view(off, shape, dt):
            esz = 2 if dt == BF16 else 4
            n = 1
            for s_ in shape[1:]:
                n *= s_
            assert off % 4 == 0 and off + n * esz <= ARENA_BYTES, (off, n * esz)
            ap = arena[:, off // 2: off // 2 + n * esz // 2]
            if dt != BF16:
                ap = ap.bitcast(dt)
            if len(shape) == 3:
                ap = ap.rearrange("p (a b) -> p a b", a=shape[1])
            elif len(shape) == 4:
                ap = ap.rearrange("p (a b c) -> p a b c", a=shape[1], b=shape[2])
            return ap

        class Carver:
            def __init__(self, base=0):
                self.off = base

            def get(self, shape, dt):
                esz = 2 if dt == BF16 else 4
                n = 1
                for s_ in shape[1:]:
                    n *= s_
                v = view(self.off, shape, dt)
                self.off += (n * esz + 3) // 4 * 4
                return v

        A("sp", lambda e: e.dma_start(out=ident_f[:], in_=ident_d[:, :]), w=["ident_f"], dma=True)
        A("sp", lambda e: e.dma_start(out=hmask[:], in_=hmask_d[:, :]), w=["hmask"], dma=True)
        A("sp", lambda e: e.dma_start(out=rc[:], in_=rc_d[:, :]), w=["rc"], dma=True)

        def ldT(dst, src, nch, key):
            A("sp", lambda e: e.dma_start(out=dst[:], in_=src.rearrange("o (c p) -> p (o c)", p=128)),
              w=[key], dma=True)

        ldT(ginT, ln_in_g, 16, "ginT")
        ldT(binT, ln_in_b, 16, "binT")
        ldT(gcqT, g_cq, 4, "gcqT")
        ldT(g1T, g_ln1, 16, "g1T")
        ldT(b1T, b_ln1, 16, "b1T")
        ldT(gckvT, g_ckv, 4, "gckvT")
        ldT(cbT, conv_b, 8, "cbT")
        ldT(gclT, g_cln, 8, "gclT")
        ldT(bclT, b_cln, 8, "bclT")
        A("dve", lambda e: e.tensor_copy(ident_b[:], ident_f[:]), r=["ident_f"], w=["ident_b"])
        A("pool", lambda e: e.memset(ones_b[:], 1.0), w=["ones_b"])
        A("pool", lambda e: e.memset(cst[:, 0:1], LN_EPS), w=["cst0"])
        A("pool", lambda e: e.memset(cst[:, 1:2], RMS_EPS), w=["cst1"])
        A("pool", lambda e: e.memset(cst[:, 2:3], 0.0), w=["cst2"])
        CST = ["cst0", "cst1", "cst2"]

        if stop_after == "C0":
            A("sp", lambda e: e.dma_start(out=dbg["gen"][:, 0:128], in_=ident_b[:]), r=["ident_b", "ginT", "binT", "gcqT", "gckvT", "cbT", "gclT", "bclT", "hmask", "rc", "ones_b"] + CST, w=["dbg_gen"], dma=True)
            dbg_keys.append("dbg_gen")
            S.enabled = False
        gctr = [0]

        def ln_part(row0, xts, xn, xnk, nt=4):
            g = gctr[0]
            gctr[0] += 1
            nb = len(xts)
            used = [xts[(g * nt + t) % nb] for t in range(nt)]

            def load(t):
                xb, xk = used[t]
                A("sp", lambda e, xb=xb, t=t: e.dma_start(out=xb, in_=x[row0 + t * 128: row0 + (t + 1) * 128, :]),
                  w=[xk], dma=True)
            for t in range(min(nb, nt)):
                load(t)
            for t in range(nt):
                xb, xk = used[t]

                def bns(e, xb=xb, t=t):
                    for c in range(4):
                        i = e.bn_stats(out=stats[:, t, c * 6:(c + 1) * 6], in_=xb[:, c * 512:(c + 1) * 512])
                    return i
                A("dve", bns, r=[xk], w=[("stats", t)])
                A("dve", lambda e, t=t: e.bn_aggr(out=mv[:, t, :], in_=stats[:, t, :]), r=[("stats", t)], w=[("mv", t)])
                A("act", lambda e, t=t: e.activation(out=sd[:, t:t + 1], in_=mv[:, t, 1:2], func=AF.Sqrt, bias=cst[:, 0:1], scale=1.0),
                  r=[("mv", t), "cst0"], w=[("sd", t)])
                A("dve", lambda e, t=t: e.reciprocal(out=rstd[:, t:t + 1], in_=sd[:, t:t + 1]), r=[("sd", t)], w=[("rstd", t)])
                A("dve", lambda e, t=t: e.scalar_tensor_tensor(out=nmr[:, t:t + 1], in0=mv[:, t, 0:1], scalar=-1.0, in1=rstd[:, t:t + 1],
                                                               op0=ALU.mult, op1=ALU.mult), r=[("mv", t), ("rstd", t)], w=[("nmr", t)])
                A("act", lambda e, xb=xb, t=t: e.activation(out=xn[:, t, :], in_=xb, func=AF.Identity,
                                                           scale=rstd[:, t:t + 1], bias=nmr[:, t:t + 1]),
                  r=[xk, ("rstd", t), ("nmr", t)], w=[(xnk, t)])
                if t + nb < nt:
                    load(t + nb)

        def tr_part(xn, xnk, dstT, dkey, trb, nt=4):
            for c in range(16):
                bank = trb[c % len(trb)]
                pv = pb[bank][:].bitcast(BF16)[:, 0:nt * 128]

                def tr(e, c=c, pv=pv):
                    for t in range(nt):
                        i = e.transpose(pv[:, t * 128:(t + 1) * 128], xn[:, t, c * 128:(c + 1) * 128], ident_b[:])
                    return i
                A("pe", tr, r=[(xnk, t) for t in range(nt)] + ["ident_b"], w=[PB(bank)])
                if c % 2 == 0:
                    A("act", lambda e, c=c, pv=pv: e.activation(out=dstT[:, c, 0:nt * 128], in_=pv, func=AF.Identity,
                                                               scale=ginT[:, c:c + 1], bias=binT[:, c:c + 1]),
                      r=[PB(bank), "ginT", "binT"], w=[(dkey, c)])
                else:
                    A("dve", lambda e, c=c, pv=pv: e.tensor_scalar(out=dstT[:, c, 0:nt * 128], in0=pv, scalar1=ginT[:, c:c + 1],
                                                                  scalar2=binT[:, c:c + 1], op0=ALU.mult, op1=ALU.add),
                      r=[PB(bank), "ginT", "binT"], w=[(dkey, c)])

        def rms_block(src_banks, gT, gkey, dst_fn, dkeys, tmp, tkey, statbank, n=512):
            sq, sdv, rs = tmp["sq"], tmp["sdv"], tmp["rs"]
            for m in range(4):
                A("act", lambda e, m=m: e.activation(out=sq[:, m, 0:n], in_=pb[src_banks[m]][:, 0:n], func=AF.Square),
                  r=[PB(src_banks[m])], w=[(tkey, "sq", m)])

            def mm(e):
                for m in range(4):
                    i = e.matmul(pb[statbank][:, 0:n], ones_b[:], sq[:, m, 0:n], start=(m == 0), stop=(m == 3))
                return i
            A("pe", mm, r=[(tkey, "sq", m) for m in range(4)] + ["ones_b"], w=[PB(statbank)])
            A("act", lambda e: e.activation(out=sdv[:, 0:n], in_=pb[statbank][:, 0:n], func=AF.Sqrt, bias=cst[:, 1:2],
                                            scale=1.0 / 512.0), r=[PB(statbank), "cst1"], w=[(tkey, "sdv")])
            A("dve", lambda e: e.reciprocal(out=rs[:, 0:n], in_=sdv[:, 0:n]), r=[(tkey, "sdv")], w=[(tkey, "rs")])
            for m in range(4):
                A("dve", lambda e, m=m: e.scalar_tensor_tensor(out=dst_fn(m), in0=pb[src_banks[m]][:, 0:n], scalar=gT[:, m:m + 1],
                                                              in1=rs[:, 0:n], op0=ALU.mult, op1=ALU.mult),
                  r=[PB(src_banks[m]), (tkey, "rs"), gkey], w=[dkeys[m]])

        OFF_CQN = 0
        cqnT = view(OFF_CQN, [128, 4, NTOK], BF16)
        P0 = 16 * 1024
        cv = Carver(P0)
        x0T = cv.get([128, 16, NTOK], BF16)
        x0Th = cv.get([128, 16, 32], BF16)
        uT = cv.get([128, 8, 2080], BF16)
        R1 = cv.off
        xts = [(cv.get([128, D], F32), ("xt", i)) for i in range(4)]
        xn = cv.get([128, 4, D], BF16)
        xn2 = cv.get([128, 4, D], BF16)
        xnl = [xn, xn2]
        ln_part(0, xts, xnl[0], ("xn", 0))
        for g in range(4):
            if g + 1 < 4:
                ln_part((g + 1) * 512, xts, xnl[(g + 1) % 2], ("xn", (g + 1) % 2))
            tr_part(xnl[g % 2], ("xn", g % 2), x0T[:, :, g * 512:(g + 1) * 512], ("x0T", g), (4, 5, 6, 7))
        if stop_after == "A0g":
            A("sp", lambda e: e.dma_start(out=dbg["gen"][:, :], in_=x0T[:, 0:2, :].rearrange("p a b -> p (a b)")),
              r=[(("x0T", g), c) for g in range(4) for c in range(2)], w=["dbg_gen"], dma=True)
            dbg_keys.append("dbg_gen")
            S.enabled = False
        xh, xhk = xts[0]
        xnh = xn
        A("sp", lambda e: e.dma_start(out=xh[0:16, :], in_=x[SEQ - 16:SEQ, :]), w=[xhk], dma=True)
        A("sp", lambda e: e.dma_start(out=xh[16:32, :], in_=x[NTOK:NTOK + 16, :]), w=[xhk + ("b",)], r=[xhk], dma=True)
        def bnsh(e):
            for c in range(4):
                i = e.bn_stats(out=stats[0:32, 0, c * 6:(c + 1) * 6], in_=xh[0:32, c * 512:(c + 1) * 512])
            return i
        A("dve", bnsh, r=[xhk, xhk + ("b",)], w=[("stats", 0)])
        A("dve", lambda e: e.bn_aggr(out=mv[0:32, 0, :], in_=stats[0:32, 0, :]), r=[("stats", 0)], w=[("mv", 0)])
        A("act", lambda e: e.activation(out=sd[0:32, 0:1], in_=mv[0:32, 0:1, 1], func=AF.Sqrt, bias=cst[0:32, 0:1], scale=1.0),
          r=[("mv", 0), "cst0"], w=[("sd", 0)])
        A("dve", lambda e: e.reciprocal(out=rstd[0:32, 0:1], in_=sd[0:32, 0:1]), r=[("sd", 0)], w=[("rstd", 0)])
        A("dve", lambda e: e.scalar_tensor_tensor(out=nmr[0:32, 0:1], in0=mv[0:32, 0:1, 0], scalar=-1.0, in1=rstd[0:32, 0:1],
                                                  op0=ALU.mult, op1=ALU.mult), r=[("mv", 0), ("rstd", 0)], w=[("nmr", 0)])
        A("act", lambda e: e.activation(out=xnh[0:32, 0, :], in_=xh[0:32, :], func=AF.Identity, scale=rstd[0:32, 0:1],
                                        bias=nmr[0:32, 0:1]), r=[xhk, xhk + ("b",), ("rstd", 0), ("nmr", 0)], w=[(("xn", 0), 0)])
        for c in range(16):
            bank = (4, 5, 6, 7)[c % 4]
            pv = pb[bank][:].bitcast(BF16)[:, 0:32]
            A("pe", lambda e, c=c, pv=pv: e.transpose(pv, xnh[0:32, 0, c * 128:(c + 1) * 128], ident_b[0:32, 0:32]),
              r=[(("xn", 0), 0), "ident_b"], w=[PB(bank)])
            A("dve", lambda e, c=c, pv=pv: e.tensor_scalar(out=x0Th[:, c, :], in0=pv, scalar1=ginT[:, c:c + 1],
                                                          scalar2=binT[:, c:c + 1], op0=ALU.mult, op1=ALU.add),
              r=[PB(bank), "ginT", "binT"], w=[("x0Th", c)])
        if stop_after == "A0a":
            A("sp", lambda e: e.dma_start(out=dbg["gen"][:, :], in_=x0T[:, 0:2, :].rearrange("p a b -> p (a b)")),
              r=[(("x0T", g), c) for g in range(4) for c in range(2)] + [("x0Th", c) for c in range(16)], w=["dbg_gen"], dma=True)
            dbg_keys.append("dbg_gen")
            S.enabled = False
        S.barrier()
        cv = Carver(R1)
        wcq = cv.get([128, 16, 512], BF16)
        wch = [cv.get([128, 16, 256], BF16) for _ in range(2)]
        tmpA = {"sq": cv.get([128, 4, 512], BF16),
                "sdv": cv.get([128, 512], F32), "rs": cv.get([128, 512], F32)}
        sig = [cv.get([128, 512], F32) for _ in range(2)]
        uh = cv.get([128, 32], F32)
        X0K = lambda g: [(("x0T", g), c) for c in range(16)]
        w_in_r = w_in.rearrange("(k p) n -> p k n", p=128)
        A("pool", lambda e: e.dma_start(out=wcq, in_=w_in_r[:, :, 0:512]), w=["wcq"], dma=True)
        for tb in range(4):
            banks = [0, 1, 2, 3]
            for m in range(4):
                def mm(e, m=m, tb=tb):
                    for k in range(16):
                        i = e.matmul(pb[m][:], wcq[:, k, m * 128:(m + 1) * 128], x0T[:, k, tb * 512:(tb + 1) * 512],
                                     start=(k == 0), stop=(k == 15))
                    return i
                A("pe", mm, r=["wcq"] + X0K(tb), w=[PB(m)])
            rms_block(banks, gcqT, "gcqT", lambda m, tb=tb: cqnT[:, m, tb * 512:(tb + 1) * 512],
                      [("cqnT", m, tb) for m in range(4)], tmpA, "tA", 4)
        for ch in range(8):
            wb = wch[ch % 2]
            wk = ("wch", ch % 2)
            A("pool", lambda e, wb=wb, ch=ch: e.dma_start(out=wb[:, :, 0:128], in_=w_in_r[:, :, 1088 + ch * 128:1088 + (ch + 1) * 128]),
              w=[wk], dma=True)
            A("pool", lambda e, wb=wb, ch=ch: e.dma_start(out=wb[:, :, 128:256], in_=w_in_r[:, :, 2112 + ch * 128:2112 + (ch + 1) * 128]),
              w=[wk + ("g",)], dma=True)
            for tb in range(5):
                n = 512 if tb < 4 else 32
                ba, bg = (0, 1) if tb % 2 == 0 else (2, 3)
                rhs_fn = (lambda k, tb=tb: x0T[:, k, tb * 512:(tb + 1) * 512]) if tb < 4 else (lambda k: x0Th[:, k, :])
                rk = X0K(tb) if tb < 4 else [("x0Th", c) for c in range(16)]

                def mm(e, wb=wb, ba=ba, bg=bg, n=n, rhs_fn=rhs_fn):
                    for k in range(16):
                        e.matmul(pb[ba][:, 0:n], wb[:, k, 0:128], rhs_fn(k), start=(k == 0), stop=(k == 15))
                    for k in range(16):
                        i = e.matmul(pb[bg][:, 0:n], wb[:, k, 128:256], rhs_fn(k), start=(k == 0), stop=(k == 15))
                    return i
                A("pe", mm, r=[wk, wk + ("g",)] + rk, w=[PB(ba), PB(bg)])
                sg = sig[tb % 2]
                sk = ("sig", tb % 2)
                A("act", lambda e, sg=sg, bg=bg, n=n: e.activation(out=sg[:, 0:n], in_=pb[bg][:, 0:n], func=AF.Sigmoid),
                  r=[PB(bg)], w=[sk])
                if tb < 4:
                    A("dve", lambda e, sg=sg, ba=ba, ch=ch, tb=tb: e.tensor_tensor(
                        out=uT[:, ch, 16 + tb * 512:16 + (tb + 1) * 512], in0=pb[ba][:], in1=sg[:], op=ALU.mult),
                      r=[PB(ba), sk], w=[("uT", ch, tb)])
                else:
                    A("dve", lambda e, sg=sg, ba=ba: e.tensor_tensor(out=uh[:], in0=pb[ba][:, 0:32], in1=sg[:, 0:32], op=ALU.mult),
                      r=[PB(ba), sk], w=["uh"])
                    A("dve", lambda e, ch=ch: e.tensor_scalar(out=uT[:, ch, 0:16], in0=uh[:, 0:16], scalar1=hmask[:, 0:1], scalar2=0.0,
                                                             op0=ALU.mult), r=["uh", "hmask"], w=[("uT", ch, "lo")])
                    A("dve", lambda e, ch=ch: e.tensor_scalar(out=uT[:, ch, 2064:2080], in0=uh[:, 16:32], scalar1=hmask[:, 1:2], scalar2=0.0,
                                                             op0=ALU.mult), r=["uh", "hmask"], w=[("uT", ch, "hi")])
        if stop_after == "A0b":
            A("sp", lambda e: e.dma_start(out=dbg["gen"][:, 0:2080], in_=uT[:, 0, :]),
              r=[("uT", 0, q) for q in (0, 1, 2, 3, "lo", "hi")] + [("cqnT", m, tb) for m in range(4) for tb in range(4)], w=["dbg_gen"], dma=True)
            dbg_keys.append("dbg_gen")
            S.enabled = False
        S.barrier()
        cv = Carver(P0)
        diag = cv.get([128, 8, 31, 128], BF16)
        assert cv.off <= P0 + 65 * 1024
        cv = Carver(R1)
        cw_sb = cv.get([128, 1024], F32)
        cwT = cv.get([128, 8, 31], F32)
        cw_bf = cv.get([128, 1024], BF16)
        ycv = cv.get([128, 8, 512], F32)
        ybf = cv.get([128, 8, 512], BF16)
        ysq = cv.get([128, 8, 512], BF16)
        mean = cv.get([128, 512], F32)
        ex2 = cv.get([128, 512], F32)
        rsc = cv.get([128, 512], F32)
        costg = [cv.get([128, 8, 512], BF16) for _ in range(2)]
        A("sp", lambda e: e.dma_start(out=cw_sb[0:31, :], in_=conv_w[:, :]), w=["cw_sb"], dma=True)
        A("dve", lambda e: e.tensor_copy(cw_bf[0:31, :], cw_sb[0:31, :]), r=["cw_sb"], w=["cw_bf"])
        pv0 = pb[0][:].bitcast(BF16)
        for ch in range(8):
            A("pe", lambda e, ch=ch: e.transpose(pv0[:, ch * 32:ch * 32 + 31], cw_bf[0:31, ch * 128:(ch + 1) * 128], ident_b[0:31, 0:31]),
              r=["cw_bf", "ident_b"], w=[PB(0)])
        A("dve", lambda e: e.tensor_copy(cwT[:], pv0[:, 0:256].rearrange("p (c k) -> p c k", k=32)[:, :, 0:31]),
          r=[PB(0)], w=["cwT"])
        for ch in range(8):
            A("dve", lambda e, ch=ch: e.tensor_tensor(out=diag[:, ch, :, :], in0=ident_b[:].unsqueeze(1).to_broadcast([128, 31, 128]),
                                                     in1=cwT[:, ch, :].unsqueeze(2).to_broadcast([128, 31, 128]), op=ALU.mult),
              r=["ident_b", "cwT"], w=[("diag", ch)])
        for tb in range(4):
            for ch in range(8):
                bk = 1 + (ch % 4)

                def mm(e, ch=ch, tb=tb, bk=bk):
                    for k in range(31):
                        i = e.matmul(pb[bk][:], diag[:, ch, k, :], uT[:, ch, tb * 512 + k + 1: tb * 512 + k + 1 + 512],
                                     start=(k == 0), stop=(k == 30))
                    return i
                A("pe", mm, r=[("diag", ch)] + [("uT", ch, q) for q in (0, 1, 2, 3, "lo", "hi")], w=[PB(bk)])
                A("act", lambda e, ch=ch, bk=bk: e.activation(out=ycv[:, ch, :], in_=pb[bk][:], func=AF.Identity, bias=cbT[:, ch:ch + 1], scale=1.0),
                  r=[PB(bk), "cbT"], w=[("ycv", ch)])
                A("act", lambda e, ch=ch: e.activation(out=ysq[:, ch, :], in_=ycv[:, ch, :], func=AF.Square), r=[("ycv", ch)], w=[("ysq", ch)])
                A("dve", lambda e, ch=ch: e.tensor_copy(ybf[:, ch, :], ycv[:, ch, :]), r=[("ycv", ch)], w=[("ybf", ch)])

            def mm1(e):
                for ch in range(8):
                    i = e.matmul(pb[5][:], ones_b[:], ybf[:, ch, :], start=(ch == 0), stop=(ch == 7))
                return i

            def mm2(e):
                for ch in range(8):
                    i = e.matmul(pb[6][:], ones_b[:], ysq[:, ch, :], start=(ch == 0), stop=(ch == 7))
                return i
            A("pe", mm1, r=[("ybf", ch) for ch in range(8)] + ["ones_b"], w=[PB(5)])
            A("pe", mm2, r=[("ysq", ch) for ch in range(8)] + ["ones_b"], w=[PB(6)])
            A("act", lambda e: e.activation(out=mean[:], in_=pb[5][:], func=AF.Identity, scale=1.0 / 1024.0, bias=cst[:, 2:3]),
              r=[PB(5), "cst2"], w=["mean"])
            A("act", lambda e: e.activation(out=ex2[:], in_=pb[6][:], func=AF.Identity, scale=1.0 / 1024.0, bias=cst[:, 2:3]),
              r=[PB(6), "cst2"], w=["ex2"])
            A("dve", lambda e: e.tensor_tensor(out=rsc[:], in0=mean[:], in1=mean[:], op=ALU.mult), r=["mean"], w=["rsc"])
            A("dve", lambda e: e.tensor_tensor(out=ex2[:], in0=ex2[:], in1=rsc[:], op=ALU.subtract), r=["ex2", "rsc"], w=["ex2"])
            A("act", lambda e: e.activation(out=rsc[:], in_=ex2[:], func=AF.Sqrt, bias=cst[:, 0:1], scale=1.0), r=["ex2", "cst0"], w=["rsc"])
            A("dve", lambda e: e.reciprocal(out=rsc[:], in_=rsc[:]), r=["rsc"], w=["rsc"])
            A("dve", lambda e: e.tensor_tensor(out=ycv[:], in0=ycv[:], in1=mean[:].unsqueeze(1).to_broadcast([128, 8, 512]), op=ALU.subtract),
              r=[("ycv", ch) for ch in range(8)] + ["mean"], w=[("ycv", ch) for ch in range(8)])
            A("dve", lambda e: e.tensor_tensor(out=ycv[:], in0=ycv[:], in1=rsc[:].unsqueeze(1).to_broadcast([128, 8, 512]), op=ALU.mult),
              r=[("ycv", ch) for ch in range(8)] + ["rsc"], w=[("ycv", ch) for ch in range(8)])
            cs = costg[tb % 2]
            ck = ("costg", tb % 2)
            for ch in range(8):
                A("act", lambda e, ch=ch, cs=cs: e.activation(out=cs[:, ch, :], in_=ycv[:, ch, :], func=AF.Silu, scale=gclT[:, ch:ch + 1],
                                                             bias=bclT[:, ch:ch + 1]), r=[("ycv", ch), "gclT", "bclT"], w=[ck + (ch,)])
            A("sp", lambda e, cs=cs, tb=tb: e.dma_start(out=catT_d[1024:2048, tb * 512:(tb + 1) * 512].rearrange("(c p) n -> p c n", p=128), in_=cs[:]),
              r=[ck + (ch,) for ch in range(8)], w=[("catT_d", "conv", tb)], dma=True)
        S.barrier()
        if stop_after is not None and S.enabled:
            A("sp", lambda e: e.dma_start(out=dbg["cqnT"][:, :], in_=view(OFF_CQN, [128, 4 * NTOK], BF16)),
              r=[("cqnT", m, tb) for m in range(4) for tb in range(4)], w=["dbg_cqnT"], dma=True)
            dbg_keys.append("dbg_cqnT")
        PH = {None: 9, "C0": -1, "A0g": -1, "A0a": -1, "A0b": -1, "A0": 0, "A1": 1, "B": 2, "C": 3, "D": 4}[stop_after]
        def stop_here(k):
            if PH != k or not S.enabled:
                return
            CATD_ = [kk for kk in S.bufs if isinstance(kk, tuple) and kk[0] == "catT_d"]
            if k == 1:
                A("sp", lambda e: e.dma_start(out=dbg["ckvnT"][:, :], in_=ckvnT.rearrange("p a b -> p (a b)")),
                  r=[("ckvnT", m, j) for m in range(4) for j in range(16)], w=["dbg_ckvnT"], dma=True)
                A("sp", lambda e: e.dma_start(out=dbg["krT"][:, :], in_=krT), r=[("krT", j) for j in range(16)], w=["dbg_krT"], dma=True)
                A("sp", lambda e: e.dma_start(out=dbg["Tq"][:, :], in_=Tq), r=[("Tq", j, s_) for j in range(4) for s_ in range(2)], w=["dbg_Tq"], dma=True)
                dbg_keys.extend(["dbg_ckvnT", "dbg_krT", "dbg_Tq"])
            if k in (0, 2):
                lo = 1024 if k == 0 else 0
                A("sp", lambda e: e.dma_start(out=dbg["catT"][lo:2048, :], in_=catT_d[lo:2048, :]), r=CATD_, w=["dbg_catT"], dma=True)
                dbg_keys.append("dbg_catT")
            if k == 3:
                A("sp", lambda e: e.dma_start(out=dbg["x1"][:, :], in_=x1_d[:, :]), r=[("x1_d", i) for i in range(16)], w=["dbg_x1"], dma=True)
                dbg_keys.append("dbg_x1")
            S.enabled = False

        stop_here(0)

        cv = Carver(16 * 1024)
        ckvnT = cv.get([128, 4, SEQ], BF16)
        krT = cv.get([128, SEQ], BF16)
        Tq = cv.get([128, NTOK], F32)
        PB_BASE = cv.off
        xts = [(cv.get([128, D], F32), ("xt", i)) for i in range(2)]
        xn = cv.get([128, 4, D], BF16)
        x0Tb = cv.get([128, 16, 512], BF16)
        wkv = cv.get([128, 16, 768], BF16)
        tmpB = {"sq": cv.get([128, 4, 512], BF16),
                "sdv": cv.get([128, 512], F32), "rs": cv.get([128, 512], F32)}
        posi = cv.get([128, 512], I32)
        u2 = cv.get([128, 2, 512], F32)
        k2i = cv.get([128, 2, 512], I32)
        k2f = cv.get([128, 2, 512], F32)
        tab = u2
        t1 = cv.get([128, 512], F32)
        A("pool", lambda e: e.dma_start(out=wkv[:, :, 0:512], in_=w_in_r[:, :, 512:1024]), w=["wkv0"], dma=True)
        for hcol, src in ((512, 1024), (576, 1024), (640, 1056), (672, 1024), (704, 1056), (736, 1024)):
            wdt = 64 if hcol < 640 else 32
            A("pool", lambda e, hcol=hcol, src=src, wdt=wdt: e.dma_start(out=wkv[:, :, hcol:hcol + wdt], in_=w_in_r[:, :, src:src + wdt]),
              w=[("wkv", hcol)], dma=True)
        WKV = ["wkv0"] + [("wkv", h_) for h_ in (512, 576, 640, 672, 704, 736)]
        ln_part(0, xts, xn, "xnB")
        for j in range(16):
            tr_part(xn, "xnB", x0Tb, "x0Tb", (6, 7))
            xk = [("x0Tb", c) for c in range(16)]
            for m in range(6):
                def mm(e, m=m):
                    for k in range(16):
                        i = e.matmul(pb[m][:], wkv[:, k, m * 128:(m + 1) * 128], x0Tb[:, k, :], start=(k == 0), stop=(k == 15))
                    return i
                A("pe", mm, r=WKV + xk, w=[PB(m)])
            if j + 1 < 16:
                ln_part((j + 1) * 512, xts, xn, "xnB")
            A("sp", lambda e, j=j: e.dma_start(out=posi[:], in_=pos[0:1, j * 512:(j + 1) * 512].partition_broadcast(128)), w=["posi"], dma=True)
            A("dve", lambda e: e.tensor_copy(t1[:], posi[:]), r=["posi"], w=["t1"])
            A("dve", lambda e: e.tensor_scalar(out=u2[:, 0, :], in0=t1[:], scalar1=rc[:, 0:1], scalar2=0.25, op0=ALU.mult, op1=ALU.add),
              r=["t1", "rc"], w=["u2a"])
            A("dve", lambda e: e.tensor_scalar(out=u2[:, 1, :], in0=t1[:], scalar1=rc[:, 0:1], scalar2=0.0, op0=ALU.mult, op1=ALU.add),
              r=["t1", "rc"], w=["u2b"])
            A("dve", lambda e: e.tensor_copy(k2i[:], u2[:]), r=["u2a", "u2b"], w=["k2i"])
            A("dve", lambda e: e.tensor_copy(k2f[:], k2i[:]), r=["k2i"], w=["k2f"])
            A("dve", lambda e: e.tensor_tensor(out=u2[:], in0=u2[:], in1=k2f[:], op=ALU.subtract), r=["u2a", "u2b", "k2f"], w=["u2a", "u2b"])
            A("dve", lambda e: e.scalar_tensor_tensor(out=k2f[:], in0=u2[:], scalar=0.5, in1=u2[:], op0=ALU.is_gt, op1=ALU.subtract),
              r=["u2a", "u2b"], w=["k2f"])
            A("act", lambda e: e.activation(out=tab[:, 0, :], in_=k2f[:, 0, :], func=AF.Sin, scale=-2.0 * math.pi), r=["k2f", "u2a"], w=["u2a"])
            A("act", lambda e: e.activation(out=tab[:, 1, :], in_=k2f[:, 1, :], func=AF.Sin, scale=rc[:, 1:2]), r=["k2f", "rc", "u2b"], w=["u2b"])
            if j < 4:
                A("pool", lambda e, j=j: e.tensor_copy(Tq[0:64, j * 512:(j + 1) * 512], tab[0:64, 0, :]), r=["u2a"], w=[("Tq", j, 0)])
                A("pool", lambda e, j=j: e.tensor_copy(Tq[64:128, j * 512:(j + 1) * 512], tab[64:128, 1, :]), r=["u2b"], w=[("Tq", j, 1)])
            A("dve", lambda e: e.tensor_tensor(out=t1[:], in0=pb[4][:], in1=tab[:, 0, :], op=ALU.mult), r=[PB(4), "u2a"], w=["t1"])
            A("dve", lambda e: e.tensor_tensor(out=k2f[:, 0, :], in0=pb[5][:], in1=tab[:, 1, :], op=ALU.mult), r=[PB(5), "u2b", "k2f"], w=["k2f"])
            A("dve", lambda e, j=j: e.tensor_tensor(out=krT[:, j * 512:(j + 1) * 512], in0=t1[:], in1=k2f[:, 0, :], op=ALU.add),
              r=["t1", "k2f"], w=[("krT", j)])
            rms_block([0, 1, 2, 3], gckvT, "gckvT", lambda m, j=j: ckvnT[:, m, j * 512:(j + 1) * 512],
                      [("ckvnT", m, j) for m in range(4)], tmpB, "tB", 4)
        S.barrier()
        stop_here(1)

        W1B = [("w1b", i) for i in range(8)]
        W2B = [("w2b", i) for i in range(8)]
        cv = Carver(PB_BASE)
        KT = cv.get([128, SEQ], BF16)
        Vh = cv.get([128, 64, 128], BF16)
        qnT = cv.get([128, NTOK], BF16)
        qrT = cv.get([128, NTOK], BF16)
        wq = [cv.get([128, 4, 256], BF16) for _ in range(2)]
        wk_ = [cv.get([128, 4, 128], BF16) for _ in range(2)]
        wv_ = [cv.get([128, 4, 128], BF16) for _ in range(2)]
        NPT = 6
        PT = [cv.get([128, 512], BF16) for _ in range(NPT)]
        rec = cv.get([128, 512], F32)
        ostg = [cv.get([128, 512], BF16) for _ in range(2)]
        w_uq_r = w_uq.rearrange("(k p) n -> p k n", p=128)
        w_uk_r = w_uk.rearrange("(k p) n -> p k n", p=128)
        w_uv_r = w_uv.rearrange("(k p) n -> p k n", p=128)
        CQ = [("cqnT", m, tb) for m in range(4) for tb in range(4)]
        CKV = lambda j: [("ckvnT", m, j) for m in range(4)]
        TQK = [("Tq", j, s_) for j in range(4) for s_ in range(2)]
        sctr = [0]
        ectr = [0]
        for h in range(8):
            p2 = h % 2
            c0 = h * 192
            for (dst0, src0, wdt) in ((0, c0, 128), (128, c0 + 128, 64), (192, c0 + 160, 32), (224, c0 + 128, 32)):
                A("pool", lambda e, p2=p2, dst0=dst0, src0=src0, wdt=wdt: e.dma_start(out=wq[p2][:, :, dst0:dst0 + wdt], in_=w_uq_r[:, :, src0:src0 + wdt]),
                  w=[("wq", p2, dst0)], dma=True)
            A("pool", lambda e, p2=p2, h=h: e.dma_start(out=wk_[p2], in_=w_uk_r[:, :, h * 128:(h + 1) * 128]), w=[("wk", p2)], dma=True)
            A("pool", lambda e, p2=p2, h=h: e.dma_start(out=wv_[p2], in_=w_uv_r[:, :, h * 128:(h + 1) * 128]), w=[("wv", p2)], dma=True)
            WQ = [("wq", p2, d_) for d_ in (0, 128, 192, 224)]
            A("pool", lambda e, i=h: e.dma_start(out=w1b_d[i * 256:(i + 1) * 256, :], in_=w_ff1[i * 256:(i + 1) * 256, :]),
              w=[("w1b", h)], dma=True)
            A("pool", lambda e, i=h: e.dma_start(out=w2b_d[i * 1024:(i + 1) * 1024, :], in_=w_ff2[i * 1024:(i + 1) * 1024, :]),
              w=[("w2b", h)], dma=True)

            def evac(out_ap, bank, rk, wkeys):
                i = ectr[0]
                ectr[0] += 1
                if i % 2 == 0:
                    A("act", lambda e: e.activation(out=out_ap, in_=pb[bank][:], func=AF.Identity, bias=cst[:, 2:3], scale=1.0),
                      r=[PB(bank), "cst2"] + rk, w=wkeys)
                else:
                    A("dve", lambda e: e.tensor_copy(out_ap, pb[bank][:]), r=[PB(bank)] + rk, w=wkeys)

            for tb in range(4):
                bk = sctr[0] % 4
                sctr[0] += 1

                def mm(e, tb=tb, bk=bk, p2=p2):
                    for k in range(4):
                        i = e.matmul(pb[bk][:], wq[p2][:, k, 0:128], cqnT[:, k, tb * 512:(tb + 1) * 512], start=(k == 0), stop=(k == 3))
                    return i
                A("pe", mm, r=WQ + CQ, w=[PB(bk)])
                evac(qnT[:, tb * 512:(tb + 1) * 512], bk, [], [("qnT", tb)])
                bk = sctr[0] % 4
                sctr[0] += 1

                def mm(e, tb=tb, bk=bk, p2=p2):
                    for k in range(4):
                        i = e.matmul(pb[bk][:], wq[p2][:, k, 128:256], cqnT[:, k, tb * 512:(tb + 1) * 512], start=(k == 0), stop=(k == 3))
                    return i
                A("pe", mm, r=WQ + CQ, w=[PB(bk)])
                A("dve", lambda e, tb=tb, bk=bk: e.tensor_tensor(out=qrT[:, tb * 512:(tb + 1) * 512], in0=pb[bk][:], in1=Tq[:, tb * 512:(tb + 1) * 512], op=ALU.mult),
                  r=[PB(bk)] + TQK, w=[("qrT", tb)])
            for j in range(16):
                bk = sctr[0] % 4
                sctr[0] += 1

                def mm(e, j=j, bk=bk, p2=p2):
                    for k in range(4):
                        i = e.matmul(pb[bk][:], wk_[p2][:, k, :], ckvnT[:, k, j * 512:(j + 1) * 512], start=(k == 0), stop=(k == 3))
                    return i
                A("pe", mm, r=[("wk", p2)] + CKV(j), w=[PB(bk)])
                evac(KT[:, j * 512:(j + 1) * 512], bk, [], [("KT", j)])
                bk = sctr[0] % 4
                sctr[0] += 1

                def mm(e, j=j, bk=bk, p2=p2):
                    for i4 in range(4):
                        kt = j * 4 + i4
                        for k in range(4):
                            i = e.matmul(pb[bk][:, i4 * 128:(i4 + 1) * 128], ckvnT[:, k, kt * 128:(kt + 1) * 128], wv_[p2][:, k, :],
                                         start=(k == 0), stop=(k == 3))
                    return i
                A("pe", mm, r=[("wv", p2)] + CKV(j), w=[PB(bk)])
                evac(Vh[:, j * 4:(j + 1) * 4, :].rearrange("p a b -> p (a b)"), bk, [], [("Vh", j)])
            QK = [("qnT", tb) for tb in range(4)] + [("qrT", tb) for tb in range(4)]
            for qb in range(4):
                ob = 4 + (qb % 2) * 2
                db = ob + 1
                qs = slice(qb * 512, (qb + 1) * 512)
                sb_of = {}

                def score(kt, qs=qs, qb=qb, sb_of=sb_of):
                    bk = sctr[0] % 4
                    sctr[0] += 1
                    sb_of[kt] = bk

                    def mm(e, kt=kt, bk=bk, qs=qs):
                        e.matmul(pb[bk][:], KT[:, kt * 128:(kt + 1) * 128], qnT[:, qs], start=True, stop=False)
                        return e.matmul(pb[bk][:], krT[:, kt * 128:(kt + 1) * 128], qrT[:, qs], start=False, stop=True)
                    A("pe", mm, r=[("KT", kt // 4), ("krT", kt // 4), ("qnT", qb), ("qrT", qb)], w=[PB(bk)])

                score(0)
                score(1)
                for kt in range(64):
                    if kt + 2 < 64:
                        score(kt + 2)
                    bk = sb_of[kt]
                    pt = PT[kt % NPT]
                    pk = ("PT", kt % NPT)
                    A("act", lambda e, bk=bk, pt=pt: e.activation(out=pt[:], in_=pb[bk][:], func=AF.Exp, scale=SCALE, bias=cst[:, 2:3]),
                      r=[PB(bk), "cst2"], w=[pk])

                    def mm(e, kt=kt, pt=pt, ob=ob, db=db):
                        e.matmul(pb[ob][:], Vh[:, kt, :], pt[:], start=(kt == 0), stop=(kt == 63))
                        return e.matmul(pb[db][:], ones_b[:], pt[:], start=(kt == 0), stop=(kt == 63))
                    A("pe", mm, r=[("Vh", kt // 4), pk, "ones_b"], w=[PB(ob), PB(db)])
                og = ostg[qb % 2]
                ok = ("ostg", qb % 2)
                A("dve", lambda e, db=db: e.reciprocal(out=rec[:], in_=pb[db][:]), r=[PB(db)], w=["rec"])
                A("dve", lambda e, ob=ob, og=og: e.tensor_tensor(out=og[:], in0=pb[ob][:], in1=rec[:], op=ALU.mult), r=[PB(ob), "rec"], w=[ok])
                A("sp", lambda e, og=og, h=h, qb=qb: e.dma_start(out=catT_d[h * 128:(h + 1) * 128, qb * 512:(qb + 1) * 512], in_=og[:]),
                  r=[ok], w=[("catT_d", "att", h, qb)], dma=True)
        S.barrier()
        stop_here(2)

        cv = Carver(0)
        catTb = [cv.get([128, 16, 512], BF16) for _ in range(2)]
        wo = cv.get([128, 16, D], BF16)
        gA = cv.get([128, D], F32)
        bAf = cv.get([128, D], F32)
        bA = cv.get([128, D], BF16)
        xts = [(cv.get([128, D], F32), ("xt", i)) for i in range(3)]
        x0c = [cv.get([128, D], F32) for _ in range(3)]
        x1b = [cv.get([128, D], BF16) for _ in range(2)]
        x1Tst = cv.get([128, 16, 512], BF16)
        CATD = [k for k in S.bufs if isinstance(k, tuple) and k[0] == "catT_d"]
        w_out_r = w_out.rearrange("(k p) n -> p k n", p=128)
        for q4 in range(4):
            A("pool", lambda e, q4=q4: e.dma_start(out=wo[:, q4 * 4:(q4 + 1) * 4, :], in_=w_out_r[:, q4 * 4:(q4 + 1) * 4, :]), w=[("wo", q4)], dma=True)
        WO = [("wo", q4) for q4 in range(4)]
        A("sp", lambda e: e.dma_start(out=gA, in_=ln_in_g[0:1, :].partition_broadcast(128)), w=["gA"], dma=True)
        A("sp", lambda e: e.dma_start(out=bAf[0:1, :], in_=ln_in_b[0:1, :]), w=["bAf"], dma=True)
        A("dve", lambda e: e.tensor_scalar(out=gA, in0=gA, scalar1=ALPHA, scalar2=0.0, op0=ALU.mult, op1=ALU.add), r=["gA"], w=["gA"])
        A("dve", lambda e: e.tensor_scalar(out=bA[0:1, :], in0=bAf[0:1, :], scalar1=ALPHA, scalar2=0.0, op0=ALU.mult, op1=ALU.add), r=["bAf"], w=["bA"])

        def ln_stats(src, skeys, sl):
            def bns(e):
                for c in range(4):
                    i = e.bn_stats(out=stats[:, sl, c * 6:(c + 1) * 6], in_=src[:, c * 512:(c + 1) * 512])
                return i
            A("dve", bns, r=skeys, w=[("stats", sl)])
            A("dve", lambda e: e.bn_aggr(out=mv[:, sl, :], in_=stats[:, sl, :]), r=[("stats", sl)], w=[("mv", sl)])
            A("act", lambda e: e.activation(out=sd[:, sl:sl + 1], in_=mv[:, sl, 1:2], func=AF.Sqrt, bias=cst[:, 0:1], scale=1.0),
              r=[("mv", sl), "cst0"], w=[("sd", sl)])
            A("dve", lambda e: e.reciprocal(out=rstd[:, sl:sl + 1], in_=sd[:, sl:sl + 1]), r=[("sd", sl)], w=[("rstd", sl)])
            A("dve", lambda e: e.scalar_tensor_tensor(out=nmr[:, sl:sl + 1], in0=mv[:, sl, 0:1], scalar=-1.0, in1=rstd[:, sl:sl + 1],
                                                      op0=ALU.mult, op1=ALU.mult), r=[("mv", sl), ("rstd", sl)], w=[("nmr", sl)])

        def c_s1(i):
            xb, xk = xts[i % 3]
            sl = i % 2
            xc = x0c[i % 3]
            xck = ("x0c", i % 3)
            ln_stats(xb, [xk], sl)
            A("act", lambda e: e.activation(out=xc, in_=xb, func=AF.Identity, scale=rstd[:, sl:sl + 1], bias=nmr[:, sl:sl + 1]),
              r=[xk, ("rstd", sl), ("nmr", sl)], w=[xck])
            A("dve", lambda e: e.tensor_tensor(out=xc, in0=xc, in1=gA, op=ALU.mult), r=[xck, "gA"], w=[xck])

        def c_mix(i):
            catT = catTb[(i // 4) % 2]
            CATK = [("catTb", (i // 4) % 2)]
            if i % 4 == 0:
                A("sp", lambda e, catT=catT, i=i: e.dma_start(out=catT, in_=catT_d[:, (i // 4) * 512:(i // 4 + 1) * 512].rearrange("(c p) n -> p c n", p=128)),
                  r=CATD, w=CATK, dma=True)
            for cb in range(4):
                def mm(e, i=i, cb=cb, catT=catT):
                    e.matmul(pb[cb][:], ones_b[0:1, :], bA[0:1, cb * 512:(cb + 1) * 512], start=True, stop=False)
                    for k in range(16):
                        r_ = e.matmul(pb[cb][:], catT[:, k, (i % 4) * 128:(i % 4 + 1) * 128], wo[:, k, cb * 512:(cb + 1) * 512], start=False, stop=(k == 15))
                    return r_
                A("pe", mm, r=CATK + WO + ["bA", "ones_b"], w=[PB(cb)])

        def c_r(i):
            xc = x0c[i % 3]
            xck = ("x0c", i % 3)
            for cb in range(4):
                A("dve", lambda e, xc=xc, cb=cb: e.tensor_tensor(out=xc[:, cb * 512:(cb + 1) * 512], in0=xc[:, cb * 512:(cb + 1) * 512],
                                                                  in1=pb[cb][:], op=ALU.add), r=[xck, PB(cb)], w=[xck])

        def c_load(i):
            xb, xk = xts[i % 3]
            A("sp", lambda e, xb=xb, i=i: e.dma_start(out=xb, in_=x[i * 128:(i + 1) * 128, :]), w=[xk], dma=True)

        c_load(0)
        c_load(1)
        c_load(2)
        c_s1(0)
        c_s1(1)
        c_mix(0)
        c_r(0)
        for i in range(16):
            if i + 3 < 16:
                c_load(i + 3)
            xc = x0c[i % 3]
            xck = ("x0c", i % 3)
            xbf = x1b[i % 2]
            xbk = ("x1b", i % 2)
            sl = 2 + i % 2
            if i + 1 < 16:
                c_mix(i + 1)
            ln_stats(xc, [xck], sl)
            A("act", lambda e, xc=xc, xbf=xbf, sl=sl: e.activation(out=xbf[:], in_=xc, func=AF.Identity, scale=rstd[:, sl:sl + 1], bias=nmr[:, sl:sl + 1]),
              r=[xck, ("rstd", sl), ("nmr", sl)], w=[xbk])
            A("act", lambda e, xc=xc, sl=sl: e.activation(out=xc, in_=xc, func=AF.Identity, scale=rstd[:, sl:sl + 1], bias=nmr[:, sl:sl + 1]),
              r=[xck, ("rstd", sl), ("nmr", sl)], w=[xck])
            A("sp", lambda e, xc=xc, i=i: e.dma_start(out=x1_d[i * 128:(i + 1) * 128, :], in_=xc), r=[xck], w=[("x1_d", i)], dma=True)
            if i + 2 < 16:
                c_s1(i + 2)
            trs = []
            for c4 in range(4):
                bank = 4 + c4
                pv = pb[bank][:].bitcast(BF16)[:, 0:512]

                def tr(e, c4=c4, pv=pv, xbf=xbf):
                    for cc in range(4):
                        c = c4 * 4 + cc
                        r_ = e.transpose(pv[:, cc * 128:(cc + 1) * 128], xbf[:, c * 128:(c + 1) * 128], ident_b[:])
                    return r_
                A("pe", tr, r=[xbk, "ident_b"], w=[PB(bank)])
                trs.append((c4, bank, pv))
            if i + 1 < 16:
                c_r(i + 1)
            for (c4, bank, pv) in trs:
                for cc in range(4):
                    c = c4 * 4 + cc
                    dstv = x1Tst[:, c, (i % 4) * 128:(i % 4 + 1) * 128]
                    srcv = pv[:, cc * 128:(cc + 1) * 128]
                    if c4 % 2 == 0:
                        A("act", lambda e, srcv=srcv, dstv=dstv, c=c: e.activation(out=dstv, in_=srcv, func=AF.Identity, scale=g1T[:, c:c + 1], bias=b1T[:, c:c + 1]),
                          r=[PB(bank), "g1T", "b1T"], w=[("x1Tst", c, i % 4)])
                    else:
                        A("dve", lambda e, srcv=srcv, dstv=dstv, c=c: e.tensor_scalar(out=dstv, in0=srcv, scalar1=g1T[:, c:c + 1], scalar2=b1T[:, c:c + 1],
                                                                                     op0=ALU.mult, op1=ALU.add),
                          r=[PB(bank), "g1T", "b1T"], w=[("x1Tst", c, i % 4)])
            if i % 4 == 3:
                tb = i // 4
                A("sp", lambda e, tb=tb: e.dma_start(out=x1T_d[:, tb * 512:(tb + 1) * 512].rearrange("(c p) n -> p c n", p=128), in_=x1Tst[:]),
                  r=[("x1Tst", c, t_) for c in range(16) for t_ in range(4)], w=[("x1T_d", tb)], dma=True)
        S.barrier()
        stop_here(3)

        cv = Carver(0)
        h1T = cv.get([128, 64, 512], BF16)
        x1Tb = [cv.get([128, 16, 512], BF16) for _ in range(1)]
        xres = [cv.get([128, D], F32) for _ in range(4)]
        gb2 = [cv.get([128, D], F32) for _ in range(4)]
        NW1 = 4
        w1 = [cv.get([128, 16, 256], BF16) for _ in range(NW1)]
        NW2 = 4
        w2 = [cv.get([128, 2, 1024], BF16) for _ in range(NW2)]
        rl = [cv.get([128, 512], F32) for _ in range(3)]
        for i, src in enumerate((g_ln1, b_ln1, g_ln2, b_ln2)):
            A("sp", lambda e, i=i, src=src: e.dma_start(out=gb2[i], in_=src[0:1, :].partition_broadcast(128)), w=[("gb2", i)], dma=True)
        for i in range(2):
            A("dve", lambda e, i=i: e.tensor_scalar(out=gb2[i], in0=gb2[i], scalar1=ALPHA, scalar2=0.0, op0=ALU.mult, op1=ALU.add),
              r=[("gb2", i)], w=[("gb2", i)])
        w1b_r = w1b_d.rearrange("(k p) f -> p k f", p=128)
        w2b_r = w2b_d.rearrange("(c p) d -> p c d", p=128)
        w1c = [0]
        w2c = [0]
        rlc = [0]
        for tb in range(4):
            xT = x1Tb[0]
            xTk = ("x1Tb", 0)
            A("sp", lambda e, xT=xT, tb=tb: e.dma_start(out=xT, in_=x1T_d[:, tb * 512:(tb + 1) * 512].rearrange("(c p) n -> p c n", p=128)),
              r=[("x1T_d", tb)], w=[xTk], dma=True)
            for t in range(4):
                A("sp", lambda e, t=t, tb=tb: e.dma_start(out=xres[t], in_=x1_d[(tb * 4 + t) * 128:(tb * 4 + t + 1) * 128, :]),
                  r=[("x1_d", tb * 4 + t)], w=[("xres", t)], dma=True)
                A("dve", lambda e, t=t: e.tensor_tensor(out=xres[t], in0=xres[t], in1=gb2[0], op=ALU.mult), r=[("xres", t), ("gb2", 0)], w=[("xres", t)])
                A("pool", lambda e, t=t: e.tensor_tensor(out=xres[t], in0=xres[t], in1=gb2[1], op=ALU.add), r=[("xres", t), ("gb2", 1)], w=[("xres", t)])
            for fg in range(32):
                wi = w1c[0] % NW1
                w1c[0] += 1
                A("sp", lambda e, wi=wi, fg=fg: e.dma_start(out=w1[wi], in_=w1b_r[:, :, fg * 256:(fg + 1) * 256]), r=W1B, w=[("w1", wi)], dma=True)
                for fc in range(2):
                    f = fg * 2 + fc
                    bk = f % 4

                    def mm(e, wi=wi, fc=fc, bk=bk, xT=xT):
                        for k in range(16):
                            r_ = e.matmul(pb[bk][:], w1[wi][:, k, fc * 128:(fc + 1) * 128], xT[:, k, :], start=(k == 0), stop=(k == 15))
                        return r_
                    A("pe", mm, r=[("w1", wi), xTk], w=[PB(bk)])
                    ri = rlc[0] % 3
                    rlc[0] += 1
                    A("act", lambda e, bk=bk, ri=ri: e.activation(out=rl[ri][:], in_=pb[bk][:], func=AF.Relu, bias=cst[:, 2:3], scale=1.0),
                      r=[PB(bk), "cst2"], w=[("rl", ri)])
                    A("dve", lambda e, ri=ri, f=f: e.tensor_tensor(out=h1T[:, f, :], in0=rl[ri][:], in1=rl[ri][:], op=ALU.mult),
                      r=[("rl", ri)], w=[("h1T", f)])
            for ps_ in range(2):
                for fg in range(32):
                    wi = w2c[0] % NW2
                    w2c[0] += 1
                    A("sp", lambda e, wi=wi, fg=fg, ps_=ps_: e.dma_start(out=w2[wi], in_=w2b_r[:, fg * 2:(fg + 1) * 2, ps_ * 1024:(ps_ + 1) * 1024]),
                      r=W2B, w=[("w2", wi)], dma=True)

                    def mm(e, wi=wi, fg=fg):
                        for fc in range(2):
                            f = fg * 2 + fc
                            for t in range(4):
                                for cb in range(2):
                                    r_ = e.matmul(pb[t * 2 + cb][:], h1T[:, f, t * 128:(t + 1) * 128], w2[wi][:, fc, cb * 512:(cb + 1) * 512],
                                                  start=(f == 0), stop=(f == 63))
                        return r_
                    A("pe", mm, r=[("w2", wi), ("h1T", fg * 2), ("h1T", fg * 2 + 1)], w=[PB(b_) for b_ in range(8)])
                for t in range(4):
                    for cb in range(2):
                        col = ps_ * 1024 + cb * 512
                        A("dve", lambda e, t=t, cb=cb, col=col: e.tensor_tensor(out=xres[t][:, col:col + 512], in0=xres[t][:, col:col + 512],
                                                                                in1=pb[t * 2 + cb][:], op=ALU.add),
                          r=[("xres", t), PB(t * 2 + cb)], w=[("xres", t)])
            for t in range(4):
                ln_stats(xres[t], [("xres", t)], t)
                A("act", lambda e, t=t: e.activation(out=xres[t], in_=xres[t], func=AF.Identity, scale=rstd[:, t:t + 1], bias=nmr[:, t:t + 1]),
                  r=[("xres", t), ("rstd", t), ("nmr", t)], w=[("xres", t)])
                A("dve", lambda e, t=t: e.tensor_tensor(out=xres[t], in0=xres[t], in1=gb2[2], op=ALU.mult), r=[("xres", t), ("gb2", 2)], w=[("xres", t)])
                A("pool", lambda e, t=t: e.tensor_tensor(out=xres[t], in0=xres[t], in1=gb2[3], op=ALU.add), r=[("xres", t), ("gb2", 3)], w=[("xres", t)])
                row = (tb * 4 + t) * 128
                A("sp", lambda e, t=t, row=row: e.dma_start(out=out[row:row + 128, :], in_=xres[t]), r=[("xres", t)], w=[("out", tb * 4 + t)], dma=True)
        final_keys = [("out", i) for i in range(16)] if S.enabled else list(dbg_keys)
        S.enabled = True
        S.emit(nc, st, final_keys=final_keys)
    return nc


_ROPE_C = None


def _consts():
    p = np.arange(128)
    inv = (10000.0 ** (-(p % 32).astype(np.float64) * (2.0 / 64))) / (2.0 * math.pi)
    sgn = np.where((p % 64) < 32, -1.0, 1.0)
    rc = np.stack([inv, -2.0 * math.pi * sgn], axis=1).astype(np.float32)
    return rc, np.eye(128, dtype=np.float32)


def _make_in_maps(x, positions, ln_in_g, ln_in_b, w_in, g_cq, w_uq, g_ckv, w_uk, w_uv, conv_w, conv_b,
           g_conv_ln, b_conv_ln, w_out, g_ln1, b_ln1, w_ff1, w_ff2, g_ln2, b_ln2):
    x = np.asarray(x)
    positions = np.asarray(positions)
    rc, ident = _consts()
    f = lambda a: np.ascontiguousarray(np.asarray(a, dtype=np.float32))
    common = {
        "ident": ident, "ropec": rc,
        "ln_in_g": f(ln_in_g).reshape(1, -1), "ln_in_b": f(ln_in_b).reshape(1, -1),
        "w_in": f(w_in[0]), "g_cq": f(g_cq[0]).reshape(1, -1), "w_uq": f(w_uq[0]),
        "g_ckv": f(g_ckv[0]).reshape(1, -1), "w_uk": f(w_uk[0]), "w_uv": f(w_uv[0]),
        "conv_w": f(conv_w[0]), "conv_b": f(conv_b[0]).reshape(1, -1),
        "g_conv_ln": f(g_conv_ln[0]).reshape(1, -1), "b_conv_ln": f(b_conv_ln[0]).reshape(1, -1),
        "w_out": f(w_out[0]), "g_ln1": f(g_ln1[0]).reshape(1, -1), "b_ln1": f(b_ln1[0]).reshape(1, -1),
        "w_ff1": f(w_ff1[0]), "w_ff2": f(w_ff2[0]),
        "g_ln2": f(g_ln2[0]).reshape(1, -1), "b_ln2": f(b_ln2[0]).reshape(1, -1),
    }
    in_maps = []
    for c in range(NCORES):
        b, r = c // 4, c % 4
        t0 = r * NTOK
        m = dict(common)
        m["x"] = np.ascontiguousarray(np.roll(x[b], -t0, axis=0), dtype=np.float32)
        m["pos"] = np.ascontiguousarray(np.roll(positions[b], -t0).reshape(1, -1).astype(np.int32))
        hm = np.zeros((128, 2), np.float32)
        hm[:, 0] = 1.0 if r > 0 else 0.0
        hm[:, 1] = 1.0 if r < 3 else 0.0
        m["hmask"] = hm
        in_maps.append(m)
    return in_maps


def kernel(**inputs):
    in_maps = _make_in_maps(**inputs)
    nc = build_program()
    res = run_bass_kernel_spmd(nc, in_maps, core_ids=list(range(NCORES)))
    outp = np.empty((2, SEQ, D), np.float32)
    for c in range(NCORES):
        b, r = c // 4, c % 4
        outp[b, r * NTOK:(r + 1) * NTOK] = res.results[c]["out"]
    return outp
```
